# Optimizing a Trainium2 kernel written in Bass

```python
import math
import jax, jax.numpy as jnp
from jax import lax
import numpy as np

D_MODEL = 2048
BATCH = 4
SEQ = 4096
DEPTH = 1

CHUNK = 64
MIX_WIDTH = D_MODEL
ATTN_WIDTH = MIX_WIDTH // 2
GMLP_WIDTH = MIX_WIDTH - ATTN_WIDTH
DIFF_HEAD_DIM = 64
DIFF_HEADS = ATTN_WIDTH // (2 * DIFF_HEAD_DIM)
Q_BLOCK = 128
GMLP_BLOCK = 128
GMLP_GROUPS = 8
GMLP_GROUP_DIM = GMLP_WIDTH // GMLP_GROUPS
IN_WIDTH = 3 * ATTN_WIDTH + 2 * GMLP_WIDTH
D_FF = 4 * D_MODEL
N_MOD = 6
DEEPNORM_ALPHA = (2.0 * DEPTH) ** 0.25
DEEPNORM_BETA = (8.0 * DEPTH) ** -0.25
LN_EPS = 1e-5

kernel_name = "hybrid_diffattn_gmlp_deepnorm_adaln"


def _ln(x):
    xf = x.astype(jnp.float32)
    mu = jnp.mean(xf, axis=-1, keepdims=True)
    var = jnp.mean(jnp.square(xf - mu), axis=-1, keepdims=True)
    return (xf - mu) * lax.rsqrt(var + LN_EPS)


def _rms(x):
    xf = x.astype(jnp.float32)
    return xf * lax.rsqrt(jnp.mean(jnp.square(xf), axis=-1, keepdims=True) + LN_EPS)


def diff_attention(q, k, v, lam, subln_g, lambda_init):
    b, s, h, _, dh = q.shape
    nblk = s // Q_BLOCK
    scale = dh ** -0.5
    kf = k.astype(jnp.float32)
    vf = v.astype(jnp.float32)
    k_chunk = jnp.arange(s) // CHUNK
    qb = q.astype(jnp.float32).reshape(b, nblk, Q_BLOCK, h, 2, dh).transpose(1, 0, 2, 3, 4, 5)
    neg = jnp.finfo(jnp.float32).min

    def one_block(args):
        q_blk, i = args
        q_chunk = (i * Q_BLOCK + jnp.arange(Q_BLOCK)) // CHUNK
        mask = k_chunk[None, :] <= q_chunk[:, None]
        sc = jnp.einsum('bqhcd,bkhcd->bhcqk', q_blk, kf) * scale
        p = jax.nn.softmax(jnp.where(mask, sc, neg), axis=-1)
        w = p[:, :, 0] - lam * p[:, :, 1]
        return jnp.einsum('bhqk,bkhe->bqhe', w, vf)

    out = lax.map(one_block, (qb, jnp.arange(nblk)))
    out = out.transpose(1, 0, 2, 3, 4).reshape(b, s, h, 2 * dh)
    out = _rms(out) * subln_g * (1.0 - lambda_init)
    return out.reshape(b, s, h * 2 * dh)


def gmlp_spatial_gate(u, vg, ln_g, ln_b, ws, bs):
    b, s, g, dg = u.shape
    nb = s // GMLP_BLOCK
    vn = _ln(vg) * ln_g + ln_b
    vb = vn.reshape(b, nb, GMLP_BLOCK, g, dg)
    pos = jnp.arange(GMLP_BLOCK)
    mask = (pos[:, None] // CHUNK) >= (pos[None, :] // CHUNK)
    wm = jnp.where(mask[None], ws.astype(jnp.float32), 0.0)
    mixed = jnp.einsum('gts,bnsgd->bntgd', wm, vb) + bs.T.astype(jnp.float32)[None, None, :, :, None]
    return (u.astype(jnp.float32) * mixed.reshape(b, s, g, dg)).reshape(b, s, g * dg)


def setup_inputs(seed: int = 0) -> dict:
    key = jax.random.key(seed)
    ks = jax.random.split(key, 24)
    f32 = jnp.float32
    n = lambda k, shape, s: jax.random.normal(k, shape, f32) * s
    L = DEPTH
    return {
        "x": jax.random.normal(ks[0], (BATCH, SEQ, D_MODEL), f32),
        "c": jax.random.normal(ks[1], (BATCH, D_MODEL), f32),
        "w_ada": n(ks[2], (L, D_MODEL, N_MOD * D_MODEL), 0.1 * D_MODEL ** -0.5),
        "b_ada": n(ks[3], (L, N_MOD * D_MODEL), 0.01),
        "w_in": n(ks[4], (L, D_MODEL, IN_WIDTH), D_MODEL ** -0.5),
        "lambda_q1": n(ks[5], (L, DIFF_HEAD_DIM), 0.1),
        "lambda_k1": n(ks[6], (L, DIFF_HEAD_DIM), 0.1),
        "lambda_q2": n(ks[7], (L, DIFF_HEAD_DIM), 0.1),
        "lambda_k2": n(ks[8], (L, DIFF_HEAD_DIM), 0.1),
        "subln_g": 1.0 + n(ks[9], (L, 2 * DIFF_HEAD_DIM), 0.02),
        "gmlp_ln_g": 1.0 + n(ks[10], (L, GMLP_GROUPS, GMLP_GROUP_DIM), 0.02),
        "gmlp_ln_b": n(ks[11], (L, GMLP_GROUPS, GMLP_GROUP_DIM), 0.02),
        "gmlp_ws": n(ks[12], (L, GMLP_GROUPS, GMLP_BLOCK, GMLP_BLOCK), GMLP_BLOCK ** -0.5),
        "gmlp_bs": 1.0 + n(ks[13], (L, GMLP_GROUPS, GMLP_BLOCK), 0.02),
        "w_out": n(ks[14], (L, MIX_WIDTH, D_MODEL), DEEPNORM_BETA * MIX_WIDTH ** -0.5),
        "ln1_g": 1.0 + n(ks[15], (L, D_MODEL), 0.02),
        "ln1_b": n(ks[16], (L, D_MODEL), 0.02),
        "w_ff1": n(ks[17], (L, D_MODEL, D_FF), D_MODEL ** -0.5),
        "w_ff2": n(ks[18], (L, D_FF, D_MODEL), DEEPNORM_BETA * D_FF ** -0.5),
        "ln2_g": 1.0 + n(ks[19], (L, D_MODEL), 0.02),
        "ln2_b": n(ks[20], (L, D_MODEL), 0.02),
    }


def reference(x, c, w_ada, b_ada, w_in, lambda_q1, lambda_k1, lambda_q2, lambda_k2, subln_g,
              gmlp_ln_g, gmlp_ln_b, gmlp_ws, gmlp_bs, w_out, ln1_g, ln1_b, w_ff1, w_ff2, ln2_g, ln2_b):
    b, s, d = x.shape
    out_dtype = x.dtype
    h_stream = x.astype(jnp.float32)
    c_act = jax.nn.silu(c.astype(jnp.float32))
    for l in range(DEPTH):
        mod = (c_act @ w_ada[l] + b_ada[l]).reshape(b, N_MOD, d)
        sh1, sc1, g1, sh2, sc2, g2 = [mod[:, i][:, None, :] for i in range(N_MOD)]

        a_in = _ln(h_stream) * (1.0 + sc1) + sh1
        proj = a_in @ w_in[l]
        q, k, v, u, vg = jnp.split(proj, [ATTN_WIDTH, 2 * ATTN_WIDTH, 3 * ATTN_WIDTH,
                                          3 * ATTN_WIDTH + GMLP_WIDTH], axis=-1)
        q = q.reshape(b, s, DIFF_HEADS, 2, DIFF_HEAD_DIM)
        k = k.reshape(b, s, DIFF_HEADS, 2, DIFF_HEAD_DIM)
        v = v.reshape(b, s, DIFF_HEADS, 2 * DIFF_HEAD_DIM)
        lambda_init = 0.8 - 0.6 * math.exp(-0.3 * l)
        lam = (jnp.exp(jnp.sum(lambda_q1[l] * lambda_k1[l]))
               - jnp.exp(jnp.sum(lambda_q2[l] * lambda_k2[l])) + lambda_init)
        attn_out = diff_attention(q, k, v, lam, subln_g[l], lambda_init)

        u = jax.nn.gelu(u).reshape(b, s, GMLP_GROUPS, GMLP_GROUP_DIM)
        vg = jax.nn.gelu(vg).reshape(b, s, GMLP_GROUPS, GMLP_GROUP_DIM)
        gmlp_out = gmlp_spatial_gate(u, vg, gmlp_ln_g[l], gmlp_ln_b[l], gmlp_ws[l], gmlp_bs[l])

        mix = jnp.concatenate([attn_out, gmlp_out], axis=-1) @ w_out[l]
        h_stream = _ln(DEEPNORM_ALPHA * h_stream + (1.0 + g1) * mix) * ln1_g[l] + ln1_b[l]

        m_in = _ln(h_stream) * (1.0 + sc2) + sh2
        ff = jnp.square(jax.nn.relu(m_in @ w_ff1[l])) @ w_ff2[l]
        h_stream = _ln(DEEPNORM_ALPHA * h_stream + (1.0 + g2) * ff) * ln2_g[l] + ln2_b[l]
    return h_stream.astype(out_dtype)
```

```python
import numpy as np
from contextlib import ExitStack
import concourse.bass as bass
import concourse.mybir as mybir
from concourse.bass_utils import run_bass_kernel_spmd

F32 = mybir.dt.float32
BF16 = mybir.dt.bfloat16
AF = mybir.ActivationFunctionType
ALU = mybir.AluOpType
AX = mybir.AxisListType

ENGS = ("pe", "act", "dve", "pool", "sp")
ALPHA = float((2.0 * 1) ** 0.25)
LN_EPS = 1e-5
LAMBDA_INIT = 0.2


class Res:
    __slots__ = ("name", "w", "r")

    def __init__(self, name=""):
        self.name = name
        self.w = None
        self.r = []


def RL(n):
    return [Res() for _ in range(n)]


class Op:
    __slots__ = ("eng", "fn", "deps", "needs", "token", "dma", "slot")


class Sched:
    def __init__(self, nc, es, n_sp=24, n_pool=10):
        self.nc = nc
        self.ops = {e: [] for e in ENGS}
        self.esem = {e: es.enter_context(nc.semaphore("s_" + e)) for e in ENGS}
        self.dsem = {
            "sp": [es.enter_context(nc.semaphore(f"d_sp{i}")) for i in range(n_sp)],
            "pool": [es.enter_context(nc.semaphore(f"d_pl{i}")) for i in range(n_pool)],
        }
        self.dcnt = {k: 0 for k in self.dsem}
        self.dlast = {k: [None] * len(v) for k, v in self.dsem.items()}
        self.final_waits = []

    def add(self, eng, fn, reads=(), writes=(), dma=False, extra=()):
        op = Op()
        op.eng = eng
        op.fn = fn
        op.needs = False
        op.dma = dma
        op.token = None
        op.slot = None
        deps = []
        for r in reads:
            if r.w is not None:
                deps.append((r.w, 0))
        for w in writes:
            if w.w is not None:
                deps.append((w.w, 1))
            for rr in w.r:
                deps.append((rr, 2))
        for d in extra:
            deps.append((d, 0))
        if dma:
            n = self.dcnt[eng]
            ns = len(self.dsem[eng])
            slot = n % ns
            prev = self.dlast[eng][slot]
            if prev is not None:
                deps.append((prev, 3))
            op.slot = (slot, 16 * (n // ns + 1))
            self.dlast[eng][slot] = op
            self.dcnt[eng] = n + 1
        final = []
        seen = set()
        for d, kind in deps:
            if d is op or id(d) in seen:
                continue
            if d.eng == eng and not d.dma and not dma:
                if eng == "pe":
                    continue
            seen.add(id(d))
            final.append(d)
            d.needs = True
        op.deps = final
        for r in reads:
            r.r.append(op)
        for w in writes:
            w.w = op
            w.r = []
        self.ops[eng].append(op)
        return op

    def barrier(self):
        lasts = [self.ops[e][-1] for e in ENGS
                 if self.ops[e] and not self.ops[e][-1].dma and self.ops[e][-1].fn is not None]
        dmas = [op for k in self.dlast for op in self.dlast[k] if op is not None]
        for e in ("pe", "act", "dve", "pool", "sp"):
            self.add(e, None, extra=lasts + dmas)

    def wait_at_end(self, op):
        op.needs = True
        self.final_waits.append(op)

    def emit(self):
        nc = self.nc
        for e in ENGS:
            c = 0
            for op in self.ops[e]:
                if op.dma:
                    op.token = (self.dsem[e][op.slot[0]], op.slot[1])
                elif op.needs:
                    c += 1
                    op.token = (self.esem[e], c)

        def run(e, eng, extra=()):
            waited = {}
            for op in self.ops[e]:
                w = {}
                for d in op.deps:
                    s, v = d.token
                    k = id(s)
                    if waited.get(k, 0) >= v:
                        continue
                    if k not in w or w[k][1] < v:
                        w[k] = (s, v)
                for k, (s, v) in w.items():
                    eng.wait_ge(s, v)
                    waited[k] = v
                if op.fn is None:
                    assert not op.needs
                    continue
                ins = op.fn(eng)
                if op.dma:
                    ins.then_inc(op.token[0], 16)
                elif op.needs:
                    ins.then_inc(op.token[0], 1)
            for op in extra:
                s, v = op.token
                if waited.get(id(s), 0) < v:
                    eng.wait_ge(s, v)
                    waited[id(s)] = v

        with nc.Block() as block:
            @block.tensor
            def _(eng):
                run("pe", eng)

            @block.scalar
            def _(eng):
                run("act", eng)

            @block.vector
            def _(eng):
                run("dve", eng)

            @block.gpsimd
            def _(eng):
                run("pool", eng)

            @block.sync
            def _(eng):
                run("sp", eng, self.final_waits)


def build_program(debug=False, stop_after=99):
    import os
    LITE = os.environ.get("DEV_LITE", "") == "1"
    nc = bass.Bass("TRN2", target_bir_lowering=False)

    def din(name, shape, dt=F32):
        if LITE and name in ("w_ada",):
            shape = [128, 128]
        return nc.dram_tensor(name, shape, dt, kind="ExternalInput").ap()

    def dscr(name, shape, dt=BF16):
        kind = "ExternalOutput" if debug else "Internal"
        return nc.dram_tensor(name, shape, dt, kind=kind).ap()

    xs = din("xs", [4096, 2048])
    valid = din("valid", [128, 32])
    ccol = din("ccol", [128, 16])
    w_ada = din("w_ada", [2048, 12288])
    b_ada = din("b_ada", [1, 12288])
    w_in = din("w_in", [2048, 5120])
    lq1 = din("lambda_q1", [1, 64])
    lk1 = din("lambda_k1", [1, 64])
    lq2 = din("lambda_q2", [1, 64])
    lk2 = din("lambda_k2", [1, 64])
    subln = din("subln_g", [1, 128])
    glg = din("gmlp_ln_g", [1, 1024])
    glb = din("gmlp_ln_b", [1, 1024])
    gws = din("gmlp_ws", [8, 128, 128])
    gbs = din("gmlp_bs", [1, 1024])
    w_out = din("w_out", [2048, 2048])
    ln1g = din("ln1_g", [1, 2048])
    ln1b = din("ln1_b", [1, 2048])
    w_ff1 = din("w_ff1", [2048, 8192])
    w_ff2 = din("w_ff2", [8192, 2048])
    ln2g = din("ln2_g", [1, 2048])
    ln2b = din("ln2_b", [1, 2048])
    out = nc.dram_tensor("out", [2048, 2048], F32, kind="ExternalOutput").ap()

    kscr = dscr("kscr", [8, 128, 4096])
    vscr = dscr("vscr", [8, 128, 32 * 132])
    ascr = dscr("ascr", [4, 128, 16 * 512])
    qscr = dscr("qscr", [8, 128, 2048])
    catscr = dscr("catscr", [16, 128, 2048])
    mscr = dscr("mscr", [4, 128, 16 * 512])
    h1scr = dscr("h1scr", [2048, 2048], F32)
    modscr = dscr("modscr", [2, 2048], F32)
    w1s = nc.dram_tensor("w1s", [32, 128, 16 * 256], BF16, kind="Internal").ap()
    wis = nc.dram_tensor("wis", [6, 128, 16 * 512], BF16, kind="Internal").ap()
    w2s = nc.dram_tensor("w2s", [32, 128, 8 * 512], BF16, kind="Internal").ap()

    with ExitStack() as es:
        S = Sched(nc, es)

        def sbuf(stack, name, shape, dt):
            return stack.enter_context(nc.sbuf_tensor(name, shape, dt))

        identf = sbuf(es, "identf", [128, 128], F32)
        ident = sbuf(es, "ident", [128, 128], BF16)
        modc = sbuf(es, "modc", [128, 64], F32)
        validt = sbuf(es, "validt", [128, 32], F32)
        neglam = sbuf(es, "neglam", [128, 1], F32)
        sg08 = sbuf(es, "sg08", [128, 128], F32)
        wmT = sbuf(es, "wmT", [128, 8, 128], BF16)
        bs_hi = sbuf(es, "bs_hi", [1, 1024], BF16)
        bs_lo = sbuf(es, "bs_lo", [1, 1024], BF16)
        ones_row = sbuf(es, "ones_row", [1, 128], BF16)
        arena = [sbuf(es, f"arena{i}", [128, 2048], F32) for i in range(3)]
        r_arena = RL(3)
        r_identf, r_ident, r_valid, r_neglam, r_sg08, r_wmT, r_bs, r_ones = RL(8)
        r_modc = RL(4)
        banks = [es.enter_context(nc.psum_tensor(f"bank{i}", [128, 512], F32)) for i in range(8)]
        r_bank = RL(8)
        A = [0, 1, 2, 3]
        B0, B1, T0, T1 = 4, 5, 6, 7

        def bview(b):
            return banks[b][:].bitcast(BF16)

        NST = 4
        st_t = [sbuf(es, f"st{i}", [128, 4, 6], F32) for i in range(NST)]
        mv_t = [sbuf(es, f"mv{i}", [128, 4, 2], F32) for i in range(NST)]
        sc_t = [sbuf(es, f"sc{i}", [128, 12], F32) for i in range(NST)]
        r_st, r_mv, r_sca, r_scb, r_scc = RL(NST), RL(NST), RL(NST), RL(NST), RL(NST)
        r_st4 = [RL(4) for _ in range(NST)]
        r_mv4 = [RL(4) for _ in range(NST)]
        stc = [0]

        def ln_stats(src_ap_fn, r_src, defer=False):
            k = stc[0] % NST
            stc[0] += 1
            st, mv, sc = st_t[k], mv_t[k], sc_t[k]

            for c in range(4):
                S.add("dve", lambda e, c=c: e.bn_stats(out=st[:, c, :], in_=src_ap_fn(c * 512, (c + 1) * 512)), reads=r_src, writes=[r_st4[k][c]])
            S.add("dve", lambda e: e.bn_aggr(out=mv[:, 0, :], in_=st[:]), reads=r_st4[k], writes=[r_mv[k]])
            S.add("act", lambda e: e.activation(out=sc[:, 0:1], in_=mv[:, 0, 1:2], func=AF.Ln, bias=LN_EPS, scale=1.0),
                  reads=[r_mv[k]], writes=[r_sca[k]])
            S.add("act", lambda e: e.activation(out=sc[:, 1:2], in_=sc[:, 0:1], func=AF.Exp, scale=-0.5),
                  reads=[r_sca[k]], writes=[r_scb[k]])
            def f_nb():
                S.add("dve", lambda e: e.tensor_scalar(out=sc[:, 2:3], in0=mv[:, 0, 0:1], scalar1=-1.0, scalar2=sc[:, 1:2],
                                                       op0=ALU.mult, op1=ALU.mult),
                      reads=[r_mv[k], r_scb[k]], writes=[r_scc[k]])
            if defer:
                return sc[:, 1:2], sc[:, 2:3], [r_scb[k], r_scc[k]], f_nb
            f_nb()
            return sc[:, 1:2], sc[:, 2:3], [r_scb[k], r_scc[k]]

        S.add("pool", lambda e: e.memset(identf[:], 0.0), writes=[r_identf])
        S.add("pool", lambda e: e.affine_select(out=identf[:], in_=identf[:], pattern=[[-1, 128]],
                                                compare_op=ALU.not_equal, fill=1.0, base=0, channel_multiplier=1),
              reads=[r_identf], writes=[r_identf])
        S.add("dve", lambda e: e.tensor_copy(out=ident[:], in_=identf[:]), reads=[r_identf], writes=[r_ident])
        S.add("pool", lambda e: e.memset(ones_row[:], 1.0), writes=[r_ones])
        S.add("sp", lambda e: e.dma_start(out=validt[:], in_=valid[:, :]), writes=[r_valid], dma=True)

        cbc = sbuf(es, "cbc", [128, 16, 128], BF16)
        p0 = es.enter_context(ExitStack())
        c_sb = sbuf(p0, "c_sb", [128, 16], F32)
        c_act = sbuf(p0, "c_act", [128, 16], F32)
        lam_t = sbuf(p0, "lam_t", [128, 4, 64], F32)
        lam_s = sbuf(p0, "lam_s", [128, 8], F32)
        wst = [sbuf(p0, f"wst{i}", [128, 128], BF16) for i in range(2)]
        bs_f = sbuf(p0, "bs_f", [1, 1024], F32)
        bs_f2 = sbuf(p0, "bs_f2", [1, 1024], F32)
        r_c, r_cact, r_cbc, r_junk, r_lamt, r_lams, r_bsf, r_bsf2 = RL(8)
        r_wt, r_bb, r_mtmp, r_wst = RL(2), RL(2), RL(2), RL(2)

        S.add("sp", lambda e: e.dma_start(out=c_sb[:], in_=ccol[:, :]), writes=[r_c], dma=True)
        S.add("act", lambda e: e.activation(out=c_act[:], in_=c_sb[:], func=AF.Silu), reads=[r_c], writes=[r_cact])
        S.add("dve", lambda e: e.tensor_copy(out=cbc[:], in_=c_act[:].unsqueeze(2).to_broadcast([128, 16, 128])),
              reads=[r_cact], writes=[r_cbc])

        def make_modbufs(stack, tag, bank):
            d = {}
            d["wt"] = [sbuf(stack, f"wt_ada{tag}{i}", [128, 16, 512], BF16) for i in range(2)]
            d["bb"] = [sbuf(stack, f"bb{tag}{i}", [128, 512], F32) for i in range(2)]
            d["mtmp"] = [sbuf(stack, f"mtmp{tag}{i}", [128, 512], F32) for i in range(2)]
            d["junk"] = sbuf(stack, f"junk{tag}", [128, 128], F32)
            d["r_wt"], d["r_bb"], d["r_mtmp"] = RL(2), RL(2), RL(2)
            d["r_junk"] = Res()
            d["bank"] = bank
            return d

        def mod_load(cb, MB):
            k = cb % 2
            wt_, bb_ = MB["wt"], MB["bb"]
            S.add("pool", lambda e: e.dma_start(out=wt_[k][:], in_=w_ada[:, cb * 512:(cb + 1) * 512].rearrange("(kc p) n -> p kc n", p=128)),
                  writes=[MB["r_wt"][k]], dma=True)
            S.add("sp", lambda e: e.dma_start(out=bb_[k][:], in_=b_ada[0:1, cb * 512:(cb + 1) * 512].partition_broadcast(128)),
                  writes=[MB["r_bb"][k]], dma=True)

        def mod_compute(cb, MB):
            k = cb % 2
            wt_, bb_, mtmp_, junk_ = MB["wt"], MB["bb"], MB["mtmp"], MB["junk"]
            r_wt_, r_bb_, r_mtmp_, r_junk_ = MB["r_wt"], MB["r_bb"], MB["r_mtmp"], MB["r_junk"]
            bk = MB["bank"] if MB["bank"] is not None else A[cb % 4]

            def f_mm(e):
                ins = None
                for kc in range(16):
                    ins = e.matmul(banks[bk][:], lhsT=cbc[:, kc, :], rhs=wt_[k][:, kc, :], start=(kc == 0), stop=(kc == 15))
                return ins
            S.add("pe", f_mm, reads=[r_cbc, r_wt_[k]], writes=[r_bank[bk]])
            kind = cb // 4
            plus1 = 1.0 if kind in (1, 2, 4, 5) else 0.0
            S.add("dve", lambda e: e.scalar_tensor_tensor(out=mtmp_[k][:], in0=banks[bk][:], scalar=plus1, in1=bb_[k][:],
                                                          op0=ALU.add, op1=ALU.add),
                  reads=[r_bank[bk], r_bb_[k]], writes=[r_mtmp_[k]])
            if kind in (2, 5):
                gi = 0 if kind == 2 else 1
                c0 = (cb % 4) * 512
                S.add("sp", lambda e: e.dma_start(out=modscr[gi:gi + 1, c0:c0 + 512], in_=mtmp_[k][0:1, :]),
                      reads=[r_mtmp_[k]], dma=True)
            else:
                mi = {0: 0, 1: 1, 3: 2, 4: 3}[kind]
                for j in range(4):
                    col = mi * 16 + (cb % 4) * 4 + j
                    S.add("dve", lambda e, j=j: e.tensor_tensor(out=junk_[:], in0=mtmp_[k][:, j * 128:(j + 1) * 128], in1=identf[:], op=ALU.mult),
                          reads=[r_mtmp_[k], r_identf], writes=[r_junk_])
                    S.add("dve", lambda e, col=col: e.reduce_sum(out=modc[:, col:col + 1], in_=junk_[:], axis=AX.X),
                          reads=[r_junk_], writes=[r_modc[mi]])

        MB0 = make_modbufs(p0, "a", None)
        if LITE:
            S.add("pool", lambda e: e.memset(modc[:], 1.0), writes=r_modc)
        else:
            for cb in range(8):
                mod_load(cb, MB0)
                mod_compute(cb, MB0)

        for i, t in enumerate((lq1, lk1, lq2, lk2)):
            S.add("sp", lambda e, i=i, t=t: e.dma_start(out=lam_t[:, i, :], in_=t[0:1, :].partition_broadcast(128)),
                  writes=[r_lamt], dma=True)
        S.barrier()
        S.add("dve", lambda e: e.tensor_tensor(out=lam_t[:, 0, :], in0=lam_t[:, 0, :], in1=lam_t[:, 1, :], op=ALU.mult),
              reads=[r_lamt], writes=[r_lamt])
        S.add("dve", lambda e: e.tensor_tensor(out=lam_t[:, 2, :], in0=lam_t[:, 2, :], in1=lam_t[:, 3, :], op=ALU.mult),
              reads=[r_lamt], writes=[r_lamt])
        S.add("dve", lambda e: e.reduce_sum(out=lam_s[:, 0:1], in_=lam_t[:, 0, :], axis=AX.X), reads=[r_lamt], writes=[r_lams])
        S.add("dve", lambda e: e.reduce_sum(out=lam_s[:, 1:2], in_=lam_t[:, 2, :], axis=AX.X), reads=[r_lamt], writes=[r_lams])
        S.add("act", lambda e: e.activation(out=lam_s[:, 2:4], in_=lam_s[:, 0:2], func=AF.Exp), reads=[r_lams], writes=[r_lams])
        S.add("dve", lambda e: e.scalar_tensor_tensor(out=neglam[:], in0=lam_s[:, 3:4], scalar=-LAMBDA_INIT, in1=lam_s[:, 2:3],
                                                      op0=ALU.add, op1=ALU.subtract),
              reads=[r_lams], writes=[r_neglam])
        S.add("sp", lambda e: e.dma_start(out=sg08[:], in_=subln[0:1, :].partition_broadcast(128)), writes=[r_sg08], dma=True)
        S.barrier()
        S.add("dve", lambda e: e.tensor_scalar(out=sg08[:], in0=sg08[:], scalar1=1.0 - LAMBDA_INIT, scalar2=None, op0=ALU.mult),
              reads=[r_sg08], writes=[r_sg08])
        for g in range(8):
            k = g % 2
            S.add("pool", lambda e, g=g, k=k: e.dma_start(out=wst[k][:], in_=gws[g, :, :]), writes=[r_wst[k]], dma=True)
            S.add("pe", lambda e, k=k: e.transpose(out=bview(T0)[:, k * 128:(k + 1) * 128], in_=wst[k][:], identity=ident[:]),
                  reads=[r_wst[k], r_ident], writes=[r_bank[T0]])
            S.add("dve", lambda e, g=g, k=k: e.tensor_copy(out=wmT[:, g, :], in_=bview(T0)[:, k * 128:(k + 1) * 128]),
                  reads=[r_bank[T0]], writes=[r_wmT])
        S.add("pool", lambda e: e.memset(wmT[64:128, :, 0:64], 0.0), reads=[r_wmT], writes=[r_wmT])
        S.add("sp", lambda e: e.dma_start(out=bs_f[:], in_=gbs[0:1, :]), writes=[r_bsf], dma=True)
        S.add("dve", lambda e: e.tensor_copy(out=bs_hi[:], in_=bs_f[:]), reads=[r_bsf], writes=[r_bs])
        S.add("dve", lambda e: e.tensor_copy(out=bs_f2[:], in_=bs_hi[:]), reads=[r_bs], writes=[r_bsf2])
        S.add("dve", lambda e: e.tensor_tensor(out=bs_f2[:], in0=bs_f[:], in1=bs_f2[:], op=ALU.subtract), reads=[r_bsf, r_bsf2], writes=[r_bsf2])
        S.add("dve", lambda e: e.tensor_copy(out=bs_lo[:], in_=bs_f2[:]), reads=[r_bsf2], writes=[r_bs])
        if debug:
            dbg_modc = nc.dram_tensor("dbg_modc", [128, 64], F32, kind="ExternalOutput").ap()
            S.add("sp", lambda e: e.dma_start(out=dbg_modc[:, :], in_=modc[:]), reads=r_modc, dma=True)
            dbg_id = nc.dram_tensor("dbg_id", [128, 128], F32, kind="ExternalOutput").ap()
            S.add("sp", lambda e: e.dma_start(out=dbg_id[:, :], in_=identf[:]), reads=[r_identf], dma=True)
        S.barrier()
        p0.close()

        def transpose_group(xn, r_xn, dst, r_dst, sc_col0, bi_col0):
            for kcp in range(8):
                tb = T0 if kcp % 2 == 0 else T1

                def f_tr(e, kcp=kcp, tb=tb):
                    ins = None
                    for k2 in range(2):
                        kc = kcp * 2 + k2
                        for i in range(4):
                            ins = e.transpose(out=bview(tb)[:, k2 * 512 + i * 128: k2 * 512 + (i + 1) * 128],
                                              in_=xn[:, i, kc * 128:(kc + 1) * 128], identity=ident[:])
                    return ins
                S.add("pe", f_tr, reads=list(r_xn) + [r_ident], writes=[r_bank[tb]])
                for k2 in range(2):
                    kc = kcp * 2 + k2
                    src = bview(tb)[:, k2 * 512:(k2 + 1) * 512]
                    scl = modc[:, sc_col0 + kc: sc_col0 + kc + 1]
                    bia = modc[:, bi_col0 + kc: bi_col0 + kc + 1]
                    import os
                    TRM = os.environ.get("DEV_TR", "")
                    if TRM == "noevac":
                        continue
                    if TRM == "act":
                        S.add("act", lambda e, kc=kc, src=src, scl=scl, bia=bia: e.activation(out=dst[:, kc, :], in_=src, func=AF.Identity, bias=bia, scale=scl),
                              reads=[r_bank[tb]] + r_modc, writes=[r_dst[kc]])
                    else:
                        S.add("dve", lambda e, kc=kc, src=src, scl=scl, bia=bia: e.tensor_scalar(out=dst[:, kc, :], in0=src, scalar1=scl, scalar2=bia, op0=ALU.mult, op1=ALU.add),
                              reads=[r_bank[tb]] + r_modc, writes=[r_dst[kc]])

        arot = [0]

        def next_A():
            b = A[arot[0] % 4]
            arot[0] += 1
            return b

        if stop_after >= 1:
            p1 = es.enter_context(ExitStack())
            wk = sbuf(p1, "wk", [128, 16, 1024], BF16)
            wv = sbuf(p1, "wv", [128, 16, 1024], BF16)
            xt = [sbuf(p1, f"xt{i}", [128, 2048], F32) for i in range(2)]
            xn = [sbuf(p1, f"xn{i}", [128, 4, 2048], BF16) for i in range(2)]
            ain = [sbuf(p1, f"ain{i}", [128, 16, 512], BF16) for i in range(2)]
            kst = [sbuf(p1, f"kst{i}", [128, 8, 512], BF16) for i in range(2)]
            vst = [sbuf(p1, f"vst{i}", [128, 8, 132], BF16) for i in range(2)]
            r_wk2, r_wv2 = RL(2), RL(2)
            r_xt = RL(2)
            r_xn = [RL(4), RL(4)]
            r_ain = [RL(16), RL(16)]
            r_kst = [RL(8), RL(8)]
            r_vst = [RL(3), RL(3)]
            for hb in range(2):
                S.add("pool", lambda e, hb=hb: e.dma_start(out=wk[:, :, hb * 512:(hb + 1) * 512],
                                                           in_=w_in[:, 1024 + hb * 512:1024 + (hb + 1) * 512].rearrange("(kc p) n -> p kc n", p=128)),
                      writes=[r_wk2[hb]], dma=True)
            for hb in range(2):
                S.add("pool", lambda e, hb=hb: e.dma_start(out=wv[:, :, hb * 512:(hb + 1) * 512],
                                                           in_=w_in[:, 2048 + hb * 512:2048 + (hb + 1) * 512].rearrange("(kc p) n -> p kc n", p=128)),
                      writes=[r_wv2[hb]], dma=True)
            for pi_, col0_ in enumerate((0, 512, 4096, 4608, 3072, 3584)):
                S.add("pool", lambda e, pi_=pi_, col0_=col0_: e.dma_start(out=wis[pi_, :, :].rearrange("p (k n) -> p k n", k=16),
                                                                           in_=w_in[:, col0_:col0_ + 512].rearrange("(kc p) n -> p kc n", p=128)), dma=True)
            for k in range(2):
                S.add("pool", lambda e, k=k: e.memset(vst[k][:], 0.0), writes=r_vst[k])
            xcnt = [0]

            def ln_group(G):
                gb = G % 2
                for i in range(4):
                    b = xcnt[0] % 2
                    xcnt[0] += 1
                    row0 = (G * 4 + i) * 128
                    S.add("sp", lambda e, b=b, row0=row0: e.dma_start(out=xt[b][:], in_=xs[row0:row0 + 128, :]), writes=[r_xt[b]], dma=True)
                    rstd, nb, rr = ln_stats(lambda a, c, b=b: xt[b][:, a:c], [r_xt[b]])
                    S.add("act", lambda e, b=b, i=i, gb=gb, rstd=rstd, nb=nb: e.activation(out=xn[gb][:, i, :], in_=xt[b][:], func=AF.Identity, bias=nb, scale=rstd),
                          reads=[r_xt[b]] + rr, writes=[r_xn[gb][i]])

            def tr_group(G):
                gb = G % 2
                transpose_group(xn[gb], r_xn[gb], ain[gb], r_ain[gb], 16, 0)

            def k_group(G):
                gb = G % 2
                for h in range(8):
                    bk = next_A()

                    def f_mm(e, h=h, bk=bk):
                        ins = None
                        for kc in range(16):
                            ins = e.matmul(banks[bk][:], lhsT=wk[:, kc, h * 128:(h + 1) * 128], rhs=ain[gb][:, kc, :], start=(kc == 0), stop=(kc == 15))
                        return ins
                    S.add("pe", f_mm, reads=r_ain[gb] + r_wk2, writes=[r_bank[bk]])
                    if h % 2 == 0:
                        S.add("act", lambda e, h=h, bk=bk: e.copy(out=kst[gb][:, h, :], in_=banks[bk][:]), reads=[r_bank[bk]], writes=[r_kst[gb][h]])
                    else:
                        S.add("dve", lambda e, h=h, bk=bk: e.tensor_copy(out=kst[gb][:, h, :], in_=banks[bk][:]), reads=[r_bank[bk]], writes=[r_kst[gb][h]])
                S.add("pool", lambda e: e.dma_start(out=kscr.rearrange("h p t -> p h t")[:, :, G * 512:(G + 1) * 512], in_=kst[gb][:]),
                      reads=r_kst[gb], dma=True)

            def v_group(G):
                gb = G % 2
                for i in range(4):
                    pos = G * 4 + i
                    vb = pos % 2
                    for cbv in range(2):
                        bk = next_A()

                        def f_mm(e, i=i, cbv=cbv, bk=bk):
                            ins = None
                            for kc in range(16):
                                ins = e.matmul(banks[bk][:], lhsT=ain[gb][:, kc, i * 128:(i + 1) * 128], rhs=wv[:, kc, cbv * 512:(cbv + 1) * 512], start=(kc == 0), stop=(kc == 15))
                            return ins
                        S.add("pe", f_mm, reads=r_ain[gb] + r_wv2, writes=[r_bank[bk]])
                        src = banks[bk][:].rearrange("p (h c) -> p h c", h=4)
                        dstv = vst[vb][:, cbv * 4:(cbv + 1) * 4, 0:128]
                        vcol = validt[:, pos:pos + 1]
                        if False:
                            S.add("act", lambda e, src=src, dstv=dstv, vcol=vcol: e.activation(out=dstv, in_=src, func=AF.Copy, scale=vcol),
                                  reads=[r_bank[bk], r_valid], writes=[r_vst[vb][cbv]])
                        else:
                            S.add("dve", lambda e, src=src, dstv=dstv, vcol=vcol: e.tensor_scalar(out=dstv, in0=src, scalar1=vcol, scalar2=None, op0=ALU.mult),
                                  reads=[r_bank[bk], r_valid], writes=[r_vst[vb][cbv]])
                    S.add("pool", lambda e, vb=vb, pos=pos: e.tensor_copy(out=vst[vb][:, :, 128:129], in_=validt[:, pos:pos + 1].unsqueeze(2).to_broadcast([128, 8, 1])),
                          reads=[r_valid], writes=[r_vst[vb][2]])
                    S.add("pool", lambda e, vb=vb, pos=pos: e.dma_start(out=vscr.rearrange("h p (n c) -> p h n c", c=132)[:, :, pos, :], in_=vst[vb][:]),
                          reads=r_vst[vb], dma=True)

            def a_store(G):
                gb = G % 2
                og = G // 2
                S.add("pool", lambda e: e.dma_start(out=ascr[og, :, :], in_=ain[gb][:].rearrange("p k t -> p (k t)")), reads=r_ain[gb], dma=True)

            import os
            SK = os.environ.get("DEV_SKIP", "")
            NG = int(os.environ.get("DEV_NG", "8"))
            ln_group(0)
            if "t" not in SK:
                tr_group(0)
            for G in range(NG):
                if G + 1 < NG:
                    ln_group(G + 1)
                if "k" not in SK:
                    k_group(G)
                if G + 1 < NG and "t" not in SK:
                    tr_group(G + 1)
                if "v" not in SK:
                    v_group(G)
                if G % 2 == 1 and "a" not in SK:
                    a_store(G)
            S.barrier()
            p1.close()

        if stop_after >= 2:
            p2 = es.enter_context(ExitStack())
            wr = [sbuf(p2, f"wr{i}", [128, 16, 512], BF16) for i in range(3)]
            ag = [sbuf(p2, f"ag{i}", [128, 16, 512], BF16) for i in range(2)]
            vn = sbuf(p2, "vn", [128, 16, 1024], BF16)
            qst = [sbuf(p2, f"qst{i}", [128, 4, 512], BF16) for i in range(2)]
            ust = [sbuf(p2, f"ust{i}", [128, 4, 512], BF16) for i in range(2)]
            gst = [sbuf(p2, f"gst{i}", [128, 4, 512], BF16) for i in range(2)]
            gv = [sbuf(p2, f"gv{i}", [128, 512], F32) for i in range(2)]
            gz = [sbuf(p2, f"gz{i}", [128, 512], F32) for i in range(2)]
            r_wr, r_ag = RL(3), RL(2)
            r_vn = [RL(2) for _ in range(16)]
            r_qst, r_ust, r_gst = [RL(4), RL(4)], [RL(4), RL(4)], [RL(4), RL(4)]
            r_gv, r_gz = RL(2), RL(2)
            S.add("sp", lambda e: e.dma_start(out=arena[0][:, 0:1024], in_=glg[0:1, :].partition_broadcast(128)), writes=[r_arena[0]], dma=True)
            S.add("sp", lambda e: e.dma_start(out=arena[1][:, 0:1024], in_=glb[0:1, :].partition_broadcast(128)), writes=[r_arena[1]], dma=True)
            passes = [("q", 0, 0), ("q", 1, 512), ("vg", 0, 4096), ("vg", 1, 4608), ("u", 0, 3072), ("u", 1, 3584)]
            agc = [0]
            gvc = [0]
            stq = [0]
            for pi, (kind, hb, col0) in enumerate(passes):
                wi = pi % 3
                S.add("sp", lambda e, wi=wi, pi=pi: e.dma_start(out=wr[wi][:].rearrange("p k n -> p (k n)"), in_=wis[pi, :, :]),
                      writes=[r_wr[wi]], dma=True)
                for og in range(4):
                    ab = agc[0] % 2
                    agc[0] += 1
                    S.add("sp", lambda e, ab=ab, og=og: e.dma_start(out=ag[ab][:].rearrange("p k t -> p (k t)"), in_=ascr[og, :, :]), writes=[r_ag[ab]], dma=True)
                    if kind in ("q", "u"):
                        sbi = stq[0] % 2
                        stq[0] += 1
                        for j in range(4):
                            bk = next_A()

                            def f_mm(e, j=j, bk=bk, wi=wi, ab=ab):
                                ins = None
                                for kc in range(16):
                                    ins = e.matmul(banks[bk][:], lhsT=wr[wi][:, kc, j * 128:(j + 1) * 128], rhs=ag[ab][:, kc, :], start=(kc == 0), stop=(kc == 15))
                                return ins
                            S.add("pe", f_mm, reads=[r_wr[wi], r_ag[ab]], writes=[r_bank[bk]])
                            if kind == "q":
                                if j % 2 == 0:
                                    S.add("act", lambda e, j=j, bk=bk, sbi=sbi: e.copy(out=qst[sbi][:, j, :], in_=banks[bk][:]), reads=[r_bank[bk]], writes=[r_qst[sbi][j]])
                                else:
                                    S.add("dve", lambda e, j=j, bk=bk, sbi=sbi: e.tensor_copy(out=qst[sbi][:, j, :], in_=banks[bk][:]), reads=[r_bank[bk]], writes=[r_qst[sbi][j]])
                            else:
                                S.add("act", lambda e, j=j, bk=bk, sbi=sbi: e.activation(out=ust[sbi][:, j, :], in_=banks[bk][:], func=AF.Gelu_apprx_tanh),
                                      reads=[r_bank[bk]], writes=[r_ust[sbi][j]])
                        if kind == "q":
                            S.add("pool", lambda e, sbi=sbi, hb=hb, og=og: e.dma_start(out=qscr.rearrange("h p t -> p h t")[:, hb * 4:(hb + 1) * 4, og * 512:(og + 1) * 512], in_=qst[sbi][:]),
                                  reads=r_qst[sbi], dma=True)
                        else:
                            for j in range(4):
                                gg = hb * 4 + j
                                gb_ = B0 if j % 2 == 0 else B1

                                def f_sp(e, gg=gg, gb_=gb_, og=og):
                                    ins = None
                                    for i in range(4):
                                        o = banks[gb_][:, i * 128:(i + 1) * 128]
                                        e.matmul(o, lhsT=vn[:, og * 4 + i, gg * 128:(gg + 1) * 128], rhs=wmT[:, gg, :], start=True, stop=False)
                                        e.matmul(o, lhsT=ones_row[0:1, :], rhs=bs_hi[0:1, gg * 128:(gg + 1) * 128], start=False, stop=False)
                                        ins = e.matmul(o, lhsT=ones_row[0:1, :], rhs=bs_lo[0:1, gg * 128:(gg + 1) * 128], start=False, stop=True)
                                    return ins
                                S.add("pe", f_sp, reads=[r_vn[og * 4 + i][hb] for i in range(4)] + [r_wmT, r_bs, r_ones], writes=[r_bank[gb_]])
                                S.add("dve", lambda e, j=j, gb_=gb_, sbi=sbi: e.tensor_tensor(out=gst[sbi][:, j, :], in0=banks[gb_][:], in1=ust[sbi][:, j, :], op=ALU.mult),
                                      reads=[r_bank[gb_], r_ust[sbi][j]], writes=[r_gst[sbi][j]])
                            S.add("pool", lambda e, sbi=sbi, hb=hb, og=og: e.dma_start(out=catscr.rearrange("c p t -> p c t")[:, 8 + hb * 4:8 + (hb + 1) * 4, og * 512:(og + 1) * 512], in_=gst[sbi][:]),
                                  reads=r_gst[sbi], dma=True)
                    else:
                        for i in range(4):
                            ti = og * 4 + i
                            bk = next_A()

                            def f_mm(e, i=i, bk=bk, wi=wi, ab=ab):
                                ins = None
                                for kc in range(16):
                                    ins = e.matmul(banks[bk][:], lhsT=ag[ab][:, kc, i * 128:(i + 1) * 128], rhs=wr[wi][:, kc, :], start=(kc == 0), stop=(kc == 15))
                                return ins
                            S.add("pe", f_mm, reads=[r_wr[wi], r_ag[ab]], writes=[r_bank[bk]])
                            gi = gvc[0] % 2
                            gvc[0] += 1
                            S.add("act", lambda e, bk=bk, gi=gi: e.activation(out=gv[gi][:], in_=banks[bk][:], func=AF.Gelu_apprx_tanh),
                                  reads=[r_bank[bk]], writes=[r_gv[gi]])
                            k = stc[0] % NST
                            stc[0] += 1
                            st, mv, sc = st_t[k], mv_t[k], sc_t[k]

                            for c in range(4):
                                S.add("dve", lambda e, c=c, gi=gi, st=st: e.bn_stats(out=st[:, c, :], in_=gv[gi][:, c * 128:(c + 1) * 128]),
                                      reads=[r_gv[gi]], writes=[r_st4[k][c]])
                            for c in range(4):
                                S.add("dve", lambda e, c=c, st=st, mv=mv: e.bn_aggr(out=mv[:, c, :], in_=st[:, c:c + 1, :]),
                                      reads=[r_st4[k][c]], writes=[r_mv4[k][c]])
                            S.add("act", lambda e, sc=sc, mv=mv: e.activation(out=sc[:, 0:4], in_=mv[:, :, 1], func=AF.Ln, bias=LN_EPS, scale=1.0),
                                  reads=r_mv4[k], writes=[r_sca[k]])
                            S.add("act", lambda e, sc=sc: e.activation(out=sc[:, 4:8], in_=sc[:, 0:4], func=AF.Exp, scale=-0.5),
                                  reads=[r_sca[k]], writes=[r_scb[k]])
                            S.add("dve", lambda e, gi=gi, mv=mv: e.tensor_tensor(out=gz[gi][:].rearrange("p (g c) -> p g c", g=4), in0=gv[gi][:].rearrange("p (g c) -> p g c", g=4),
                                                                                   in1=mv[:, :, 0:1].to_broadcast([128, 4, 128]), op=ALU.subtract),
                                  reads=[r_gv[gi]] + r_mv4[k], writes=[r_gz[gi]])
                            S.add("pool", lambda e, gi=gi, sc=sc: e.tensor_tensor(out=gz[gi][:].rearrange("p (g c) -> p g c", g=4), in0=gz[gi][:].rearrange("p (g c) -> p g c", g=4),
                                                                                    in1=sc[:, 4:8].unsqueeze(2).to_broadcast([128, 4, 128]), op=ALU.mult),
                                  reads=[r_gz[gi], r_scb[k]], writes=[r_gz[gi]])
                            S.add("dve", lambda e, gi=gi, hb=hb: e.tensor_tensor(out=gz[gi][:], in0=gz[gi][:], in1=arena[0][:, hb * 512:(hb + 1) * 512], op=ALU.mult),
                                  reads=[r_gz[gi], r_arena[0]], writes=[r_gz[gi]])
                            S.add("pool", lambda e, gi=gi, hb=hb, ti=ti: e.tensor_tensor(out=vn[:, ti, hb * 512:(hb + 1) * 512], in0=gz[gi][:], in1=arena[1][:, hb * 512:(hb + 1) * 512], op=ALU.add),
                                  reads=[r_gz[gi], r_arena[1]], writes=[r_vn[ti][hb]])
            S.barrier()
            p2.close()

        if stop_after >= 3:
            p3 = es.enter_context(ExitStack())
            kT = [sbuf(p3, f"kT{i}", [128, 4096], BF16) for i in range(2)]
            vv = [sbuf(p3, f"vv{i}", [128, 32, 132], BF16) for i in range(2)]
            qT = [sbuf(p3, f"qT{i}", [128, 2048], BF16) for i in range(2)]
            NPT = 4
            pt = [sbuf(p3, f"pt{i}", [128, 2, 512], BF16) for i in range(NPT)]
            acs = [sbuf(p3, f"acs{i}", [128, 8, 132], F32) for i in range(2)]
            t1 = [sbuf(p3, f"t1_{i}", [128, 128], F32) for i in range(2)]
            at = [sbuf(p3, f"at{i}", [128, 128], F32) for i in range(2)]
            jk = [sbuf(p3, f"jk{i}", [128, 128], F32) for i in range(2)]
            yb = [sbuf(p3, f"yb{i}", [128, 128], BF16) for i in range(4)]
            fs = [sbuf(p3, f"fs{i}", [128, 8], F32) for i in range(2)]
            cst = [sbuf(p3, f"cst{i}", [128, 512], BF16) for i in range(2)]
            r_kT, r_vv, r_qT = RL(2), RL(2), RL(2)
            r_pt = [RL(2) for _ in range(NPT)]
            r_acs = [RL(8), RL(8)]
            r_t1, r_at, r_jk, r_cst = RL(2), RL(2), RL(2), RL(2)
            r_yb = RL(4)
            r_fs = [RL(6) for _ in range(2)]
            r_acc = RL(8)
            acc_bank = [B0, B0, B0, B1, B1, B1, T1, T1]
            MB2 = make_modbufs(p3, "b", T0)

            def acc_ap(idx, ncol=132):
                o = (idx % 3) * 132
                return banks[acc_bank[idx]][:, o:o + ncol]

            mod_todo = list(range(8, 24)) if not LITE else []
            if mod_todo:
                mod_load(mod_todo[0], MB2)
                mod_load(mod_todo[1], MB2)
            conv_todo = []
            for hq in range(32):
                conv_todo.append(lambda e, hq=hq: e.dma_start(out=w1s[hq, :, :].rearrange("p (k n) -> p k n", k=16),
                                                               in_=w_ff1[:, hq * 256:(hq + 1) * 256].rearrange("(kc p) n -> p kc n", p=128)))
            for cb in range(4):
                for hq8 in range(8):
                    conv_todo.append(lambda e, cb=cb, hq8=hq8: e.dma_start(out=w2s[cb * 8 + hq8, :, :].rearrange("p (j n) -> p j n", j=8),
                                                                         in_=w_ff2[hq8 * 1024:(hq8 + 1) * 1024, cb * 512:(cb + 1) * 512].rearrange("(j p) n -> p j n", p=128)))
            conv_ops = []
            if os.environ.get("DEV_NOCONV", "") == "1":
                conv_todo = []

            def conv_step(nmax):
                for _ in range(nmax):
                    if conv_todo:
                        conv_ops.append(S.add("pool", conv_todo.pop(0), dma=True))

            ptc = [0]
            fcnt = [0]
            cstc = [0]
            ybc = [0]
            pending = []

            def load_head(h):
                hb2 = h % 2
                S.add("sp", lambda e: e.dma_start(out=kT[hb2][:], in_=kscr[h, :, :]), writes=[r_kT[hb2]], dma=True)
                S.add("sp", lambda e: e.dma_start(out=vv[hb2][:].rearrange("p n c -> p (n c)"), in_=vscr[h, :, :]), writes=[r_vv[hb2]], dma=True)
                S.add("sp", lambda e: e.dma_start(out=qT[hb2][:], in_=qscr[h, :, :]), writes=[r_qT[hb2]], dma=True)

            def rec_pv(hb2, pb, pos, tq0, n, diag, ip):
                def f_pv(e):
                    ins = None
                    started = set()
                    for tq in range(tq0, 4):
                        for c in range(2):
                            bkk = acc_bank[c * 4 + tq]
                            st_flag = (n == 0) and (bkk not in started)
                            started.add(bkk)
                            ins = e.matmul(acc_ap(c * 4 + tq), lhsT=pt[pb][:, c, tq * 128:(tq + 1) * 128], rhs=vv[hb2][:, pos, 0:132],
                                           start=st_flag, stop=(diag and ip == tq))
                    return ins
                S.add("pe", f_pv, reads=r_pt[pb] + [r_vv[hb2]], writes=[r_acc[c * 4 + tq] for tq in range(tq0, 4) for c in range(2)])

            def rec_finalize(h, g, ab_):
                for idx in range(8):
                    if False:
                        S.add("act", lambda e, idx=idx: e.copy(out=acs[ab_][:, idx, :], in_=acc_ap(idx)), reads=[r_acc[idx]], writes=[r_acs[ab_][idx]])
                    else:
                        S.add("dve", lambda e, idx=idx: e.tensor_copy(out=acs[ab_][:, idx, :], in_=acc_ap(idx)), reads=[r_acc[idx]], writes=[r_acs[ab_][idx]])
                ybs = []
                for tq in range(4):
                    f = fcnt[0] % 2
                    fcnt[0] += 1
                    y = ybc[0] % 4
                    ybc[0] += 1
                    ybs.append(y)
                    i1, i2 = tq, 4 + tq
                    a1, a2 = acs[ab_][:, i1, :], acs[ab_][:, i2, :]
                    ra1, ra2 = r_acs[ab_][i1], r_acs[ab_][i2]
                    S.add("dve", lambda e, f=f, a1=a1: e.reciprocal(out=fs[f][:, 0:1], in_=a1[:, 128:129]), reads=[ra1], writes=[r_fs[f][0]])
                    S.add("dve", lambda e, f=f, a2=a2: e.reciprocal(out=fs[f][:, 1:2], in_=a2[:, 128:129]), reads=[ra2], writes=[r_fs[f][1]])
                    S.add("dve", lambda e, f=f: e.tensor_tensor(out=fs[f][:, 2:3], in0=fs[f][:, 1:2], in1=neglam[:], op=ALU.mult), reads=[r_fs[f][1], r_neglam], writes=[r_fs[f][2]])
                    S.add("dve", lambda e, f=f, a1=a1: e.tensor_scalar(out=t1[f][:], in0=a1[:, 0:128], scalar1=fs[f][:, 0:1], scalar2=None, op0=ALU.mult),
                          reads=[ra1, r_fs[f][0]], writes=[r_t1[f]])
                    S.add("dve", lambda e, f=f, a2=a2: e.scalar_tensor_tensor(out=at[f][:], in0=a2[:, 0:128], scalar=fs[f][:, 2:3], in1=t1[f][:], op0=ALU.mult, op1=ALU.add),
                          reads=[ra2, r_fs[f][2], r_t1[f]], writes=[r_at[f]])
                    S.add("dve", lambda e, f=f: e.memset(fs[f][:, 3:4], 0.0), writes=[r_fs[f][3]])
                    S.add("act", lambda e, f=f: e.activation(out=jk[f][:], in_=at[f][:], func=AF.Square, accum_out=fs[f][:, 3:4]),
                          reads=[r_at[f], r_fs[f][3]], writes=[r_jk[f], r_fs[f][3]])
                    S.add("act", lambda e, f=f: e.activation(out=fs[f][:, 4:5], in_=fs[f][:, 3:4], func=AF.Ln, bias=LN_EPS, scale=1.0 / 128.0),
                          reads=[r_fs[f][3]], writes=[r_fs[f][4]])
                    S.add("act", lambda e, f=f: e.activation(out=fs[f][:, 5:6], in_=fs[f][:, 4:5], func=AF.Exp, scale=-0.5),
                          reads=[r_fs[f][4]], writes=[r_fs[f][5]])
                    S.add("dve", lambda e, f=f, y=y: e.scalar_tensor_tensor(out=yb[y][:], in0=at[f][:], scalar=fs[f][:, 5:6], in1=sg08[:], op0=ALU.mult, op1=ALU.mult),
                          reads=[r_at[f], r_fs[f][5], r_sg08], writes=[r_yb[y]])

                def tail():
                    for tq in range(4):
                        y = ybs[tq]
                        S.add("pe", lambda e, y=y, tq=tq: e.transpose(out=bview(T0)[:, tq * 128:(tq + 1) * 128], in_=yb[y][:], identity=ident[:]),
                              reads=[r_yb[y], r_ident], writes=[r_bank[T0]])
                    cb_ = cstc[0] % 2
                    cstc[0] += 1
                    S.add("dve", lambda e, cb_=cb_: e.tensor_copy(out=cst[cb_][:], in_=bview(T0)[:, 0:512]), reads=[r_bank[T0]], writes=[r_cst[cb_]])
                    S.add("sp", lambda e, cb_=cb_: e.dma_start(out=catscr[h, :, g * 512:(g + 1) * 512], in_=cst[cb_][:]), reads=[r_cst[cb_]], dma=True)
                    if mod_todo:
                        cbm = mod_todo.pop(0)
                        mod_compute(cbm, MB2)
                        if len(mod_todo) >= 2:
                            mod_load(mod_todo[1], MB2)
                    conv_step(2)
                pending.append(tail)

            load_head(0)
            gcount = 0
            for h in range(8):
                hb2 = h % 2
                if h + 1 < 8:
                    load_head(h + 1)
                for g in range(4):
                    keys = []
                    for Gk in range(2 * g):
                        for ip in range(4):
                            keys.append((Gk * 4 + ip, 0, False, ip))
                    for Gk in (2 * g, 2 * g + 1):
                        for ip in range(4):
                            keys.append((Gk * 4 + ip, ip * 128, Gk == 2 * g + 1, ip))
                    prev = None
                    for n, (pos, c0, diag, ip) in enumerate(keys):
                        pb = ptc[0] % NPT
                        ptc[0] += 1
                        for c in range(2):
                            bk = A[2 * (n % 2) + c]
                            S.add("pe", lambda e, c=c, bk=bk, pos=pos, c0=c0, hb2=hb2, g=g: e.matmul(banks[bk][:, c0:512], lhsT=kT[hb2][c * 64:(c + 1) * 64, pos * 128:(pos + 1) * 128],
                                                                                                  rhs=qT[hb2][c * 64:(c + 1) * 64, g * 512 + c0:(g + 1) * 512], start=True, stop=True),
                                  reads=[r_kT[hb2], r_qT[hb2]], writes=[r_bank[bk]])
                            S.add("act", lambda e, c=c, bk=bk, pb=pb, c0=c0: e.activation(out=pt[pb][:, c, c0:512], in_=banks[bk][:, c0:512], func=AF.Exp, scale=0.125),
                                  reads=[r_bank[bk]], writes=[r_pt[pb][c]])
                        if diag:
                            S.add("pool" if os.environ.get("DEV_P2B", "") == "1" else "dve", lambda e, pb=pb, c0=c0: e.memset(pt[pb][64:128, :, c0:c0 + 64], 0.0), reads=r_pt[pb], writes=r_pt[pb])
                        if prev is not None:
                            rec_pv(*prev)
                        prev = (hb2, pb, pos, c0 // 128, n, diag, ip)
                        if n == 2 and pending:
                            pending.pop(0)()
                    rec_pv(*prev)
                    rec_finalize(h, g, gcount % 2)
                    gcount += 1
            while pending:
                pending.pop(0)()
            while mod_todo:
                cbm = mod_todo.pop(0)
                mod_compute(cbm, MB2)
                if len(mod_todo) >= 2:
                    mod_load(mod_todo[1], MB2)
            conv_step(1000)
            S.barrier()
            p3.close()

        if stop_after >= 4:
            p4 = es.enter_context(ExitStack())
            wo = sbuf(p4, "wo", [128, 16, 2048], BF16)
            cg = [sbuf(p4, "cg0", [128, 16, 512], BF16)] * 2
            x3t = [sbuf(p4, f"x3_{i}", [128, 2048], F32) for i in range(2)]
            yt = [sbuf(p4, f"y3_{i}", [128, 2048], F32) for i in range(4)]
            mn = [sbuf(p4, "mn0", [128, 4, 2048], BF16)] * 2
            mT = [sbuf(p4, "mT0", [128, 16, 512], BF16)] * 2
            r_wo = RL(4)
            r_cg, r_x3t = [Res()] * 2, RL(2)
            r_yt = [RL(4) for _ in range(4)]
            pend3 = []
            r_mn = [RL(4)] * 2
            r_mT = [RL(16)] * 2
            for cb in range(4):
                S.add("pool", lambda e, cb=cb: e.dma_start(out=wo[:, :, cb * 512:(cb + 1) * 512], in_=w_out[:, cb * 512:(cb + 1) * 512].rearrange("(kc p) n -> p kc n", p=128)),
                      writes=[r_wo[cb]], dma=True)
            S.add("sp", lambda e: e.dma_start(out=arena[0][:], in_=modscr[0:1, :].partition_broadcast(128)), writes=[r_arena[0]], dma=True)
            S.add("sp", lambda e: e.dma_start(out=arena[1][:], in_=ln1g[0:1, :].partition_broadcast(128)), writes=[r_arena[1]], dma=True)
            S.add("sp", lambda e: e.dma_start(out=arena[2][:], in_=ln1b[0:1, :].partition_broadcast(128)), writes=[r_arena[2]], dma=True)
            for og in range(4):
                cb2 = og % 2
                S.add("sp", lambda e, og=og, cb2=cb2: e.dma_start(out=cg[cb2][:], in_=catscr.rearrange("c p t -> p c t")[:, :, og * 512:(og + 1) * 512]), writes=[r_cg[cb2]], dma=True)
                for pr in range(2):
                    tiles = (2 * pr, 2 * pr + 1)
                    for i in tiles:
                        b = i % 2
                        row0 = ((2 * og + 1) * 4 + i) * 128
                        S.add("sp", lambda e, b=b, row0=row0: e.dma_start(out=x3t[b][:], in_=xs[row0:row0 + 128, :]), writes=[r_x3t[b]], dma=True)
                    for i in tiles:
                        b = i % 2
                        for cb in range(4):
                            bk = next_A()

                            def f_mm(e, i=i, cb=cb, bk=bk, cb2=cb2):
                                ins = None
                                for kc in range(16):
                                    ins = e.matmul(banks[bk][:], lhsT=cg[cb2][:, kc, i * 128:(i + 1) * 128], rhs=wo[:, kc, cb * 512:(cb + 1) * 512], start=(kc == 0), stop=(kc == 15))
                                return ins
                            S.add("pe", f_mm, reads=[r_cg[cb2], r_wo[cb]], writes=[r_bank[bk]])
                            sl = slice(cb * 512, (cb + 1) * 512)
                            S.add("dve", lambda e, i=i, bk=bk, sl=sl: e.tensor_tensor(out=yt[i][:, sl], in0=banks[bk][:], in1=arena[0][:, sl], op=ALU.mult),
                                  reads=[r_bank[bk], r_arena[0]], writes=[r_yt[i][cb]])
                            S.add("dve", lambda e, b=b, i=i, sl=sl: e.scalar_tensor_tensor(out=yt[i][:, sl], in0=x3t[b][:, sl], scalar=ALPHA, in1=yt[i][:, sl], op0=ALU.mult, op1=ALU.add),
                                  reads=[r_x3t[b], r_yt[i][cb]], writes=[r_yt[i][cb]])
                    while pend3:
                        pend3.pop(0)()

                    def chain3(og=og, cb2=cb2, tiles=tiles):
                        lnA = {}
                        for i in tiles:
                            b = i % 2
                            lnA[i] = ln_stats(lambda a, c, i=i: yt[i][:, a:c], r_yt[i], defer=True)
                        for i in tiles:
                            lnA[i][3]()
                        for i in tiles:
                            b = i % 2
                            rstd, nb, rr, _ = lnA[i]
                            S.add("act", lambda e, i=i, rstd=rstd, nb=nb: e.activation(out=yt[i][:], in_=yt[i][:], func=AF.Identity, bias=nb, scale=rstd),
                                  reads=r_yt[i] + rr, writes=r_yt[i])
                        for i in tiles:
                            b = i % 2
                            S.add("dve", lambda e, i=i: e.tensor_tensor(out=yt[i][:], in0=yt[i][:], in1=arena[1][:], op=ALU.mult), reads=r_yt[i] + [r_arena[1]], writes=r_yt[i])
                        for i in tiles:
                            b = i % 2
                            S.add("pool" if b == 0 else "dve", lambda e, i=i: e.tensor_tensor(out=yt[i][:], in0=yt[i][:], in1=arena[2][:], op=ALU.add), reads=r_yt[i] + [r_arena[2]], writes=r_yt[i])
                        for i in tiles:
                            b = i % 2
                            ti = og * 4 + i
                            S.add("pool", lambda e, i=i, ti=ti: e.dma_start(out=h1scr[ti * 128:(ti + 1) * 128, :], in_=yt[i][:]), reads=r_yt[i], dma=True)
                        lnB = {}
                        for i in tiles:
                            b = i % 2
                            lnB[i] = ln_stats(lambda a, c, i=i: yt[i][:, a:c], r_yt[i], defer=True)
                        for i in tiles:
                            lnB[i][3]()
                        for i in tiles:
                            b = i % 2
                            rstd2, nb2, rr2, _ = lnB[i]
                            S.add("act", lambda e, i=i, cb2=cb2, rstd2=rstd2, nb2=nb2: e.activation(out=mn[cb2][:, i, :], in_=yt[i][:], func=AF.Identity, bias=nb2, scale=rstd2),
                                  reads=r_yt[i] + rr2, writes=[r_mn[cb2][i]])
                    pend3.append(chain3)
                def tg3(og=og, cb2=cb2):
                    transpose_group(mn[cb2], r_mn[cb2], mT[cb2], r_mT[cb2], 48, 32)
                    S.add("pool", lambda e: e.dma_start(out=mscr[og, :, :], in_=mT[cb2][:].rearrange("p k t -> p (k t)")), reads=r_mT[cb2], dma=True)
                pend3.append(tg3)
            while pend3:
                pend3.pop(0)()
            S.barrier()
            p4.close()

        if stop_after >= 5:
            p5 = es.enter_context(ExitStack())
            hT = sbuf(p5, "hT", [128, 64, 512], BF16)
            mg = sbuf(p5, "mg", [128, 16, 512], BF16)
            w1r = [sbuf(p5, f"w1r{i}", [128, 16, 256], BF16) for i in range(3)]
            w2r = [sbuf(p5, f"w2r{i}", [128, 8, 512], BF16) for i in range(3)]
            yf = [sbuf(p5, f"yf{i}", [128, 2048], F32) for i in range(4)]
            h1p = [sbuf(p5, f"h1p{i}", [128, 512], F32) for i in range(2)]
            rl = [sbuf(p5, f"rl{i}", [128, 512], F32) for i in range(2)]
            r_hT = RL(64)
            r_mg = Res()
            r_w1r, r_w2r = RL(3), RL(3)
            r_yf = [RL(4) for _ in range(4)]
            r_h1p, r_rl = RL(2), RL(2)
            S.add("sp", lambda e: e.dma_start(out=arena[0][:], in_=modscr[1:2, :].partition_broadcast(128)), writes=[r_arena[0]], dma=True)
            S.add("sp", lambda e: e.dma_start(out=arena[1][:], in_=ln2g[0:1, :].partition_broadcast(128)), writes=[r_arena[1]], dma=True)
            S.add("sp", lambda e: e.dma_start(out=arena[2][:], in_=ln2b[0:1, :].partition_broadcast(128)), writes=[r_arena[2]], dma=True)
            w1c, w2c, rlc, hpc = [0], [0], [0], [0]
            pend4 = []
            for og in range(4):
                S.add("sp", lambda e, og=og: e.dma_start(out=mg[:].rearrange("p k t -> p (k t)"), in_=mscr[og, :, :]), writes=[r_mg], dma=True)
                for hq in range(32):
                    wi = w1c[0] % 3
                    w1c[0] += 1
                    S.add("sp", lambda e, wi=wi, hq=hq: e.dma_start(out=w1r[wi][:].rearrange("p k n -> p (k n)"), in_=w1s[hq, :, :]),
                          writes=[r_w1r[wi]], dma=True)
                    for j in range(2):
                        hc = hq * 2 + j
                        bk = next_A()

                        def f_mm(e, j=j, bk=bk, wi=wi):
                            ins = None
                            for kc in range(16):
                                ins = e.matmul(banks[bk][:], lhsT=w1r[wi][:, kc, j * 128:(j + 1) * 128], rhs=mg[:, kc, :], start=(kc == 0), stop=(kc == 15))
                            return ins
                        S.add("pe", f_mm, reads=[r_w1r[wi], r_mg], writes=[r_bank[bk]])
                        ri = rlc[0] % 2
                        rlc[0] += 1
                        S.add("act", lambda e, bk=bk, ri=ri: e.activation(out=rl[ri][:], in_=banks[bk][:], func=AF.Relu), reads=[r_bank[bk]], writes=[r_rl[ri]])
                        eng2 = "dve" if hc % 2 == 0 else "pool"
                        S.add(eng2, lambda e, ri=ri, hc=hc: e.tensor_tensor(out=hT[:, hc, :], in0=rl[ri][:], in1=rl[ri][:], op=ALU.mult), reads=[r_rl[ri]], writes=[r_hT[hc]])
                    if hq % 8 == 5 and pend4:
                        pend4.pop(0)()
                for cb in range(4):
                    bset = [B0, B1, T0, T1] if cb % 2 == 0 else A
                    sl = slice(cb * 512, (cb + 1) * 512)
                    for hq8 in range(8):
                        wi = w2c[0] % 3
                        w2c[0] += 1
                        S.add("sp", lambda e, wi=wi, hq8=hq8, cb=cb: e.dma_start(out=w2r[wi][:].rearrange("p j n -> p (j n)"), in_=w2s[cb * 8 + hq8, :, :]),
                              writes=[r_w2r[wi]], dma=True)

                        def f_mm(e, wi=wi, hq8=hq8, bset=bset):
                            ins = None
                            for j in range(8):
                                hc = hq8 * 8 + j
                                for i in range(4):
                                    ins = e.matmul(banks[bset[i]][:], lhsT=hT[:, hc, i * 128:(i + 1) * 128], rhs=w2r[wi][:, j, :], start=(hc == 0), stop=(hc == 63))
                            return ins
                        S.add("pe", f_mm, reads=[r_w2r[wi]] + r_hT[hq8 * 8:(hq8 + 1) * 8], writes=[r_bank[b] for b in bset])
                    for i in range(4):
                        ti = og * 4 + i
                        hp = hpc[0] % 2
                        hpc[0] += 1
                        S.add("pool", lambda e, hp=hp, ti=ti, sl=sl: e.dma_start(out=h1p[hp][:], in_=h1scr[ti * 128:(ti + 1) * 128, sl]), writes=[r_h1p[hp]], dma=True)
                        S.add("dve", lambda e, i=i, sl=sl, bset=bset: e.tensor_tensor(out=yf[i][:, sl], in0=banks[bset[i]][:], in1=arena[0][:, sl], op=ALU.mult),
                              reads=[r_bank[bset[i]], r_arena[0]], writes=[r_yf[i][cb]])
                        S.add("dve", lambda e, i=i, sl=sl, hp=hp: e.scalar_tensor_tensor(out=yf[i][:, sl], in0=h1p[hp][:], scalar=ALPHA, in1=yf[i][:, sl], op0=ALU.mult, op1=ALU.add),
                              reads=[r_h1p[hp], r_yf[i][cb]], writes=[r_yf[i][cb]])
                for i in range(4):
                    ti = og * 4 + i

                    def tail4(i=i, ti=ti):
                        rstd, nb, rr = ln_stats(lambda a, c, i=i: yf[i][:, a:c], r_yf[i])
                        S.add("act", lambda e, i=i, rstd=rstd, nb=nb: e.activation(out=yf[i][:], in_=yf[i][:], func=AF.Identity, bias=nb, scale=rstd),
                              reads=r_yf[i] + rr, writes=r_yf[i])
                        S.add("dve", lambda e, i=i: e.tensor_tensor(out=yf[i][:], in0=yf[i][:], in1=arena[1][:], op=ALU.mult), reads=r_yf[i] + [r_arena[1]], writes=r_yf[i])
                        S.add("pool", lambda e, i=i: e.tensor_tensor(out=yf[i][:], in0=yf[i][:], in1=arena[2][:], op=ALU.add), reads=r_yf[i] + [r_arena[2]], writes=r_yf[i])
                        S.wait_at_end(S.add("pool", lambda e, i=i, ti=ti: e.dma_start(out=out[ti * 128:(ti + 1) * 128, :], in_=yf[i][:]), reads=r_yf[i], dma=True))
                    pend4.append(tail4)
            while pend4:
                pend4.pop(0)()
            S.barrier()
            p5.close()

        S.barrier()
        S.emit()
    return nc


def _core_layout(x, c, core):
    b, j = core // 2, core % 2
    xb = x[b]
    if j == 0:
        loc = np.concatenate([np.zeros((128, 2048), np.float32), xb[:3968]], axis=0)
    else:
        loc = xb
    loc = loc.reshape(32, 128, 2048)
    order = []
    valid = np.ones((128, 32), np.float32)
    for G in range(8):
        for i in range(4):
            L = 8 * (G // 2) + 2 * i + (G % 2)
            order.append(L)
            if j == 0 and L == 0:
                valid[:, G * 4 + i] = 0.0
    xs = np.ascontiguousarray(loc[order].reshape(4096, 2048))
    ccol = np.ascontiguousarray(c[b].reshape(16, 128).T)
    return xs, valid, ccol


_NC_CACHE = {}


def kernel(x, c, w_ada, b_ada, w_in, lambda_q1, lambda_k1, lambda_q2, lambda_k2, subln_g,
           gmlp_ln_g, gmlp_ln_b, gmlp_ws, gmlp_bs, w_out, ln1_g, ln1_b, w_ff1, w_ff2, ln2_g, ln2_b):
    f = lambda a: np.ascontiguousarray(np.asarray(a, dtype=np.float32))
    x = f(x)
    c = f(c)
    shared = {
        "w_ada": f(w_ada)[0], "b_ada": f(b_ada).reshape(1, 12288), "w_in": f(w_in)[0],
        "lambda_q1": f(lambda_q1).reshape(1, 64), "lambda_k1": f(lambda_k1).reshape(1, 64),
        "lambda_q2": f(lambda_q2).reshape(1, 64), "lambda_k2": f(lambda_k2).reshape(1, 64),
        "subln_g": f(subln_g).reshape(1, 128),
        "gmlp_ln_g": f(gmlp_ln_g).reshape(1, 1024), "gmlp_ln_b": f(gmlp_ln_b).reshape(1, 1024),
        "gmlp_ws": f(gmlp_ws)[0], "gmlp_bs": f(gmlp_bs).reshape(1, 1024),
        "w_out": f(w_out)[0], "ln1_g": f(ln1_g).reshape(1, 2048), "ln1_b": f(ln1_b).reshape(1, 2048),
        "w_ff1": f(w_ff1)[0], "w_ff2": f(w_ff2)[0], "ln2_g": f(ln2_g).reshape(1, 2048), "ln2_b": f(ln2_b).reshape(1, 2048),
    }
    in_maps = []
    for core in range(8):
        xs, valid, ccol = _core_layout(x, c, core)
        m = dict(shared)
        m.update({"xs": xs, "valid": valid, "ccol": ccol})
        in_maps.append(m)
    if "nc" not in _NC_CACHE:
        _NC_CACHE["nc"] = build_program()
    nc = _NC_CACHE["nc"]
    res = run_bass_kernel_spmd(nc, in_maps, core_ids=list(range(8)))
    outp = np.empty((4, 4096, 2048), np.float32)
    for core in range(8):
        b, j = core // 2, core % 2
        o = np.asarray(res.results[core]["out"]).reshape(16, 128, 2048)
        ov = outp[b].reshape(32, 128, 2048)
        ov[j::2] = o
    return outp
```

```python
import numpy as np
from contextlib import ExitStack
import concourse.bass as bass
import concourse.mybir as mybir
from concourse.bass_utils import run_bass_kernel_spmd

F32 = mybir.dt.float32
BF16 = mybir.dt.bfloat16
AF = mybir.ActivationFunctionType
ALU = mybir.AluOpType
AX = mybir.AxisListType

ENGS = ("pe", "act", "dve", "pool", "sp")
ALPHA = float((2.0 * 1) ** 0.25)
LN_EPS = 1e-5
LAMBDA_INIT = 0.2


class Res:
    __slots__ = ("name", "w", "r")

    def __init__(self, name=""):
        self.name = name
        self.w = None
        self.r = []


def RL(n):
    return [Res() for _ in range(n)]


class Op:
    __slots__ = ("eng", "fn", "deps", "needs", "token", "dma", "slot")


class Sched:
    def __init__(self, nc, es, n_sp=24, n_pool=10):
        self.nc = nc
        self.ops = {e: [] for e in ENGS}
        self.esem = {e: es.enter_context(nc.semaphore("s_" + e)) for e in ENGS}
        self.dsem = {
            "sp": [es.enter_context(nc.semaphore(f"d_sp{i}")) for i in range(n_sp)],
            "pool": [es.enter_context(nc.semaphore(f"d_pl{i}")) for i in range(n_pool)],
        }
        self.dcnt = {k: 0 for k in self.dsem}
        self.dlast = {k: [None] * len(v) for k, v in self.dsem.items()}
        self.final_waits = []

    def add(self, eng, fn, reads=(), writes=(), dma=False, extra=()):
        op = Op()
        op.eng = eng
        op.fn = fn
        op.needs = False
        op.dma = dma
        op.token = None
        op.slot = None
        deps = []
        for r in reads:
            if r.w is not None:
                deps.append((r.w, 0))
        for w in writes:
            if w.w is not None:
                deps.append((w.w, 1))
            for rr in w.r:
                deps.append((rr, 2))
        for d in extra:
            deps.append((d, 0))
        if dma:
            n = self.dcnt[eng]
            ns = len(self.dsem[eng])
            slot = n % ns
            prev = self.dlast[eng][slot]
            if prev is not None:
                deps.append((prev, 3))
            op.slot = (slot, 16 * (n // ns + 1))
            self.dlast[eng][slot] = op
            self.dcnt[eng] = n + 1
        final = []
        seen = set()
        for d, kind in deps:
            if d is op or id(d) in seen:
                continue
            if d.eng == eng and not d.dma and not dma:
                if eng == "pe":
                    continue
            seen.add(id(d))
            final.append(d)
            d.needs = True
        op.deps = final
        for r in reads:
            r.r.append(op)
        for w in writes:
            w.w = op
            w.r = []
        self.ops[eng].append(op)
        return op

    def barrier(self):
        lasts = [self.ops[e][-1] for e in ENGS
                 if self.ops[e] and not self.ops[e][-1].dma and self.ops[e][-1].fn is not None]
        dmas = [op for k in self.dlast for op in self.dlast[k] if op is not None]
        for e in ("pe", "act", "dve", "pool", "sp"):
            self.add(e, None, extra=lasts + dmas)

    def wait_at_end(self, op):
        op.needs = True
        self.final_waits.append(op)

    def emit(self):
        nc = self.nc
        for e in ENGS:
            c = 0
            for op in self.ops[e]:
                if op.dma:
                    op.token = (self.dsem[e][op.slot[0]], op.slot[1])
                elif op.needs:
                    c += 1
                    op.token = (self.esem[e], c)

        def run(e, eng, extra=()):
            waited = {}
            for op in self.ops[e]:
                w = {}
                for d in op.deps:
                    s, v = d.token
                    k = id(s)
                    if waited.get(k, 0) >= v:
                        continue
                    if k not in w or w[k][1] < v:
                        w[k] = (s, v)
                for k, (s, v) in w.items():
                    eng.wait_ge(s, v)
                    waited[k] = v
                if op.fn is None:
                    assert not op.needs
                    continue
                ins = op.fn(eng)
                if op.dma:
                    ins.then_inc(op.token[0], 16)
                elif op.needs:
                    ins.then_inc(op.token[0], 1)
            for op in extra:
                s, v = op.token
                if waited.get(id(s), 0) < v:
                    eng.wait_ge(s, v)
                    waited[id(s)] = v

        with nc.Block() as block:
            @block.tensor
            def _(eng):
                run("pe", eng)

            @block.scalar
            def _(eng):
                run("act", eng)

            @block.vector
            def _(eng):
                run("dve", eng)

            @block.gpsimd
            def _(eng):
                run("pool", eng)

            @block.sync
            def _(eng):
                run("sp", eng, self.final_waits)


def build_program(debug=False, stop_after=99):
    import os
    LITE = os.environ.get("DEV_LITE", "") == "1"
    nc = bass.Bass("TRN2", target_bir_lowering=False)

    def din(name, shape, dt=F32):
        if LITE and name in ("w_ada",):
            shape = [128, 128]
        return nc.dram_tensor(name, shape, dt, kind="ExternalInput").ap()

    def dscr(name, shape, dt=BF16):
        kind = "ExternalOutput" if debug else "Internal"
        return nc.dram_tensor(name, shape, dt, kind=kind).ap()

    xs = din("xs", [4096, 2048])
    valid = din("valid", [128, 32])
    ccol = din("ccol", [128, 16])
    w_ada = din("w_ada", [2048, 12288])
    b_ada = din("b_ada", [1, 12288])
    w_in = din("w_in", [2048, 5120])
    lq1 = din("lambda_q1", [1, 64])
    lk1 = din("lambda_k1", [1, 64])
    lq2 = din("lambda_q2", [1, 64])
    lk2 = din("lambda_k2", [1, 64])
    subln = din("subln_g", [1, 128])
    glg = din("gmlp_ln_g", [1, 1024])
    glb = din("gmlp_ln_b", [1, 1024])
    gws = din("gmlp_ws", [8, 128, 128])
    gbs = din("gmlp_bs", [1, 1024])
    w_out = din("w_out", [2048, 2048])
    ln1g = din("ln1_g", [1, 2048])
    ln1b = din("ln1_b", [1, 2048])
    w_ff1 = din("w_ff1", [2048, 8192])
    w_ff2 = din("w_ff2", [8192, 2048])
    ln2g = din("ln2_g", [1, 2048])
    ln2b = din("ln2_b", [1, 2048])
    out = nc.dram_tensor("out", [2048, 2048], F32, kind="ExternalOutput").ap()

    kscr = dscr("kscr", [8, 128, 4096])
    vscr = dscr("vscr", [8, 128, 32 * 132])
    ascr = dscr("ascr", [4, 128, 16 * 512])
    qscr = dscr("qscr", [8, 128, 2048])
    catscr = dscr("catscr", [16, 128, 2048])
    mscr = dscr("mscr", [4, 128, 16 * 512])
    h1scr = dscr("h1scr", [2048, 2048], F32)
    modscr = dscr("modscr", [2, 2048], F32)
    w1s = nc.dram_tensor("w1s", [32, 128, 16 * 256], BF16, kind="Internal").ap()
    wis = nc.dram_tensor("wis", [6, 128, 16 * 512], BF16, kind="Internal").ap()
    w2s = nc.dram_tensor("w2s", [32, 128, 8 * 512], BF16, kind="Internal").ap()

    with ExitStack() as es:
        S = Sched(nc, es)

        def sbuf(stack, name, shape, dt):
            return stack.enter_context(nc.sbuf_tensor(name, shape, dt))

        identf = sbuf(es, "identf", [128, 128], F32)
        ident = sbuf(es, "ident", [128, 128], BF16)
        modc = sbuf(es, "modc", [128, 64], F32)
        validt = sbuf(es, "validt", [128, 32], F32)
        neglam = sbuf(es, "neglam", [128, 1], F32)
        sg08 = sbuf(es, "sg08", [128, 128], F32)
        wmT = sbuf(es, "wmT", [128, 8, 128], BF16)
        bs_hi = sbuf(es, "bs_hi", [1, 1024], BF16)
        bs_lo = sbuf(es, "bs_lo", [1, 1024], BF16)
        ones_row = sbuf(es, "ones_row", [1, 128], BF16)
        arena = [sbuf(es, f"arena{i}", [128, 2048], F32) for i in range(3)]
        r_arena = RL(3)
        r_identf, r_ident, r_valid, r_neglam, r_sg08, r_wmT, r_bs, r_ones = RL(8)
        r_modc = RL(4)
        banks = [es.enter_context(nc.psum_tensor(f"bank{i}", [128, 512], F32)) for i in range(8)]
        r_bank = RL(8)
        A = [0, 1, 2, 3]
        B0, B1, T0, T1 = 4, 5, 6, 7

        def bview(b):
            return banks[b][:].bitcast(BF16)

        NST = 4
        st_t = [sbuf(es, f"st{i}", [128, 4, 6], F32) for i in range(NST)]
        mv_t = [sbuf(es, f"mv{i}", [128, 4, 2], F32) for i in range(NST)]
        sc_t = [sbuf(es, f"sc{i}", [128, 12], F32) for i in range(NST)]
        r_st, r_mv, r_sca, r_scb, r_scc = RL(NST), RL(NST), RL(NST), RL(NST), RL(NST)
        r_st4 = [RL(4) for _ in range(NST)]
        r_mv4 = [RL(4) for _ in range(NST)]
        stc = [0]

        def ln_stats(src_ap_fn, r_src):
            k = stc[0] % NST
            stc[0] += 1
            st, mv, sc = st_t[k], mv_t[k], sc_t[k]

            for c in range(4):
                S.add("dve", lambda e, c=c: e.bn_stats(out=st[:, c, :], in_=src_ap_fn(c * 512, (c + 1) * 512)), reads=r_src, writes=[r_st4[k][c]])
            S.add("dve", lambda e: e.bn_aggr(out=mv[:, 0, :], in_=st[:]), reads=r_st4[k], writes=[r_mv[k]])
            S.add("act", lambda e: e.activation(out=sc[:, 0:1], in_=mv[:, 0, 1:2], func=AF.Ln, bias=LN_EPS, scale=1.0),
                  reads=[r_mv[k]], writes=[r_sca[k]])
            S.add("act", lambda e: e.activation(out=sc[:, 1:2], in_=sc[:, 0:1], func=AF.Exp, scale=-0.5),
                  reads=[r_sca[k]], writes=[r_scb[k]])
            S.add("dve", lambda e: e.tensor_scalar(out=sc[:, 2:3], in0=mv[:, 0, 0:1], scalar1=-1.0, scalar2=sc[:, 1:2],
                                                   op0=ALU.mult, op1=ALU.mult),
                  reads=[r_mv[k], r_scb[k]], writes=[r_scc[k]])
            return sc[:, 1:2], sc[:, 2:3], [r_scb[k], r_scc[k]]

        S.add("pool", lambda e: e.memset(identf[:], 0.0), writes=[r_identf])
        S.add("pool", lambda e: e.affine_select(out=identf[:], in_=identf[:], pattern=[[-1, 128]],
                                                compare_op=ALU.not_equal, fill=1.0, base=0, channel_multiplier=1),
              reads=[r_identf], writes=[r_identf])
        S.add("dve", lambda e: e.tensor_copy(out=ident[:], in_=identf[:]), reads=[r_identf], writes=[r_ident])
        S.add("pool", lambda e: e.memset(ones_row[:], 1.0), writes=[r_ones])
        S.add("sp", lambda e: e.dma_start(out=validt[:], in_=valid[:, :]), writes=[r_valid], dma=True)

        cbc = sbuf(es, "cbc", [128, 16, 128], BF16)
        p0 = es.enter_context(ExitStack())
        c_sb = sbuf(p0, "c_sb", [128, 16], F32)
        c_act = sbuf(p0, "c_act", [128, 16], F32)
        lam_t = sbuf(p0, "lam_t", [128, 4, 64], F32)
        lam_s = sbuf(p0, "lam_s", [128, 8], F32)
        wst = [sbuf(p0, f"wst{i}", [128, 128], BF16) for i in range(2)]
        bs_f = sbuf(p0, "bs_f", [1, 1024], F32)
        bs_f2 = sbuf(p0, "bs_f2", [1, 1024], F32)
        r_c, r_cact, r_cbc, r_junk, r_lamt, r_lams, r_bsf, r_bsf2 = RL(8)
        r_wt, r_bb, r_mtmp, r_wst = RL(2), RL(2), RL(2), RL(2)

        S.add("sp", lambda e: e.dma_start(out=c_sb[:], in_=ccol[:, :]), writes=[r_c], dma=True)
        S.add("act", lambda e: e.activation(out=c_act[:], in_=c_sb[:], func=AF.Silu), reads=[r_c], writes=[r_cact])
        S.add("dve", lambda e: e.tensor_copy(out=cbc[:], in_=c_act[:].unsqueeze(2).to_broadcast([128, 16, 128])),
              reads=[r_cact], writes=[r_cbc])

        def make_modbufs(stack, tag, bank):
            d = {}
            d["wt"] = [sbuf(stack, f"wt_ada{tag}{i}", [128, 16, 512], BF16) for i in range(2)]
            d["bb"] = [sbuf(stack, f"bb{tag}{i}", [128, 512], F32) for i in range(2)]
            d["mtmp"] = [sbuf(stack, f"mtmp{tag}{i}", [128, 512], F32) for i in range(2)]
            d["junk"] = sbuf(stack, f"junk{tag}", [128, 128], F32)
            d["r_wt"], d["r_bb"], d["r_mtmp"] = RL(2), RL(2), RL(2)
            d["r_junk"] = Res()
            d["bank"] = bank
            return d

        def mod_load(cb, MB):
            k = cb % 2
            wt_, bb_ = MB["wt"], MB["bb"]
            S.add("pool", lambda e: e.dma_start(out=wt_[k][:], in_=w_ada[:, cb * 512:(cb + 1) * 512].rearrange("(kc p) n -> p kc n", p=128)),
                  writes=[MB["r_wt"][k]], dma=True)
            S.add("sp", lambda e: e.dma_start(out=bb_[k][:], in_=b_ada[0:1, cb * 512:(cb + 1) * 512].partition_broadcast(128)),
                  writes=[MB["r_bb"][k]], dma=True)

        def mod_compute(cb, MB):
            k = cb % 2
            wt_, bb_, mtmp_, junk_ = MB["wt"], MB["bb"], MB["mtmp"], MB["junk"]
            r_wt_, r_bb_, r_mtmp_, r_junk_ = MB["r_wt"], MB["r_bb"], MB["r_mtmp"], MB["r_junk"]
            bk = MB["bank"] if MB["bank"] is not None else A[cb % 4]

            def f_mm(e):
                ins = None
                for kc in range(16):
                    ins = e.matmul(banks[bk][:], lhsT=cbc[:, kc, :], rhs=wt_[k][:, kc, :], start=(kc == 0), stop=(kc == 15))
                return ins
            S.add("pe", f_mm, reads=[r_cbc, r_wt_[k]], writes=[r_bank[bk]])
            kind = cb // 4
            plus1 = 1.0 if kind in (1, 2, 4, 5) else 0.0
            S.add("dve", lambda e: e.scalar_tensor_tensor(out=mtmp_[k][:], in0=banks[bk][:], scalar=plus1, in1=bb_[k][:],
                                                          op0=ALU.add, op1=ALU.add),
                  reads=[r_bank[bk], r_bb_[k]], writes=[r_mtmp_[k]])
            if kind in (2, 5):
                gi = 0 if kind == 2 else 1
                c0 = (cb % 4) * 512
                S.add("sp", lambda e: e.dma_start(out=modscr[gi:gi + 1, c0:c0 + 512], in_=mtmp_[k][0:1, :]),
                      reads=[r_mtmp_[k]], dma=True)
            else:
                mi = {0: 0, 1: 1, 3: 2, 4: 3}[kind]
                for j in range(4):
                    col = mi * 16 + (cb % 4) * 4 + j
                    S.add("dve", lambda e, j=j: e.tensor_tensor(out=junk_[:], in0=mtmp_[k][:, j * 128:(j + 1) * 128], in1=identf[:], op=ALU.mult),
                          reads=[r_mtmp_[k], r_identf], writes=[r_junk_])
                    S.add("dve", lambda e, col=col: e.reduce_sum(out=modc[:, col:col + 1], in_=junk_[:], axis=AX.X),
                          reads=[r_junk_], writes=[r_modc[mi]])

        MB0 = make_modbufs(p0, "a", None)
        if LITE:
            S.add("pool", lambda e: e.memset(modc[:], 1.0), writes=r_modc)
        else:
            for cb in range(8):
                mod_load(cb, MB0)
                mod_compute(cb, MB0)

        for i, t in enumerate((lq1, lk1, lq2, lk2)):
            S.add("sp", lambda e, i=i, t=t: e.dma_start(out=lam_t[:, i, :], in_=t[0:1, :].partition_broadcast(128)),
                  writes=[r_lamt], dma=True)
        S.barrier()
        S.add("dve", lambda e: e.tensor_tensor(out=lam_t[:, 0, :], in0=lam_t[:, 0, :], in1=lam_t[:, 1, :], op=ALU.mult),
              reads=[r_lamt], writes=[r_lamt])
        S.add("dve", lambda e: e.tensor_tensor(out=lam_t[:, 2, :], in0=lam_t[:, 2, :], in1=lam_t[:, 3, :], op=ALU.mult),
              reads=[r_lamt], writes=[r_lamt])
        S.add("dve", lambda e: e.reduce_sum(out=lam_s[:, 0:1], in_=lam_t[:, 0, :], axis=AX.X), reads=[r_lamt], writes=[r_lams])
        S.add("dve", lambda e: e.reduce_sum(out=lam_s[:, 1:2], in_=lam_t[:, 2, :], axis=AX.X), reads=[r_lamt], writes=[r_lams])
        S.add("act", lambda e: e.activation(out=lam_s[:, 2:4], in_=lam_s[:, 0:2], func=AF.Exp), reads=[r_lams], writes=[r_lams])
        S.add("dve", lambda e: e.scalar_tensor_tensor(out=neglam[:], in0=lam_s[:, 3:4], scalar=-LAMBDA_INIT, in1=lam_s[:, 2:3],
                                                      op0=ALU.add, op1=ALU.subtract),
              reads=[r_lams], writes=[r_neglam])
        S.add("sp", lambda e: e.dma_start(out=sg08[:], in_=subln[0:1, :].partition_broadcast(128)), writes=[r_sg08], dma=True)
        S.barrier()
        S.add("dve", lambda e: e.tensor_scalar(out=sg08[:], in0=sg08[:], scalar1=1.0 - LAMBDA_INIT, scalar2=None, op0=ALU.mult),
              reads=[r_sg08], writes=[r_sg08])
        for g in range(8):
            k = g % 2
            S.add("pool", lambda e, g=g, k=k: e.dma_start(out=wst[k][:], in_=gws[g, :, :]), writes=[r_wst[k]], dma=True)
            S.add("pe", lambda e, k=k: e.transpose(out=bview(T0)[:, k * 128:(k + 1) * 128], in_=wst[k][:], identity=ident[:]),
                  reads=[r_wst[k], r_ident], writes=[r_bank[T0]])
            S.add("dve", lambda e, g=g, k=k: e.tensor_copy(out=wmT[:, g, :], in_=bview(T0)[:, k * 128:(k + 1) * 128]),
                  reads=[r_bank[T0]], writes=[r_wmT])
        S.add("pool", lambda e: e.memset(wmT[64:128, :, 0:64], 0.0), reads=[r_wmT], writes=[r_wmT])
        S.add("sp", lambda e: e.dma_start(out=bs_f[:], in_=gbs[0:1, :]), writes=[r_bsf], dma=True)
        S.add("dve", lambda e: e.tensor_copy(out=bs_hi[:], in_=bs_f[:]), reads=[r_bsf], writes=[r_bs])
        S.add("dve", lambda e: e.tensor_copy(out=bs_f2[:], in_=bs_hi[:]), reads=[r_bs], writes=[r_bsf2])
        S.add("dve", lambda e: e.tensor_tensor(out=bs_f2[:], in0=bs_f[:], in1=bs_f2[:], op=ALU.subtract), reads=[r_bsf, r_bsf2], writes=[r_bsf2])
        S.add("dve", lambda e: e.tensor_copy(out=bs_lo[:], in_=bs_f2[:]), reads=[r_bsf2], writes=[r_bs])
        if debug:
            dbg_modc = nc.dram_tensor("dbg_modc", [128, 64], F32, kind="ExternalOutput").ap()
            S.add("sp", lambda e: e.dma_start(out=dbg_modc[:, :], in_=modc[:]), reads=r_modc, dma=True)
            dbg_id = nc.dram_tensor("dbg_id", [128, 128], F32, kind="ExternalOutput").ap()
            S.add("sp", lambda e: e.dma_start(out=dbg_id[:, :], in_=identf[:]), reads=[r_identf], dma=True)
        S.barrier()
        p0.close()

        def transpose_group(xn, r_xn, dst, r_dst, sc_col0, bi_col0):
            for kcp in range(8):
                tb = T0 if kcp % 2 == 0 else T1

                def f_tr(e, kcp=kcp, tb=tb):
                    ins = None
                    for k2 in range(2):
                        kc = kcp * 2 + k2
                        for i in range(4):
                            ins = e.transpose(out=bview(tb)[:, k2 * 512 + i * 128: k2 * 512 + (i + 1) * 128],
                                              in_=xn[:, i, kc * 128:(kc + 1) * 128], identity=ident[:])
                    return ins
                S.add("pe", f_tr, reads=list(r_xn) + [r_ident], writes=[r_bank[tb]])
                for k2 in range(2):
                    kc = kcp * 2 + k2
                    src = bview(tb)[:, k2 * 512:(k2 + 1) * 512]
                    scl = modc[:, sc_col0 + kc: sc_col0 + kc + 1]
                    bia = modc[:, bi_col0 + kc: bi_col0 + kc + 1]
                    import os
                    TRM = os.environ.get("DEV_TR", "")
                    if TRM == "noevac":
                        continue
                    if TRM == "act":
                        S.add("act", lambda e, kc=kc, src=src, scl=scl, bia=bia: e.activation(out=dst[:, kc, :], in_=src, func=AF.Identity, bias=bia, scale=scl),
                              reads=[r_bank[tb]] + r_modc, writes=[r_dst[kc]])
                    else:
                        S.add("dve", lambda e, kc=kc, src=src, scl=scl, bia=bia: e.tensor_scalar(out=dst[:, kc, :], in0=src, scalar1=scl, scalar2=bia, op0=ALU.mult, op1=ALU.add),
                              reads=[r_bank[tb]] + r_modc, writes=[r_dst[kc]])

        arot = [0]

        def next_A():
            b = A[arot[0] % 4]
            arot[0] += 1
            return b

        if stop_after >= 1:
            p1 = es.enter_context(ExitStack())
            wk = sbuf(p1, "wk", [128, 16, 1024], BF16)
            wv = sbuf(p1, "wv", [128, 16, 1024], BF16)
            xt = [sbuf(p1, f"xt{i}", [128, 2048], F32) for i in range(2)]
            xn = [sbuf(p1, f"xn{i}", [128, 4, 2048], BF16) for i in range(2)]
            ain = [sbuf(p1, f"ain{i}", [128, 16, 512], BF16) for i in range(2)]
            kst = [sbuf(p1, f"kst{i}", [128, 8, 512], BF16) for i in range(2)]
            vst = [sbuf(p1, f"vst{i}", [128, 8, 132], BF16) for i in range(2)]
            r_wk2, r_wv2 = RL(2), RL(2)
            r_xt = RL(2)
            r_xn = [RL(4), RL(4)]
            r_ain = [RL(16), RL(16)]
            r_kst = [RL(8), RL(8)]
            r_vst = [RL(3), RL(3)]
            for hb in range(2):
                S.add("pool", lambda e, hb=hb: e.dma_start(out=wk[:, :, hb * 512:(hb + 1) * 512],
                                                           in_=w_in[:, 1024 + hb * 512:1024 + (hb + 1) * 512].rearrange("(kc p) n -> p kc n", p=128)),
                      writes=[r_wk2[hb]], dma=True)
            for hb in range(2):
                S.add("pool", lambda e, hb=hb: e.dma_start(out=wv[:, :, hb * 512:(hb + 1) * 512],
                                                           in_=w_in[:, 2048 + hb * 512:2048 + (hb + 1) * 512].rearrange("(kc p) n -> p kc n", p=128)),
                      writes=[r_wv2[hb]], dma=True)
            for pi_, col0_ in enumerate((0, 512, 4096, 4608, 3072, 3584)):
                S.add("pool", lambda e, pi_=pi_, col0_=col0_: e.dma_start(out=wis[pi_, :, :].rearrange("p (k n) -> p k n", k=16),
                                                                           in_=w_in[:, col0_:col0_ + 512].rearrange("(kc p) n -> p kc n", p=128)), dma=True)
            for k in range(2):
                S.add("pool", lambda e, k=k: e.memset(vst[k][:], 0.0), writes=r_vst[k])
            xcnt = [0]

            def ln_group(G):
                gb = G % 2
                for i in range(4):
                    b = xcnt[0] % 2
                    xcnt[0] += 1
                    row0 = (G * 4 + i) * 128
                    S.add("sp", lambda e, b=b, row0=row0: e.dma_start(out=xt[b][:], in_=xs[row0:row0 + 128, :]), writes=[r_xt[b]], dma=True)
                    rstd, nb, rr = ln_stats(lambda a, c, b=b: xt[b][:, a:c], [r_xt[b]])
                    S.add("act", lambda e, b=b, i=i, gb=gb, rstd=rstd, nb=nb: e.activation(out=xn[gb][:, i, :], in_=xt[b][:], func=AF.Identity, bias=nb, scale=rstd),
                          reads=[r_xt[b]] + rr, writes=[r_xn[gb][i]])

            def tr_group(G):
                gb = G % 2
                transpose_group(xn[gb], r_xn[gb], ain[gb], r_ain[gb], 16, 0)

            def k_group(G):
                gb = G % 2
                for h in range(8):
                    bk = next_A()

                    def f_mm(e, h=h, bk=bk):
                        ins = None
                        for kc in range(16):
                            ins = e.matmul(banks[bk][:], lhsT=wk[:, kc, h * 128:(h + 1) * 128], rhs=ain[gb][:, kc, :], start=(kc == 0), stop=(kc == 15))
                        return ins
                    S.add("pe", f_mm, reads=r_ain[gb] + r_wk2, writes=[r_bank[bk]])
                    if h % 2 == 0:
                        S.add("act", lambda e, h=h, bk=bk: e.copy(out=kst[gb][:, h, :], in_=banks[bk][:]), reads=[r_bank[bk]], writes=[r_kst[gb][h]])
                    else:
                        S.add("dve", lambda e, h=h, bk=bk: e.tensor_copy(out=kst[gb][:, h, :], in_=banks[bk][:]), reads=[r_bank[bk]], writes=[r_kst[gb][h]])
                S.add("pool", lambda e: e.dma_start(out=kscr.rearrange("h p t -> p h t")[:, :, G * 512:(G + 1) * 512], in_=kst[gb][:]),
                      reads=r_kst[gb], dma=True)

            def v_group(G):
                gb = G % 2
                for i in range(4):
                    pos = G * 4 + i
                    vb = pos % 2
                    for cbv in range(2):
                        bk = next_A()

                        def f_mm(e, i=i, cbv=cbv, bk=bk):
                            ins = None
                            for kc in range(16):
                                ins = e.matmul(banks[bk][:], lhsT=ain[gb][:, kc, i * 128:(i + 1) * 128], rhs=wv[:, kc, cbv * 512:(cbv + 1) * 512], start=(kc == 0), stop=(kc == 15))
                            return ins
                        S.add("pe", f_mm, reads=r_ain[gb] + r_wv2, writes=[r_bank[bk]])
                        src = banks[bk][:].rearrange("p (h c) -> p h c", h=4)
                        dstv = vst[vb][:, cbv * 4:(cbv + 1) * 4, 0:128]
                        vcol = validt[:, pos:pos + 1]
                        if False:
                            S.add("act", lambda e, src=src, dstv=dstv, vcol=vcol: e.activation(out=dstv, in_=src, func=AF.Copy, scale=vcol),
                                  reads=[r_bank[bk], r_valid], writes=[r_vst[vb][cbv]])
                        else:
                            S.add("dve", lambda e, src=src, dstv=dstv, vcol=vcol: e.tensor_scalar(out=dstv, in0=src, scalar1=vcol, scalar2=None, op0=ALU.mult),
                                  reads=[r_bank[bk], r_valid], writes=[r_vst[vb][cbv]])
                    S.add("pool", lambda e, vb=vb, pos=pos: e.tensor_copy(out=vst[vb][:, :, 128:129], in_=validt[:, pos:pos + 1].unsqueeze(2).to_broadcast([128, 8, 1])),
                          reads=[r_valid], writes=[r_vst[vb][2]])
                    S.add("pool", lambda e, vb=vb, pos=pos: e.dma_start(out=vscr.rearrange("h p (n c) -> p h n c", c=132)[:, :, pos, :], in_=vst[vb][:]),
                          reads=r_vst[vb], dma=True)

            def a_store(G):
                gb = G % 2
                og = G // 2
                S.add("pool", lambda e: e.dma_start(out=ascr[og, :, :], in_=ain[gb][:].rearrange("p k t -> p (k t)")), reads=r_ain[gb], dma=True)

            import os
            SK = os.environ.get("DEV_SKIP", "")
            NG = int(os.environ.get("DEV_NG", "8"))
            ln_group(0)
            if "t" not in SK:
                tr_group(0)
            for G in range(NG):
                if G + 1 < NG:
                    ln_group(G + 1)
                if "k" not in SK:
                    k_group(G)
                if G + 1 < NG and "t" not in SK:
                    tr_group(G + 1)
                if "v" not in SK:
                    v_group(G)
                if G % 2 == 1 and "a" not in SK:
                    a_store(G)
            S.barrier()
            p1.close()

        if stop_after >= 2:
            p2 = es.enter_context(ExitStack())
            wr = [sbuf(p2, f"wr{i}", [128, 16, 512], BF16) for i in range(3)]
            ag = [sbuf(p2, f"ag{i}", [128, 16, 512], BF16) for i in range(2)]
            vn = sbuf(p2, "vn", [128, 16, 1024], BF16)
            qst = [sbuf(p2, f"qst{i}", [128, 4, 512], BF16) for i in range(2)]
            ust = [sbuf(p2, f"ust{i}", [128, 4, 512], BF16) for i in range(2)]
            gst = [sbuf(p2, f"gst{i}", [128, 4, 512], BF16) for i in range(2)]
            gv = [sbuf(p2, f"gv{i}", [128, 512], F32) for i in range(4)]
            gz = [sbuf(p2, f"gz{i}", [128, 512], F32) for i in range(2)]
            r_wr, r_ag = RL(3), RL(2)
            r_vn = [RL(2) for _ in range(16)]
            r_qst, r_ust, r_gst = [RL(4), RL(4)], [RL(4), RL(4)], [RL(4), RL(4)]
            r_gv, r_gz = RL(4), RL(2)
            S.add("sp", lambda e: e.dma_start(out=arena[0][:, 0:1024], in_=glg[0:1, :].partition_broadcast(128)), writes=[r_arena[0]], dma=True)
            S.add("sp", lambda e: e.dma_start(out=arena[1][:, 0:1024], in_=glb[0:1, :].partition_broadcast(128)), writes=[r_arena[1]], dma=True)
            passes = [("q", 0, 0), ("q", 1, 512), ("vg", 0, 4096), ("vg", 1, 4608), ("u", 0, 3072), ("u", 1, 3584)]
            agc = [0]
            gvc = [0]
            stq = [0]
            for pi, (kind, hb, col0) in enumerate(passes):
                wi = pi % 3
                S.add("sp", lambda e, wi=wi, pi=pi: e.dma_start(out=wr[wi][:].rearrange("p k n -> p (k n)"), in_=wis[pi, :, :]),
                      writes=[r_wr[wi]], dma=True)
                for og in range(4):
                    ab = agc[0] % 2
                    agc[0] += 1
                    S.add("sp", lambda e, ab=ab, og=og: e.dma_start(out=ag[ab][:].rearrange("p k t -> p (k t)"), in_=ascr[og, :, :]), writes=[r_ag[ab]], dma=True)
                    if kind in ("q", "u"):
                        sbi = stq[0] % 2
                        stq[0] += 1
                        for j in range(4):
                            bk = next_A()

                            def f_mm(e, j=j, bk=bk, wi=wi, ab=ab):
                                ins = None
                                for kc in range(16):
                                    ins = e.matmul(banks[bk][:], lhsT=wr[wi][:, kc, j * 128:(j + 1) * 128], rhs=ag[ab][:, kc, :], start=(kc == 0), stop=(kc == 15))
                                return ins
                            S.add("pe", f_mm, reads=[r_wr[wi], r_ag[ab]], writes=[r_bank[bk]])
                            if kind == "q":
                                if j % 2 == 0:
                                    S.add("act", lambda e, j=j, bk=bk, sbi=sbi: e.copy(out=qst[sbi][:, j, :], in_=banks[bk][:]), reads=[r_bank[bk]], writes=[r_qst[sbi][j]])
                                else:
                                    S.add("dve", lambda e, j=j, bk=bk, sbi=sbi: e.tensor_copy(out=qst[sbi][:, j, :], in_=banks[bk][:]), reads=[r_bank[bk]], writes=[r_qst[sbi][j]])
                            else:
                                S.add("act", lambda e, j=j, bk=bk, sbi=sbi: e.activation(out=ust[sbi][:, j, :], in_=banks[bk][:], func=AF.Gelu_apprx_tanh),
                                      reads=[r_bank[bk]], writes=[r_ust[sbi][j]])
                        if kind == "q":
                            S.add("pool", lambda e, sbi=sbi, hb=hb, og=og: e.dma_start(out=qscr.rearrange("h p t -> p h t")[:, hb * 4:(hb + 1) * 4, og * 512:(og + 1) * 512], in_=qst[sbi][:]),
                                  reads=r_qst[sbi], dma=True)
                        else:
                            for j in range(4):
                                gg = hb * 4 + j
                                gb_ = B0 if j % 2 == 0 else B1

                                def f_sp(e, gg=gg, gb_=gb_, og=og):
                                    ins = None
                                    for i in range(4):
                                        o = banks[gb_][:, i * 128:(i + 1) * 128]
                                        e.matmul(o, lhsT=vn[:, og * 4 + i, gg * 128:(gg + 1) * 128], rhs=wmT[:, gg, :], start=True, stop=False)
                                        e.matmul(o, lhsT=ones_row[0:1, :], rhs=bs_hi[0:1, gg * 128:(gg + 1) * 128], start=False, stop=False)
                                        ins = e.matmul(o, lhsT=ones_row[0:1, :], rhs=bs_lo[0:1, gg * 128:(gg + 1) * 128], start=False, stop=True)
                                    return ins
                                S.add("pe", f_sp, reads=[r_vn[og * 4 + i][hb] for i in range(4)] + [r_wmT, r_bs, r_ones], writes=[r_bank[gb_]])
                                S.add("dve", lambda e, j=j, gb_=gb_, sbi=sbi: e.tensor_tensor(out=gst[sbi][:, j, :], in0=banks[gb_][:], in1=ust[sbi][:, j, :], op=ALU.mult),
                                      reads=[r_bank[gb_], r_ust[sbi][j]], writes=[r_gst[sbi][j]])
                            S.add("pool", lambda e, sbi=sbi, hb=hb, og=og: e.dma_start(out=catscr.rearrange("c p t -> p c t")[:, 8 + hb * 4:8 + (hb + 1) * 4, og * 512:(og + 1) * 512], in_=gst[sbi][:]),
                                  reads=r_gst[sbi], dma=True)
                    else:
                        for i in range(4):
                            bk = next_A()

                            def f_mm(e, i=i, bk=bk, wi=wi, ab=ab):
                                ins = None
                                for kc in range(16):
                                    ins = e.matmul(banks[bk][:], lhsT=ag[ab][:, kc, i * 128:(i + 1) * 128], rhs=wr[wi][:, kc, :], start=(kc == 0), stop=(kc == 15))
                                return ins
                            S.add("pe", f_mm, reads=[r_wr[wi], r_ag[ab]], writes=[r_bank[bk]])
                            S.add("act", lambda e, bk=bk, i=i: e.activation(out=gv[i][:], in_=banks[bk][:], func=AF.Gelu_apprx_tanh),
                                  reads=[r_bank[bk]], writes=[r_gv[i]])
                        for i in range(4):
                            ti = og * 4 + i
                            gi = i % 2
                            k = stc[0] % NST
                            stc[0] += 1
                            st, mv, sc = st_t[k], mv_t[k], sc_t[k]

                            for c in range(4):
                                S.add("dve", lambda e, c=c, i=i, st=st: e.bn_stats(out=st[:, c, :], in_=gv[i][:, c * 128:(c + 1) * 128]),
                                      reads=[r_gv[i]], writes=[r_st4[k][c]])
                            for c in range(4):
                                S.add("dve", lambda e, c=c, st=st, mv=mv: e.bn_aggr(out=mv[:, c, :], in_=st[:, c:c + 1, :]),
                                      reads=[r_st4[k][c]], writes=[r_mv4[k][c]])
                            S.add("act", lambda e, sc=sc, mv=mv: e.activation(out=sc[:, 0:4], in_=mv[:, :, 1], func=AF.Ln, bias=LN_EPS, scale=1.0),
                                  reads=r_mv4[k], writes=[r_sca[k]])
                            S.add("act", lambda e, sc=sc: e.activation(out=sc[:, 4:8], in_=sc[:, 0:4], func=AF.Exp, scale=-0.5),
                                  reads=[r_sca[k]], writes=[r_scb[k]])
                            S.add("dve", lambda e, gi=gi, i=i, mv=mv: e.tensor_tensor(out=gz[gi][:].rearrange("p (g c) -> p g c", g=4), in0=gv[i][:].rearrange("p (g c) -> p g c", g=4),
                                                                                        in1=mv[:, :, 0:1].to_broadcast([128, 4, 128]), op=ALU.subtract),
                                  reads=[r_gv[i]] + r_mv4[k], writes=[r_gz[gi]])
                            S.add("pool", lambda e, gi=gi, sc=sc: e.tensor_tensor(out=gz[gi][:].rearrange("p (g c) -> p g c", g=4), in0=gz[gi][:].rearrange("p (g c) -> p g c", g=4),
                                                                                    in1=sc[:, 4:8].unsqueeze(2).to_broadcast([128, 4, 128]), op=ALU.mult),
                                  reads=[r_gz[gi], r_scb[k]], writes=[r_gz[gi]])
                            S.add("dve", lambda e, gi=gi, hb=hb: e.tensor_tensor(out=gz[gi][:], in0=gz[gi][:], in1=arena[0][:, hb * 512:(hb + 1) * 512], op=ALU.mult),
                                  reads=[r_gz[gi], r_arena[0]], writes=[r_gz[gi]])
                            S.add("pool", lambda e, gi=gi, hb=hb, ti=ti: e.tensor_tensor(out=vn[:, ti, hb * 512:(hb + 1) * 512], in0=gz[gi][:], in1=arena[1][:, hb * 512:(hb + 1) * 512], op=ALU.add),
                                  reads=[r_gz[gi], r_arena[1]], writes=[r_vn[ti][hb]])
            S.barrier()
            p2.close()

        if stop_after >= 3:
            p3 = es.enter_context(ExitStack())
            kT = [sbuf(p3, f"kT{i}", [128, 4096], BF16) for i in range(2)]
            vv = [sbuf(p3, f"vv{i}", [128, 32, 132], BF16) for i in range(2)]
            qT = [sbuf(p3, f"qT{i}", [128, 2048], BF16) for i in range(2)]
            NPT = 4
            pt = [sbuf(p3, f"pt{i}", [128, 2, 512], BF16) for i in range(NPT)]
            acs = [sbuf(p3, f"acs{i}", [128, 8, 132], F32) for i in range(2)]
            t1 = [sbuf(p3, f"t1_{i}", [128, 128], F32) for i in range(2)]
            at = [sbuf(p3, f"at{i}", [128, 128], F32) for i in range(2)]
            jk = [sbuf(p3, f"jk{i}", [128, 128], F32) for i in range(2)]
            yb = [sbuf(p3, f"yb{i}", [128, 128], BF16) for i in range(4)]
            fs = [sbuf(p3, f"fs{i}", [128, 8], F32) for i in range(2)]
            cst = [sbuf(p3, f"cst{i}", [128, 512], BF16) for i in range(2)]
            r_kT, r_vv, r_qT = RL(2), RL(2), RL(2)
            r_pt = [RL(2) for _ in range(NPT)]
            r_acs = [RL(8), RL(8)]
            r_t1, r_at, r_jk, r_cst = RL(2), RL(2), RL(2), RL(2)
            r_yb = RL(4)
            r_fs = [RL(6) for _ in range(2)]
            r_acc = RL(8)
            acc_bank = [B0, B0, B0, B1, B1, B1, T1, T1]
            MB2 = make_modbufs(p3, "b", T0)

            def acc_ap(idx, ncol=132):
                o = (idx % 3) * 132
                return banks[acc_bank[idx]][:, o:o + ncol]

            mod_todo = list(range(8, 24)) if not LITE else []
            if mod_todo:
                mod_load(mod_todo[0], MB2)
                mod_load(mod_todo[1], MB2)
            conv_todo = []
            for hq in range(32):
                conv_todo.append(lambda e, hq=hq: e.dma_start(out=w1s[hq, :, :].rearrange("p (k n) -> p k n", k=16),
                                                               in_=w_ff1[:, hq * 256:(hq + 1) * 256].rearrange("(kc p) n -> p kc n", p=128)))
            for cb in range(4):
                for hq8 in range(8):
                    conv_todo.append(lambda e, cb=cb, hq8=hq8: e.dma_start(out=w2s[cb * 8 + hq8, :, :].rearrange("p (j n) -> p j n", j=8),
                                                                         in_=w_ff2[hq8 * 1024:(hq8 + 1) * 1024, cb * 512:(cb + 1) * 512].rearrange("(j p) n -> p j n", p=128)))
            conv_ops = []
            if os.environ.get("DEV_NOCONV", "") == "1":
                conv_todo = []

            def conv_step(nmax):
                for _ in range(nmax):
                    if conv_todo:
                        conv_ops.append(S.add("pool", conv_todo.pop(0), dma=True))

            ptc = [0]
            fcnt = [0]
            cstc = [0]
            ybc = [0]
            pending = []

            def load_head(h):
                hb2 = h % 2
                S.add("sp", lambda e: e.dma_start(out=kT[hb2][:], in_=kscr[h, :, :]), writes=[r_kT[hb2]], dma=True)
                S.add("sp", lambda e: e.dma_start(out=vv[hb2][:].rearrange("p n c -> p (n c)"), in_=vscr[h, :, :]), writes=[r_vv[hb2]], dma=True)
                S.add("sp", lambda e: e.dma_start(out=qT[hb2][:], in_=qscr[h, :, :]), writes=[r_qT[hb2]], dma=True)

            def rec_pv(hb2, pb, pos, tq0, n, diag, ip):
                def f_pv(e):
                    ins = None
                    started = set()
                    for tq in range(tq0, 4):
                        for c in range(2):
                            bkk = acc_bank[c * 4 + tq]
                            st_flag = (n == 0) and (bkk not in started)
                            started.add(bkk)
                            ins = e.matmul(acc_ap(c * 4 + tq), lhsT=pt[pb][:, c, tq * 128:(tq + 1) * 128], rhs=vv[hb2][:, pos, 0:132],
                                           start=st_flag, stop=(diag and ip == tq))
                    return ins
                S.add("pe", f_pv, reads=r_pt[pb] + [r_vv[hb2]], writes=[r_acc[c * 4 + tq] for tq in range(tq0, 4) for c in range(2)])

            def rec_finalize(h, g, ab_):
                for idx in range(8):
                    if False:
                        S.add("act", lambda e, idx=idx: e.copy(out=acs[ab_][:, idx, :], in_=acc_ap(idx)), reads=[r_acc[idx]], writes=[r_acs[ab_][idx]])
                    else:
                        S.add("dve", lambda e, idx=idx: e.tensor_copy(out=acs[ab_][:, idx, :], in_=acc_ap(idx)), reads=[r_acc[idx]], writes=[r_acs[ab_][idx]])
                ybs = []
                for tq in range(4):
                    f = fcnt[0] % 2
                    fcnt[0] += 1
                    y = ybc[0] % 4
                    ybc[0] += 1
                    ybs.append(y)
                    i1, i2 = tq, 4 + tq
                    a1, a2 = acs[ab_][:, i1, :], acs[ab_][:, i2, :]
                    ra1, ra2 = r_acs[ab_][i1], r_acs[ab_][i2]
                    S.add("dve", lambda e, f=f, a1=a1: e.reciprocal(out=fs[f][:, 0:1], in_=a1[:, 128:129]), reads=[ra1], writes=[r_fs[f][0]])
                    S.add("dve", lambda e, f=f, a2=a2: e.reciprocal(out=fs[f][:, 1:2], in_=a2[:, 128:129]), reads=[ra2], writes=[r_fs[f][1]])
                    S.add("dve", lambda e, f=f: e.tensor_tensor(out=fs[f][:, 2:3], in0=fs[f][:, 1:2], in1=neglam[:], op=ALU.mult), reads=[r_fs[f][1], r_neglam], writes=[r_fs[f][2]])
                    S.add("dve", lambda e, f=f, a1=a1: e.tensor_scalar(out=t1[f][:], in0=a1[:, 0:128], scalar1=fs[f][:, 0:1], scalar2=None, op0=ALU.mult),
                          reads=[ra1, r_fs[f][0]], writes=[r_t1[f]])
                    S.add("dve", lambda e, f=f, a2=a2: e.scalar_tensor_tensor(out=at[f][:], in0=a2[:, 0:128], scalar=fs[f][:, 2:3], in1=t1[f][:], op0=ALU.mult, op1=ALU.add),
                          reads=[ra2, r_fs[f][2], r_t1[f]], writes=[r_at[f]])
                    S.add("dve", lambda e, f=f: e.memset(fs[f][:, 3:4], 0.0), writes=[r_fs[f][3]])
                    S.add("act", lambda e, f=f: e.activation(out=jk[f][:], in_=at[f][:], func=AF.Square, accum_out=fs[f][:, 3:4]),
                          reads=[r_at[f], r_fs[f][3]], writes=[r_jk[f], r_fs[f][3]])
                    S.add("act", lambda e, f=f: e.activation(out=fs[f][:, 4:5], in_=fs[f][:, 3:4], func=AF.Ln, bias=LN_EPS, scale=1.0 / 128.0),
                          reads=[r_fs[f][3]], writes=[r_fs[f][4]])
                    S.add("act", lambda e, f=f: e.activation(out=fs[f][:, 5:6], in_=fs[f][:, 4:5], func=AF.Exp, scale=-0.5),
                          reads=[r_fs[f][4]], writes=[r_fs[f][5]])
                    S.add("dve", lambda e, f=f, y=y: e.scalar_tensor_tensor(out=yb[y][:], in0=at[f][:], scalar=fs[f][:, 5:6], in1=sg08[:], op0=ALU.mult, op1=ALU.mult),
                          reads=[r_at[f], r_fs[f][5], r_sg08], writes=[r_yb[y]])

                def tail():
                    for tq in range(4):
                        y = ybs[tq]
                        S.add("pe", lambda e, y=y, tq=tq: e.transpose(out=bview(T0)[:, tq * 128:(tq + 1) * 128], in_=yb[y][:], identity=ident[:]),
                              reads=[r_yb[y], r_ident], writes=[r_bank[T0]])
                    cb_ = cstc[0] % 2
                    cstc[0] += 1
                    S.add("dve", lambda e, cb_=cb_: e.tensor_copy(out=cst[cb_][:], in_=bview(T0)[:, 0:512]), reads=[r_bank[T0]], writes=[r_cst[cb_]])
                    S.add("sp", lambda e, cb_=cb_: e.dma_start(out=catscr[h, :, g * 512:(g + 1) * 512], in_=cst[cb_][:]), reads=[r_cst[cb_]], dma=True)
                    if mod_todo:
                        cbm = mod_todo.pop(0)
                        mod_compute(cbm, MB2)
                        if len(mod_todo) >= 2:
                            mod_load(mod_todo[1], MB2)
                    conv_step(2)
                pending.append(tail)

            load_head(0)
            gcount = 0
            for h in range(8):
                hb2 = h % 2
                if h + 1 < 8:
                    load_head(h + 1)
                for g in range(4):
                    keys = []
                    for Gk in range(2 * g):
                        for ip in range(4):
                            keys.append((Gk * 4 + ip, 0, False, ip))
                    for Gk in (2 * g, 2 * g + 1):
                        for ip in range(4):
                            keys.append((Gk * 4 + ip, ip * 128, Gk == 2 * g + 1, ip))
                    prev = None
                    for n, (pos, c0, diag, ip) in enumerate(keys):
                        pb = ptc[0] % NPT
                        ptc[0] += 1
                        for c in range(2):
                            bk = A[2 * (n % 2) + c]
                            S.add("pe", lambda e, c=c, bk=bk, pos=pos, c0=c0, hb2=hb2, g=g: e.matmul(banks[bk][:, c0:512], lhsT=kT[hb2][c * 64:(c + 1) * 64, pos * 128:(pos + 1) * 128],
                                                                                                  rhs=qT[hb2][c * 64:(c + 1) * 64, g * 512 + c0:(g + 1) * 512], start=True, stop=True),
                                  reads=[r_kT[hb2], r_qT[hb2]], writes=[r_bank[bk]])
                            S.add("act", lambda e, c=c, bk=bk, pb=pb, c0=c0: e.activation(out=pt[pb][:, c, c0:512], in_=banks[bk][:, c0:512], func=AF.Exp, scale=0.125),
                                  reads=[r_bank[bk]], writes=[r_pt[pb][c]])
                        if diag:
                            S.add("pool" if os.environ.get("DEV_P2B", "") == "1" else "dve", lambda e, pb=pb, c0=c0: e.memset(pt[pb][64:128, :, c0:c0 + 64], 0.0), reads=r_pt[pb], writes=r_pt[pb])
                        if prev is not None:
                            rec_pv(*prev)
                        prev = (hb2, pb, pos, c0 // 128, n, diag, ip)
                        if n == 2 and pending:
                            pending.pop(0)()
                    rec_pv(*prev)
                    rec_finalize(h, g, gcount % 2)
                    gcount += 1
            while pending:
                pending.pop(0)()
            while mod_todo:
                cbm = mod_todo.pop(0)
                mod_compute(cbm, MB2)
                if len(mod_todo) >= 2:
                    mod_load(mod_todo[1], MB2)
            conv_step(1000)
            S.barrier()
            p3.close()

        if stop_after >= 4:
            p4 = es.enter_context(ExitStack())
            wo = sbuf(p4, "wo", [128, 16, 2048], BF16)
            cg = [sbuf(p4, f"cg{i}", [128, 16, 512], BF16) for i in range(2)]
            x3t = [sbuf(p4, f"x3_{i}", [128, 2048], F32) for i in range(2)]
            yt = [sbuf(p4, f"y3_{i}", [128, 2048], F32) for i in range(2)]
            mn = [sbuf(p4, "mn0", [128, 4, 2048], BF16)] * 2
            mT = [sbuf(p4, "mT0", [128, 16, 512], BF16)] * 2
            r_wo = RL(4)
            r_cg, r_x3t = RL(2), RL(2)
            r_yt = [RL(4), RL(4)]
            r_mn = [RL(4)] * 2
            r_mT = [RL(16)] * 2
            for cb in range(4):
                S.add("pool", lambda e, cb=cb: e.dma_start(out=wo[:, :, cb * 512:(cb + 1) * 512], in_=w_out[:, cb * 512:(cb + 1) * 512].rearrange("(kc p) n -> p kc n", p=128)),
                      writes=[r_wo[cb]], dma=True)
            S.add("sp", lambda e: e.dma_start(out=arena[0][:], in_=modscr[0:1, :].partition_broadcast(128)), writes=[r_arena[0]], dma=True)
            S.add("sp", lambda e: e.dma_start(out=arena[1][:], in_=ln1g[0:1, :].partition_broadcast(128)), writes=[r_arena[1]], dma=True)
            S.add("sp", lambda e: e.dma_start(out=arena[2][:], in_=ln1b[0:1, :].partition_broadcast(128)), writes=[r_arena[2]], dma=True)
            tcnt = [0]
            for og in range(4):
                cb2 = og % 2
                S.add("sp", lambda e, og=og, cb2=cb2: e.dma_start(out=cg[cb2][:], in_=catscr.rearrange("c p t -> p c t")[:, :, og * 512:(og + 1) * 512]), writes=[r_cg[cb2]], dma=True)
                for i in range(4):
                    ti = og * 4 + i
                    b = tcnt[0] % 2
                    tcnt[0] += 1
                    row0 = ((2 * og + 1) * 4 + i) * 128
                    S.add("sp", lambda e, b=b, row0=row0: e.dma_start(out=x3t[b][:], in_=xs[row0:row0 + 128, :]), writes=[r_x3t[b]], dma=True)
                    for cb in range(4):
                        bk = next_A()

                        def f_mm(e, i=i, cb=cb, bk=bk, cb2=cb2):
                            ins = None
                            for kc in range(16):
                                ins = e.matmul(banks[bk][:], lhsT=cg[cb2][:, kc, i * 128:(i + 1) * 128], rhs=wo[:, kc, cb * 512:(cb + 1) * 512], start=(kc == 0), stop=(kc == 15))
                            return ins
                        S.add("pe", f_mm, reads=[r_cg[cb2], r_wo[cb]], writes=[r_bank[bk]])
                        sl = slice(cb * 512, (cb + 1) * 512)
                        S.add("dve", lambda e, b=b, bk=bk, sl=sl: e.tensor_tensor(out=yt[b][:, sl], in0=banks[bk][:], in1=arena[0][:, sl], op=ALU.mult),
                              reads=[r_bank[bk], r_arena[0]], writes=[r_yt[b][cb]])
                        S.add("dve", lambda e, b=b, sl=sl: e.scalar_tensor_tensor(out=yt[b][:, sl], in0=x3t[b][:, sl], scalar=ALPHA, in1=yt[b][:, sl], op0=ALU.mult, op1=ALU.add),
                              reads=[r_x3t[b], r_yt[b][cb]], writes=[r_yt[b][cb]])
                    rstd, nb, rr = ln_stats(lambda a, c, b=b: yt[b][:, a:c], r_yt[b])
                    S.add("act", lambda e, b=b, rstd=rstd, nb=nb: e.activation(out=yt[b][:], in_=yt[b][:], func=AF.Identity, bias=nb, scale=rstd),
                          reads=r_yt[b] + rr, writes=r_yt[b])
                    S.add("dve", lambda e, b=b: e.tensor_tensor(out=yt[b][:], in0=yt[b][:], in1=arena[1][:], op=ALU.mult), reads=r_yt[b] + [r_arena[1]], writes=r_yt[b])
                    S.add("dve", lambda e, b=b: e.tensor_tensor(out=yt[b][:], in0=yt[b][:], in1=arena[2][:], op=ALU.add), reads=r_yt[b] + [r_arena[2]], writes=r_yt[b])
                    S.add("pool", lambda e, b=b, ti=ti: e.dma_start(out=h1scr[ti * 128:(ti + 1) * 128, :], in_=yt[b][:]), reads=r_yt[b], dma=True)
                    rstd2, nb2, rr2 = ln_stats(lambda a, c, b=b: yt[b][:, a:c], r_yt[b])
                    S.add("act", lambda e, b=b, i=i, cb2=cb2, rstd2=rstd2, nb2=nb2: e.activation(out=mn[cb2][:, i, :], in_=yt[b][:], func=AF.Identity, bias=nb2, scale=rstd2),
                          reads=r_yt[b] + rr2, writes=[r_mn[cb2][i]])
                transpose_group(mn[cb2], r_mn[cb2], mT[cb2], r_mT[cb2], 48, 32)
                S.add("pool", lambda e, og=og, cb2=cb2: e.dma_start(out=mscr[og, :, :], in_=mT[cb2][:].rearrange("p k t -> p (k t)")), reads=r_mT[cb2], dma=True)
            S.barrier()
            p4.close()

        if stop_after >= 5:
            p5 = es.enter_context(ExitStack())
            hT = sbuf(p5, "hT", [128, 64, 512], BF16)
            mg = sbuf(p5, "mg", [128, 16, 512], BF16)
            w1r = [sbuf(p5, f"w1r{i}", [128, 16, 256], BF16) for i in range(3)]
            w2r = [sbuf(p5, f"w2r{i}", [128, 8, 512], BF16) for i in range(3)]
            yf = [sbuf(p5, f"yf{i}", [128, 2048], F32) for i in range(4)]
            h1p = [sbuf(p5, f"h1p{i}", [128, 512], F32) for i in range(2)]
            rl = [sbuf(p5, f"rl{i}", [128, 512], F32) for i in range(2)]
            r_hT = RL(64)
            r_mg = Res()
            r_w1r, r_w2r = RL(3), RL(3)
            r_yf = [RL(4) for _ in range(4)]
            r_h1p, r_rl = RL(2), RL(2)
            S.add("sp", lambda e: e.dma_start(out=arena[0][:], in_=modscr[1:2, :].partition_broadcast(128)), writes=[r_arena[0]], dma=True)
            S.add("sp", lambda e: e.dma_start(out=arena[1][:], in_=ln2g[0:1, :].partition_broadcast(128)), writes=[r_arena[1]], dma=True)
            S.add("sp", lambda e: e.dma_start(out=arena[2][:], in_=ln2b[0:1, :].partition_broadcast(128)), writes=[r_arena[2]], dma=True)
            w1c, w2c, rlc, hpc = [0], [0], [0], [0]
            pend4 = []
            for og in range(4):
                S.add("sp", lambda e, og=og: e.dma_start(out=mg[:].rearrange("p k t -> p (k t)"), in_=mscr[og, :, :]), writes=[r_mg], dma=True)
                for hq in range(32):
                    wi = w1c[0] % 3
                    w1c[0] += 1
                    S.add("sp", lambda e, wi=wi, hq=hq: e.dma_start(out=w1r[wi][:].rearrange("p k n -> p (k n)"), in_=w1s[hq, :, :]),
                          writes=[r_w1r[wi]], dma=True)
                    for j in range(2):
                        hc = hq * 2 + j
                        bk = next_A()

                        def f_mm(e, j=j, bk=bk, wi=wi):
                            ins = None
                            for kc in range(16):
                                ins = e.matmul(banks[bk][:], lhsT=w1r[wi][:, kc, j * 128:(j + 1) * 128], rhs=mg[:, kc, :], start=(kc == 0), stop=(kc == 15))
                            return ins
                        S.add("pe", f_mm, reads=[r_w1r[wi], r_mg], writes=[r_bank[bk]])
                        ri = rlc[0] % 2
                        rlc[0] += 1
                        S.add("act", lambda e, bk=bk, ri=ri: e.activation(out=rl[ri][:], in_=banks[bk][:], func=AF.Relu), reads=[r_bank[bk]], writes=[r_rl[ri]])
                        eng2 = "dve" if hc % 2 == 0 else "pool"
                        S.add(eng2, lambda e, ri=ri, hc=hc: e.tensor_tensor(out=hT[:, hc, :], in0=rl[ri][:], in1=rl[ri][:], op=ALU.mult), reads=[r_rl[ri]], writes=[r_hT[hc]])
                    if hq % 8 == 5 and pend4:
                        pend4.pop(0)()
                for cb in range(4):
                    bset = [B0, B1, T0, T1] if cb % 2 == 0 else A
                    sl = slice(cb * 512, (cb + 1) * 512)
                    for hq8 in range(8):
                        wi = w2c[0] % 3
                        w2c[0] += 1
                        S.add("sp", lambda e, wi=wi, hq8=hq8, cb=cb: e.dma_start(out=w2r[wi][:].rearrange("p j n -> p (j n)"), in_=w2s[cb * 8 + hq8, :, :]),
                              writes=[r_w2r[wi]], dma=True)

                        def f_mm(e, wi=wi, hq8=hq8, bset=bset):
                            ins = None
                            for j in range(8):
                                hc = hq8 * 8 + j
                                for i in range(4):
                                    ins = e.matmul(banks[bset[i]][:], lhsT=hT[:, hc, i * 128:(i + 1) * 128], rhs=w2r[wi][:, j, :], start=(hc == 0), stop=(hc == 63))
                            return ins
                        S.add("pe", f_mm, reads=[r_w2r[wi]] + r_hT[hq8 * 8:(hq8 + 1) * 8], writes=[r_bank[b] for b in bset])
                    for i in range(4):
                        ti = og * 4 + i
                        hp = hpc[0] % 2
                        hpc[0] += 1
                        S.add("pool", lambda e, hp=hp, ti=ti, sl=sl: e.dma_start(out=h1p[hp][:], in_=h1scr[ti * 128:(ti + 1) * 128, sl]), writes=[r_h1p[hp]], dma=True)
                        S.add("dve", lambda e, i=i, sl=sl, bset=bset: e.tensor_tensor(out=yf[i][:, sl], in0=banks[bset[i]][:], in1=arena[0][:, sl], op=ALU.mult),
                              reads=[r_bank[bset[i]], r_arena[0]], writes=[r_yf[i][cb]])
                        S.add("dve", lambda e, i=i, sl=sl, hp=hp: e.scalar_tensor_tensor(out=yf[i][:, sl], in0=h1p[hp][:], scalar=ALPHA, in1=yf[i][:, sl], op0=ALU.mult, op1=ALU.add),
                              reads=[r_h1p[hp], r_yf[i][cb]], writes=[r_yf[i][cb]])
                for i in range(4):
                    ti = og * 4 + i

                    def tail4(i=i, ti=ti):
                        rstd, nb, rr = ln_stats(lambda a, c, i=i: yf[i][:, a:c], r_yf[i])
                        S.add("act", lambda e, i=i, rstd=rstd, nb=nb: e.activation(out=yf[i][:], in_=yf[i][:], func=AF.Identity, bias=nb, scale=rstd),
                              reads=r_yf[i] + rr, writes=r_yf[i])
                        S.add("dve", lambda e, i=i: e.tensor_tensor(out=yf[i][:], in0=yf[i][:], in1=arena[1][:], op=ALU.mult), reads=r_yf[i] + [r_arena[1]], writes=r_yf[i])
                        S.add("pool", lambda e, i=i: e.tensor_tensor(out=yf[i][:], in0=yf[i][:], in1=arena[2][:], op=ALU.add), reads=r_yf[i] + [r_arena[2]], writes=r_yf[i])
                        S.wait_at_end(S.add("pool", lambda e, i=i, ti=ti: e.dma_start(out=out[ti * 128:(ti + 1) * 128, :], in_=yf[i][:]), reads=r_yf[i], dma=True))
                    pend4.append(tail4)
            while pend4:
                pend4.pop(0)()
            S.barrier()
            p5.close()

        S.barrier()
        S.emit()
    return nc


def _core_layout(x, c, core):
    b, j = core // 2, core % 2
    xb = x[b]
    if j == 0:
        loc = np.concatenate([np.zeros((128, 2048), np.float32), xb[:3968]], axis=0)
    else:
        loc = xb
    loc = loc.reshape(32, 128, 2048)
    order = []
    valid = np.ones((128, 32), np.float32)
    for G in range(8):
        for i in range(4):
            L = 8 * (G // 2) + 2 * i + (G % 2)
            order.append(L)
            if j == 0 and L == 0:
                valid[:, G * 4 + i] = 0.0
    xs = np.ascontiguousarray(loc[order].reshape(4096, 2048))
    ccol = np.ascontiguousarray(c[b].reshape(16, 128).T)
    return xs, valid, ccol


_NC_CACHE = {}


def kernel(x, c, w_ada, b_ada, w_in, lambda_q1, lambda_k1, lambda_q2, lambda_k2, subln_g,
           gmlp_ln_g, gmlp_ln_b, gmlp_ws, gmlp_bs, w_out, ln1_g, ln1_b, w_ff1, w_ff2, ln2_g, ln2_b):
    f = lambda a: np.ascontiguousarray(np.asarray(a, dtype=np.float32))
    x = f(x)
    c = f(c)
    shared = {
        "w_ada": f(w_ada)[0], "b_ada": f(b_ada).reshape(1, 12288), "w_in": f(w_in)[0],
        "lambda_q1": f(lambda_q1).reshape(1, 64), "lambda_k1": f(lambda_k1).reshape(1, 64),
        "lambda_q2": f(lambda_q2).reshape(1, 64), "lambda_k2": f(lambda_k2).reshape(1, 64),
        "subln_g": f(subln_g).reshape(1, 128),
        "gmlp_ln_g": f(gmlp_ln_g).reshape(1, 1024), "gmlp_ln_b": f(gmlp_ln_b).reshape(1, 1024),
        "gmlp_ws": f(gmlp_ws)[0], "gmlp_bs": f(gmlp_bs).reshape(1, 1024),
        "w_out": f(w_out)[0], "ln1_g": f(ln1_g).reshape(1, 2048), "ln1_b": f(ln1_b).reshape(1, 2048),
        "w_ff1": f(w_ff1)[0], "w_ff2": f(w_ff2)[0], "ln2_g": f(ln2_g).reshape(1, 2048), "ln2_b": f(ln2_b).reshape(1, 2048),
    }
    in_maps = []
    for core in range(8):
        xs, valid, ccol = _core_layout(x, c, core)
        m = dict(shared)
        m.update({"xs": xs, "valid": valid, "ccol": ccol})
        in_maps.append(m)
    if "nc" not in _NC_CACHE:
        _NC_CACHE["nc"] = build_program()
    nc = _NC_CACHE["nc"]
    res = run_bass_kernel_spmd(nc, in_maps, core_ids=list(range(8)))
    outp = np.empty((4, 4096, 2048), np.float32)
    for core in range(8):
        b, j = core // 2, core % 2
        o = np.asarray(res.results[core]["out"]).reshape(16, 128, 2048)
        ov = outp[b].reshape(32, 128, 2048)
        ov[j::2] = o
    return outp
```

```python
import numpy as np
from contextlib import ExitStack
import concourse.bass as bass
import concourse.mybir as mybir
from concourse.bass_utils import run_bass_kernel_spmd

F32 = mybir.dt.float32
BF16 = mybir.dt.bfloat16
AF = mybir.ActivationFunctionType
ALU = mybir.AluOpType
AX = mybir.AxisListType

ENGS = ("pe", "act", "dve", "pool", "sp")
ALPHA = float((2.0 * 1) ** 0.25)
LN_EPS = 1e-5
LAMBDA_INIT = 0.2


class Res:
    __slots__ = ("name", "w", "r")

    def __init__(self, name=""):
        self.name = name
        self.w = None
        self.r = []


def RL(n):
    return [Res() for _ in range(n)]


class Op:
    __slots__ = ("eng", "fn", "deps", "needs", "token", "dma", "slot")


class Sched:
    def __init__(self, nc, es, n_sp=24, n_pool=10):
        self.nc = nc
        self.ops = {e: [] for e in ENGS}
        self.esem = {e: es.enter_context(nc.semaphore("s_" + e)) for e in ENGS}
        self.dsem = {
            "sp": [es.enter_context(nc.semaphore(f"d_sp{i}")) for i in range(n_sp)],
            "pool": [es.enter_context(nc.semaphore(f"d_pl{i}")) for i in range(n_pool)],
        }
        self.dcnt = {k: 0 for k in self.dsem}
        self.dlast = {k: [None] * len(v) for k, v in self.dsem.items()}
        self.final_waits = []

    def add(self, eng, fn, reads=(), writes=(), dma=False, extra=()):
        op = Op()
        op.eng = eng
        op.fn = fn
        op.needs = False
        op.dma = dma
        op.token = None
        op.slot = None
        deps = []
        for r in reads:
            if r.w is not None:
                deps.append((r.w, 0))
        for w in writes:
            if w.w is not None:
                deps.append((w.w, 1))
            for rr in w.r:
                deps.append((rr, 2))
        for d in extra:
            deps.append((d, 0))
        if dma:
            n = self.dcnt[eng]
            ns = len(self.dsem[eng])
            slot = n % ns
            prev = self.dlast[eng][slot]
            if prev is not None:
                deps.append((prev, 3))
            op.slot = (slot, 16 * (n // ns + 1))
            self.dlast[eng][slot] = op
            self.dcnt[eng] = n + 1
        final = []
        seen = set()
        for d, kind in deps:
            if d is op or id(d) in seen:
                continue
            if d.eng == eng and not d.dma and not dma:
                if eng == "pe":
                    continue
            seen.add(id(d))
            final.append(d)
            d.needs = True
        op.deps = final
        for r in reads:
            r.r.append(op)
        for w in writes:
            w.w = op
            w.r = []
        self.ops[eng].append(op)
        return op

    def barrier(self):
        lasts = [self.ops[e][-1] for e in ENGS
                 if self.ops[e] and not self.ops[e][-1].dma and self.ops[e][-1].fn is not None]
        dmas = [op for k in self.dlast for op in self.dlast[k] if op is not None]
        for e in ("pe", "act", "dve", "pool", "sp"):
            self.add(e, None, extra=lasts + dmas)

    def wait_at_end(self, op):
        op.needs = True
        self.final_waits.append(op)

    def emit(self):
        nc = self.nc
        for e in ENGS:
            c = 0
            for op in self.ops[e]:
                if op.dma:
                    op.token = (self.dsem[e][op.slot[0]], op.slot[1])
                elif op.needs:
                    c += 1
                    op.token = (self.esem[e], c)

        def run(e, eng, extra=()):
            waited = {}
            for op in self.ops[e]:
                w = {}
                for d in op.deps:
                    s, v = d.token
                    k = id(s)
                    if waited.get(k, 0) >= v:
                        continue
                    if k not in w or w[k][1] < v:
                        w[k] = (s, v)
                for k, (s, v) in w.items():
                    eng.wait_ge(s, v)
                    waited[k] = v
                if op.fn is None:
                    assert not op.needs
                    continue
                ins = op.fn(eng)
                if op.dma:
                    ins.then_inc(op.token[0], 16)
                elif op.needs:
                    ins.then_inc(op.token[0], 1)
            for op in extra:
                s, v = op.token
                if waited.get(id(s), 0) < v:
                    eng.wait_ge(s, v)
                    waited[id(s)] = v

        with nc.Block() as block:
            @block.tensor
            def _(eng):
                run("pe", eng)

            @block.scalar
            def _(eng):
                run("act", eng)

            @block.vector
            def _(eng):
                run("dve", eng)

            @block.gpsimd
            def _(eng):
                run("pool", eng)

            @block.sync
            def _(eng):
                run("sp", eng, self.final_waits)


def build_program(debug=False, stop_after=99):
    import os
    LITE = os.environ.get("DEV_LITE", "") == "1"
    nc = bass.Bass("TRN2", target_bir_lowering=False)

    def din(name, shape, dt=F32):
        if LITE and name in ("w_ada",):
            shape = [128, 128]
        return nc.dram_tensor(name, shape, dt, kind="ExternalInput").ap()

    def dscr(name, shape, dt=BF16):
        kind = "ExternalOutput" if debug else "Internal"
        return nc.dram_tensor(name, shape, dt, kind=kind).ap()

    xs = din("xs", [4096, 2048])
    valid = din("valid", [128, 32])
    ccol = din("ccol", [128, 16])
    w_ada = din("w_ada", [2048, 12288])
    b_ada = din("b_ada", [1, 12288])
    w_in = din("w_in", [2048, 5120])
    lq1 = din("lambda_q1", [1, 64])
    lk1 = din("lambda_k1", [1, 64])
    lq2 = din("lambda_q2", [1, 64])
    lk2 = din("lambda_k2", [1, 64])
    subln = din("subln_g", [1, 128])
    glg = din("gmlp_ln_g", [1, 1024])
    glb = din("gmlp_ln_b", [1, 1024])
    gws = din("gmlp_ws", [8, 128, 128])
    gbs = din("gmlp_bs", [1, 1024])
    w_out = din("w_out", [2048, 2048])
    ln1g = din("ln1_g", [1, 2048])
    ln1b = din("ln1_b", [1, 2048])
    w_ff1 = din("w_ff1", [2048, 8192])
    w_ff2 = din("w_ff2", [8192, 2048])
    ln2g = din("ln2_g", [1, 2048])
    ln2b = din("ln2_b", [1, 2048])
    out = nc.dram_tensor("out", [2048, 2048], F32, kind="ExternalOutput").ap()

    kscr = dscr("kscr", [8, 128, 4096])
    vscr = dscr("vscr", [8, 128, 32 * 132])
    ascr = dscr("ascr", [4, 128, 16 * 512])
    qscr = dscr("qscr", [8, 128, 2048])
    catscr = dscr("catscr", [16, 128, 2048])
    mscr = dscr("mscr", [4, 128, 16 * 512])
    h1scr = dscr("h1scr", [2048, 2048], F32)
    modscr = dscr("modscr", [2, 2048], F32)
    w1s = nc.dram_tensor("w1s", [32, 128, 16 * 256], BF16, kind="Internal").ap()
    wis = nc.dram_tensor("wis", [6, 128, 16 * 512], BF16, kind="Internal").ap()
    w2s = nc.dram_tensor("w2s", [32, 128, 8 * 512], BF16, kind="Internal").ap()

    with ExitStack() as es:
        S = Sched(nc, es)

        def sbuf(stack, name, shape, dt):
            return stack.enter_context(nc.sbuf_tensor(name, shape, dt))

        identf = sbuf(es, "identf", [128, 128], F32)
        ident = sbuf(es, "ident", [128, 128], BF16)
        modc = sbuf(es, "modc", [128, 64], F32)
        validt = sbuf(es, "validt", [128, 32], F32)
        neglam = sbuf(es, "neglam", [128, 1], F32)
        sg08 = sbuf(es, "sg08", [128, 128], F32)
        wmT = sbuf(es, "wmT", [128, 8, 128], BF16)
        bs_hi = sbuf(es, "bs_hi", [1, 1024], BF16)
        bs_lo = sbuf(es, "bs_lo", [1, 1024], BF16)
        ones_row = sbuf(es, "ones_row", [1, 128], BF16)
        arena = [sbuf(es, f"arena{i}", [128, 2048], F32) for i in range(3)]
        r_arena = RL(3)
        r_identf, r_ident, r_valid, r_neglam, r_sg08, r_wmT, r_bs, r_ones = RL(8)
        r_modc = RL(4)
        banks = [es.enter_context(nc.psum_tensor(f"bank{i}", [128, 512], F32)) for i in range(8)]
        r_bank = RL(8)
        A = [0, 1, 2, 3]
        B0, B1, T0, T1 = 4, 5, 6, 7

        def bview(b):
            return banks[b][:].bitcast(BF16)

        NST = 4
        st_t = [sbuf(es, f"st{i}", [128, 4, 6], F32) for i in range(NST)]
        mv_t = [sbuf(es, f"mv{i}", [128, 4, 2], F32) for i in range(NST)]
        sc_t = [sbuf(es, f"sc{i}", [128, 12], F32) for i in range(NST)]
        r_st, r_mv, r_sca, r_scb, r_scc = RL(NST), RL(NST), RL(NST), RL(NST), RL(NST)
        r_st4 = [RL(4) for _ in range(NST)]
        r_mv4 = [RL(4) for _ in range(NST)]
        stc = [0]

        def ln_stats(src_ap_fn, r_src):
            k = stc[0] % NST
            stc[0] += 1
            st, mv, sc = st_t[k], mv_t[k], sc_t[k]

            for c in range(4):
                S.add("dve", lambda e, c=c: e.bn_stats(out=st[:, c, :], in_=src_ap_fn(c * 512, (c + 1) * 512)), reads=r_src, writes=[r_st4[k][c]])
            S.add("dve", lambda e: e.bn_aggr(out=mv[:, 0, :], in_=st[:]), reads=r_st4[k], writes=[r_mv[k]])
            S.add("act", lambda e: e.activation(out=sc[:, 0:1], in_=mv[:, 0, 1:2], func=AF.Ln, bias=LN_EPS, scale=1.0),
                  reads=[r_mv[k]], writes=[r_sca[k]])
            S.add("act", lambda e: e.activation(out=sc[:, 1:2], in_=sc[:, 0:1], func=AF.Exp, scale=-0.5),
                  reads=[r_sca[k]], writes=[r_scb[k]])
            S.add("dve", lambda e: e.tensor_scalar(out=sc[:, 2:3], in0=mv[:, 0, 0:1], scalar1=-1.0, scalar2=sc[:, 1:2],
                                                   op0=ALU.mult, op1=ALU.mult),
                  reads=[r_mv[k], r_scb[k]], writes=[r_scc[k]])
            return sc[:, 1:2], sc[:, 2:3], [r_scb[k], r_scc[k]]

        S.add("pool", lambda e: e.memset(identf[:], 0.0), writes=[r_identf])
        S.add("pool", lambda e: e.affine_select(out=identf[:], in_=identf[:], pattern=[[-1, 128]],
                                                compare_op=ALU.not_equal, fill=1.0, base=0, channel_multiplier=1),
              reads=[r_identf], writes=[r_identf])
        S.add("dve", lambda e: e.tensor_copy(out=ident[:], in_=identf[:]), reads=[r_identf], writes=[r_ident])
        S.add("pool", lambda e: e.memset(ones_row[:], 1.0), writes=[r_ones])
        S.add("sp", lambda e: e.dma_start(out=validt[:], in_=valid[:, :]), writes=[r_valid], dma=True)

        cbc = sbuf(es, "cbc", [128, 16, 128], BF16)
        p0 = es.enter_context(ExitStack())
        c_sb = sbuf(p0, "c_sb", [128, 16], F32)
        c_act = sbuf(p0, "c_act", [128, 16], F32)
        lam_t = sbuf(p0, "lam_t", [128, 4, 64], F32)
        lam_s = sbuf(p0, "lam_s", [128, 8], F32)
        wst = [sbuf(p0, f"wst{i}", [128, 128], BF16) for i in range(2)]
        bs_f = sbuf(p0, "bs_f", [1, 1024], F32)
        bs_f2 = sbuf(p0, "bs_f2", [1, 1024], F32)
        r_c, r_cact, r_cbc, r_junk, r_lamt, r_lams, r_bsf, r_bsf2 = RL(8)
        r_wt, r_bb, r_mtmp, r_wst = RL(2), RL(2), RL(2), RL(2)

        S.add("sp", lambda e: e.dma_start(out=c_sb[:], in_=ccol[:, :]), writes=[r_c], dma=True)
        S.add("act", lambda e: e.activation(out=c_act[:], in_=c_sb[:], func=AF.Silu), reads=[r_c], writes=[r_cact])
        S.add("dve", lambda e: e.tensor_copy(out=cbc[:], in_=c_act[:].unsqueeze(2).to_broadcast([128, 16, 128])),
              reads=[r_cact], writes=[r_cbc])

        def make_modbufs(stack, tag, bank):
            d = {}
            d["wt"] = [sbuf(stack, f"wt_ada{tag}{i}", [128, 16, 512], BF16) for i in range(2)]
            d["bb"] = [sbuf(stack, f"bb{tag}{i}", [128, 512], F32) for i in range(2)]
            d["mtmp"] = [sbuf(stack, f"mtmp{tag}{i}", [128, 512], F32) for i in range(2)]
            d["junk"] = sbuf(stack, f"junk{tag}", [128, 128], F32)
            d["r_wt"], d["r_bb"], d["r_mtmp"] = RL(2), RL(2), RL(2)
            d["r_junk"] = Res()
            d["bank"] = bank
            return d

        def mod_load(cb, MB):
            k = cb % 2
            wt_, bb_ = MB["wt"], MB["bb"]
            S.add("pool", lambda e: e.dma_start(out=wt_[k][:], in_=w_ada[:, cb * 512:(cb + 1) * 512].rearrange("(kc p) n -> p kc n", p=128)),
                  writes=[MB["r_wt"][k]], dma=True)
            S.add("sp", lambda e: e.dma_start(out=bb_[k][:], in_=b_ada[0:1, cb * 512:(cb + 1) * 512].partition_broadcast(128)),
                  writes=[MB["r_bb"][k]], dma=True)

        def mod_compute(cb, MB):
            k = cb % 2
            wt_, bb_, mtmp_, junk_ = MB["wt"], MB["bb"], MB["mtmp"], MB["junk"]
            r_wt_, r_bb_, r_mtmp_, r_junk_ = MB["r_wt"], MB["r_bb"], MB["r_mtmp"], MB["r_junk"]
            bk = MB["bank"] if MB["bank"] is not None else A[cb % 4]

            def f_mm(e):
                ins = None
                for kc in range(16):
                    ins = e.matmul(banks[bk][:], lhsT=cbc[:, kc, :], rhs=wt_[k][:, kc, :], start=(kc == 0), stop=(kc == 15))
                return ins
            S.add("pe", f_mm, reads=[r_cbc, r_wt_[k]], writes=[r_bank[bk]])
            kind = cb // 4
            plus1 = 1.0 if kind in (1, 2, 4, 5) else 0.0
            S.add("dve", lambda e: e.scalar_tensor_tensor(out=mtmp_[k][:], in0=banks[bk][:], scalar=plus1, in1=bb_[k][:],
                                                          op0=ALU.add, op1=ALU.add),
                  reads=[r_bank[bk], r_bb_[k]], writes=[r_mtmp_[k]])
            if kind in (2, 5):
                gi = 0 if kind == 2 else 1
                c0 = (cb % 4) * 512
                S.add("sp", lambda e: e.dma_start(out=modscr[gi:gi + 1, c0:c0 + 512], in_=mtmp_[k][0:1, :]),
                      reads=[r_mtmp_[k]], dma=True)
            else:
                mi = {0: 0, 1: 1, 3: 2, 4: 3}[kind]
                for j in range(4):
                    col = mi * 16 + (cb % 4) * 4 + j
                    S.add("dve", lambda e, j=j: e.tensor_tensor(out=junk_[:], in0=mtmp_[k][:, j * 128:(j + 1) * 128], in1=identf[:], op=ALU.mult),
                          reads=[r_mtmp_[k], r_identf], writes=[r_junk_])
                    S.add("dve", lambda e, col=col: e.reduce_sum(out=modc[:, col:col + 1], in_=junk_[:], axis=AX.X),
                          reads=[r_junk_], writes=[r_modc[mi]])

        MB0 = make_modbufs(p0, "a", None)
        if LITE:
            S.add("pool", lambda e: e.memset(modc[:], 1.0), writes=r_modc)
        else:
            for cb in range(8):
                mod_load(cb, MB0)
                mod_compute(cb, MB0)

        for i, t in enumerate((lq1, lk1, lq2, lk2)):
            S.add("sp", lambda e, i=i, t=t: e.dma_start(out=lam_t[:, i, :], in_=t[0:1, :].partition_broadcast(128)),
                  writes=[r_lamt], dma=True)
        S.barrier()
        S.add("dve", lambda e: e.tensor_tensor(out=lam_t[:, 0, :], in0=lam_t[:, 0, :], in1=lam_t[:, 1, :], op=ALU.mult),
              reads=[r_lamt], writes=[r_lamt])
        S.add("dve", lambda e: e.tensor_tensor(out=lam_t[:, 2, :], in0=lam_t[:, 2, :], in1=lam_t[:, 3, :], op=ALU.mult),
              reads=[r_lamt], writes=[r_lamt])
        S.add("dve", lambda e: e.reduce_sum(out=lam_s[:, 0:1], in_=lam_t[:, 0, :], axis=AX.X), reads=[r_lamt], writes=[r_lams])
        S.add("dve", lambda e: e.reduce_sum(out=lam_s[:, 1:2], in_=lam_t[:, 2, :], axis=AX.X), reads=[r_lamt], writes=[r_lams])
        S.add("act", lambda e: e.activation(out=lam_s[:, 2:4], in_=lam_s[:, 0:2], func=AF.Exp), reads=[r_lams], writes=[r_lams])
        S.add("dve", lambda e: e.scalar_tensor_tensor(out=neglam[:], in0=lam_s[:, 3:4], scalar=-LAMBDA_INIT, in1=lam_s[:, 2:3],
                                                      op0=ALU.add, op1=ALU.subtract),
              reads=[r_lams], writes=[r_neglam])
        S.add("sp", lambda e: e.dma_start(out=sg08[:], in_=subln[0:1, :].partition_broadcast(128)), writes=[r_sg08], dma=True)
        S.barrier()
        S.add("dve", lambda e: e.tensor_scalar(out=sg08[:], in0=sg08[:], scalar1=1.0 - LAMBDA_INIT, scalar2=None, op0=ALU.mult),
              reads=[r_sg08], writes=[r_sg08])
        for g in range(8):
            k = g % 2
            S.add("pool", lambda e, g=g, k=k: e.dma_start(out=wst[k][:], in_=gws[g, :, :]), writes=[r_wst[k]], dma=True)
            S.add("pe", lambda e, k=k: e.transpose(out=bview(T0)[:, k * 128:(k + 1) * 128], in_=wst[k][:], identity=ident[:]),
                  reads=[r_wst[k], r_ident], writes=[r_bank[T0]])
            S.add("dve", lambda e, g=g, k=k: e.tensor_copy(out=wmT[:, g, :], in_=bview(T0)[:, k * 128:(k + 1) * 128]),
                  reads=[r_bank[T0]], writes=[r_wmT])
        S.add("pool", lambda e: e.memset(wmT[64:128, :, 0:64], 0.0), reads=[r_wmT], writes=[r_wmT])
        S.add("sp", lambda e: e.dma_start(out=bs_f[:], in_=gbs[0:1, :]), writes=[r_bsf], dma=True)
        S.add("dve", lambda e: e.tensor_copy(out=bs_hi[:], in_=bs_f[:]), reads=[r_bsf], writes=[r_bs])
        S.add("dve", lambda e: e.tensor_copy(out=bs_f2[:], in_=bs_hi[:]), reads=[r_bs], writes=[r_bsf2])
        S.add("dve", lambda e: e.tensor_tensor(out=bs_f2[:], in0=bs_f[:], in1=bs_f2[:], op=ALU.subtract), reads=[r_bsf, r_bsf2], writes=[r_bsf2])
        S.add("dve", lambda e: e.tensor_copy(out=bs_lo[:], in_=bs_f2[:]), reads=[r_bsf2], writes=[r_bs])
        if debug:
            dbg_modc = nc.dram_tensor("dbg_modc", [128, 64], F32, kind="ExternalOutput").ap()
            S.add("sp", lambda e: e.dma_start(out=dbg_modc[:, :], in_=modc[:]), reads=r_modc, dma=True)
            dbg_id = nc.dram_tensor("dbg_id", [128, 128], F32, kind="ExternalOutput").ap()
            S.add("sp", lambda e: e.dma_start(out=dbg_id[:, :], in_=identf[:]), reads=[r_identf], dma=True)
        S.barrier()
        p0.close()

        def transpose_group(xn, r_xn, dst, r_dst, sc_col0, bi_col0):
            for kcp in range(8):
                tb = T0 if kcp % 2 == 0 else T1

                def f_tr(e, kcp=kcp, tb=tb):
                    ins = None
                    for k2 in range(2):
                        kc = kcp * 2 + k2
                        for i in range(4):
                            ins = e.transpose(out=bview(tb)[:, k2 * 512 + i * 128: k2 * 512 + (i + 1) * 128],
                                              in_=xn[:, i, kc * 128:(kc + 1) * 128], identity=ident[:])
                    return ins
                S.add("pe", f_tr, reads=list(r_xn) + [r_ident], writes=[r_bank[tb]])
                for k2 in range(2):
                    kc = kcp * 2 + k2
                    src = bview(tb)[:, k2 * 512:(k2 + 1) * 512]
                    scl = modc[:, sc_col0 + kc: sc_col0 + kc + 1]
                    bia = modc[:, bi_col0 + kc: bi_col0 + kc + 1]
                    import os
                    TRM = os.environ.get("DEV_TR", "")
                    if TRM == "noevac":
                        continue
                    if TRM == "act":
                        S.add("act", lambda e, kc=kc, src=src, scl=scl, bia=bia: e.activation(out=dst[:, kc, :], in_=src, func=AF.Identity, bias=bia, scale=scl),
                              reads=[r_bank[tb]] + r_modc, writes=[r_dst[kc]])
                    else:
                        S.add("dve", lambda e, kc=kc, src=src, scl=scl, bia=bia: e.tensor_scalar(out=dst[:, kc, :], in0=src, scalar1=scl, scalar2=bia, op0=ALU.mult, op1=ALU.add),
                              reads=[r_bank[tb]] + r_modc, writes=[r_dst[kc]])

        arot = [0]

        def next_A():
            b = A[arot[0] % 4]
            arot[0] += 1
            return b

        if stop_after >= 1:
            p1 = es.enter_context(ExitStack())
            wk = sbuf(p1, "wk", [128, 16, 1024], BF16)
            wv = sbuf(p1, "wv", [128, 16, 1024], BF16)
            xt = [sbuf(p1, f"xt{i}", [128, 2048], F32) for i in range(2)]
            xn = [sbuf(p1, f"xn{i}", [128, 4, 2048], BF16) for i in range(2)]
            ain = [sbuf(p1, f"ain{i}", [128, 16, 512], BF16) for i in range(2)]
            kst = [sbuf(p1, f"kst{i}", [128, 8, 512], BF16) for i in range(2)]
            vst = [sbuf(p1, f"vst{i}", [128, 8, 132], BF16) for i in range(2)]
            r_wk2, r_wv2 = RL(2), RL(2)
            r_xt = RL(2)
            r_xn = [RL(4), RL(4)]
            r_ain = [RL(16), RL(16)]
            r_kst = [RL(8), RL(8)]
            r_vst = [RL(3), RL(3)]
            for hb in range(2):
                S.add("pool", lambda e, hb=hb: e.dma_start(out=wk[:, :, hb * 512:(hb + 1) * 512],
                                                           in_=w_in[:, 1024 + hb * 512:1024 + (hb + 1) * 512].rearrange("(kc p) n -> p kc n", p=128)),
                      writes=[r_wk2[hb]], dma=True)
            for hb in range(2):
                S.add("pool", lambda e, hb=hb: e.dma_start(out=wv[:, :, hb * 512:(hb + 1) * 512],
                                                           in_=w_in[:, 2048 + hb * 512:2048 + (hb + 1) * 512].rearrange("(kc p) n -> p kc n", p=128)),
                      writes=[r_wv2[hb]], dma=True)
            wis_todo = list(enumerate((0, 512, 4096, 4608, 3072, 3584)))

            def wis_step(nmax):
                for _ in range(nmax):
                    if wis_todo:
                        pi_, col0_ = wis_todo.pop(0)
                        S.add("pool", lambda e, pi_=pi_, col0_=col0_: e.dma_start(out=wis[pi_, :, :].rearrange("p (k n) -> p k n", k=16),
                                                                                   in_=w_in[:, col0_:col0_ + 512].rearrange("(kc p) n -> p kc n", p=128)), dma=True)
            for k in range(2):
                S.add("pool", lambda e, k=k: e.memset(vst[k][:], 0.0), writes=r_vst[k])
            xcnt = [0]

            def ln_group(G):
                gb = G % 2
                for i in range(4):
                    b = xcnt[0] % 2
                    xcnt[0] += 1
                    row0 = (G * 4 + i) * 128
                    S.add("sp", lambda e, b=b, row0=row0: e.dma_start(out=xt[b][:], in_=xs[row0:row0 + 128, :]), writes=[r_xt[b]], dma=True)
                    rstd, nb, rr = ln_stats(lambda a, c, b=b: xt[b][:, a:c], [r_xt[b]])
                    S.add("act", lambda e, b=b, i=i, gb=gb, rstd=rstd, nb=nb: e.activation(out=xn[gb][:, i, :], in_=xt[b][:], func=AF.Identity, bias=nb, scale=rstd),
                          reads=[r_xt[b]] + rr, writes=[r_xn[gb][i]])

            def tr_group(G):
                gb = G % 2
                transpose_group(xn[gb], r_xn[gb], ain[gb], r_ain[gb], 16, 0)

            def k_group(G):
                gb = G % 2
                for h in range(8):
                    bk = next_A()

                    def f_mm(e, h=h, bk=bk):
                        ins = None
                        for kc in range(16):
                            ins = e.matmul(banks[bk][:], lhsT=wk[:, kc, h * 128:(h + 1) * 128], rhs=ain[gb][:, kc, :], start=(kc == 0), stop=(kc == 15))
                        return ins
                    S.add("pe", f_mm, reads=r_ain[gb] + r_wk2, writes=[r_bank[bk]])
                    if h % 2 == 0:
                        S.add("act", lambda e, h=h, bk=bk: e.copy(out=kst[gb][:, h, :], in_=banks[bk][:]), reads=[r_bank[bk]], writes=[r_kst[gb][h]])
                    else:
                        S.add("dve", lambda e, h=h, bk=bk: e.tensor_copy(out=kst[gb][:, h, :], in_=banks[bk][:]), reads=[r_bank[bk]], writes=[r_kst[gb][h]])
                S.add("pool", lambda e: e.dma_start(out=kscr.rearrange("h p t -> p h t")[:, :, G * 512:(G + 1) * 512], in_=kst[gb][:]),
                      reads=r_kst[gb], dma=True)

            def v_group(G):
                gb = G % 2
                for i in range(4):
                    pos = G * 4 + i
                    vb = pos % 2
                    for cbv in range(2):
                        bk = next_A()

                        def f_mm(e, i=i, cbv=cbv, bk=bk):
                            ins = None
                            for kc in range(16):
                                ins = e.matmul(banks[bk][:], lhsT=ain[gb][:, kc, i * 128:(i + 1) * 128], rhs=wv[:, kc, cbv * 512:(cbv + 1) * 512], start=(kc == 0), stop=(kc == 15))
                            return ins
                        S.add("pe", f_mm, reads=r_ain[gb] + r_wv2, writes=[r_bank[bk]])
                        src = banks[bk][:].rearrange("p (h c) -> p h c", h=4)
                        dstv = vst[vb][:, cbv * 4:(cbv + 1) * 4, 0:128]
                        vcol = validt[:, pos:pos + 1]
                        if False:
                            S.add("act", lambda e, src=src, dstv=dstv, vcol=vcol: e.activation(out=dstv, in_=src, func=AF.Copy, scale=vcol),
                                  reads=[r_bank[bk], r_valid], writes=[r_vst[vb][cbv]])
                        else:
                            S.add("dve", lambda e, src=src, dstv=dstv, vcol=vcol: e.tensor_scalar(out=dstv, in0=src, scalar1=vcol, scalar2=None, op0=ALU.mult),
                                  reads=[r_bank[bk], r_valid], writes=[r_vst[vb][cbv]])
                    S.add("pool", lambda e, vb=vb, pos=pos: e.tensor_copy(out=vst[vb][:, :, 128:129], in_=validt[:, pos:pos + 1].unsqueeze(2).to_broadcast([128, 8, 1])),
                          reads=[r_valid], writes=[r_vst[vb][2]])
                    S.add("pool", lambda e, vb=vb, pos=pos: e.dma_start(out=vscr.rearrange("h p (n c) -> p h n c", c=132)[:, :, pos, :], in_=vst[vb][:]),
                          reads=r_vst[vb], dma=True)

            def a_store(G):
                gb = G % 2
                og = G // 2
                S.add("pool", lambda e: e.dma_start(out=ascr[og, :, :], in_=ain[gb][:].rearrange("p k t -> p (k t)")), reads=r_ain[gb], dma=True)

            import os
            SK = os.environ.get("DEV_SKIP", "")
            NG = int(os.environ.get("DEV_NG", "8"))
            ln_group(0)
            if "t" not in SK:
                tr_group(0)
            for G in range(NG):
                if G + 1 < NG:
                    ln_group(G + 1)
                if G >= 2:
                    wis_step(1)
                if "k" not in SK:
                    k_group(G)
                if G + 1 < NG and "t" not in SK:
                    tr_group(G + 1)
                if "v" not in SK:
                    v_group(G)
                if G % 2 == 1 and "a" not in SK:
                    a_store(G)
            wis_step(100)
            S.barrier()
            p1.close()

        if stop_after >= 2:
            p2 = es.enter_context(ExitStack())
            wr = [sbuf(p2, f"wr{i}", [128, 16, 512], BF16) for i in range(3)]
            ag = [sbuf(p2, f"ag{i}", [128, 16, 512], BF16) for i in range(2)]
            vn = sbuf(p2, "vn", [128, 16, 1024], BF16)
            qst = [sbuf(p2, f"qst{i}", [128, 4, 512], BF16) for i in range(2)]
            ust = [sbuf(p2, f"ust{i}", [128, 4, 512], BF16) for i in range(2)]
            gst = [sbuf(p2, f"gst{i}", [128, 4, 512], BF16) for i in range(2)]
            gv = [sbuf(p2, f"gv{i}", [128, 512], F32) for i in range(2)]
            gz = [sbuf(p2, f"gz{i}", [128, 512], F32) for i in range(2)]
            r_wr, r_ag = RL(3), RL(2)
            r_vn = [RL(2) for _ in range(16)]
            r_qst, r_ust, r_gst = [RL(4), RL(4)], [RL(4), RL(4)], [RL(4), RL(4)]
            r_gv, r_gz = RL(2), RL(2)
            S.add("sp", lambda e: e.dma_start(out=arena[0][:, 0:1024], in_=glg[0:1, :].partition_broadcast(128)), writes=[r_arena[0]], dma=True)
            S.add("sp", lambda e: e.dma_start(out=arena[1][:, 0:1024], in_=glb[0:1, :].partition_broadcast(128)), writes=[r_arena[1]], dma=True)
            passes = [("q", 0, 0), ("q", 1, 512), ("vg", 0, 4096), ("vg", 1, 4608), ("u", 0, 3072), ("u", 1, 3584)]
            agc = [0]
            gvc = [0]
            stq = [0]
            for pi, (kind, hb, col0) in enumerate(passes):
                wi = pi % 3
                S.add("sp", lambda e, wi=wi, pi=pi: e.dma_start(out=wr[wi][:].rearrange("p k n -> p (k n)"), in_=wis[pi, :, :]),
                      writes=[r_wr[wi]], dma=True)
                for og in range(4):
                    ab = agc[0] % 2
                    agc[0] += 1
                    S.add("sp", lambda e, ab=ab, og=og: e.dma_start(out=ag[ab][:].rearrange("p k t -> p (k t)"), in_=ascr[og, :, :]), writes=[r_ag[ab]], dma=True)
                    if kind in ("q", "u"):
                        sbi = stq[0] % 2
                        stq[0] += 1
                        for j in range(4):
                            bk = next_A()

                            def f_mm(e, j=j, bk=bk, wi=wi, ab=ab):
                                ins = None
                                for kc in range(16):
                                    ins = e.matmul(banks[bk][:], lhsT=wr[wi][:, kc, j * 128:(j + 1) * 128], rhs=ag[ab][:, kc, :], start=(kc == 0), stop=(kc == 15))
                                return ins
                            S.add("pe", f_mm, reads=[r_wr[wi], r_ag[ab]], writes=[r_bank[bk]])
                            if kind == "q":
                                if j % 2 == 0:
                                    S.add("act", lambda e, j=j, bk=bk, sbi=sbi: e.copy(out=qst[sbi][:, j, :], in_=banks[bk][:]), reads=[r_bank[bk]], writes=[r_qst[sbi][j]])
                                else:
                                    S.add("dve", lambda e, j=j, bk=bk, sbi=sbi: e.tensor_copy(out=qst[sbi][:, j, :], in_=banks[bk][:]), reads=[r_bank[bk]], writes=[r_qst[sbi][j]])
                            else:
                                S.add("act", lambda e, j=j, bk=bk, sbi=sbi: e.activation(out=ust[sbi][:, j, :], in_=banks[bk][:], func=AF.Gelu_apprx_tanh),
                                      reads=[r_bank[bk]], writes=[r_ust[sbi][j]])
                        if kind == "q":
                            S.add("pool", lambda e, sbi=sbi, hb=hb, og=og: e.dma_start(out=qscr.rearrange("h p t -> p h t")[:, hb * 4:(hb + 1) * 4, og * 512:(og + 1) * 512], in_=qst[sbi][:]),
                                  reads=r_qst[sbi], dma=True)
                        else:
                            for j in range(4):
                                gg = hb * 4 + j
                                gb_ = B0 if j % 2 == 0 else B1

                                def f_sp(e, gg=gg, gb_=gb_, og=og):
                                    ins = None
                                    for i in range(4):
                                        o = banks[gb_][:, i * 128:(i + 1) * 128]
                                        e.matmul(o, lhsT=vn[:, og * 4 + i, gg * 128:(gg + 1) * 128], rhs=wmT[:, gg, :], start=True, stop=False)
                                        e.matmul(o, lhsT=ones_row[0:1, :], rhs=bs_hi[0:1, gg * 128:(gg + 1) * 128], start=False, stop=False)
                                        ins = e.matmul(o, lhsT=ones_row[0:1, :], rhs=bs_lo[0:1, gg * 128:(gg + 1) * 128], start=False, stop=True)
                                    return ins
                                S.add("pe", f_sp, reads=[r_vn[og * 4 + i][hb] for i in range(4)] + [r_wmT, r_bs, r_ones], writes=[r_bank[gb_]])
                                S.add("dve", lambda e, j=j, gb_=gb_, sbi=sbi: e.tensor_tensor(out=gst[sbi][:, j, :], in0=banks[gb_][:], in1=ust[sbi][:, j, :], op=ALU.mult),
                                      reads=[r_bank[gb_], r_ust[sbi][j]], writes=[r_gst[sbi][j]])
                            S.add("pool", lambda e, sbi=sbi, hb=hb, og=og: e.dma_start(out=catscr.rearrange("c p t -> p c t")[:, 8 + hb * 4:8 + (hb + 1) * 4, og * 512:(og + 1) * 512], in_=gst[sbi][:]),
                                  reads=r_gst[sbi], dma=True)
                    else:
                        for i in range(4):
                            ti = og * 4 + i
                            bk = next_A()

                            def f_mm(e, i=i, bk=bk, wi=wi, ab=ab):
                                ins = None
                                for kc in range(16):
                                    ins = e.matmul(banks[bk][:], lhsT=ag[ab][:, kc, i * 128:(i + 1) * 128], rhs=wr[wi][:, kc, :], start=(kc == 0), stop=(kc == 15))
                                return ins
                            S.add("pe", f_mm, reads=[r_wr[wi], r_ag[ab]], writes=[r_bank[bk]])
                            gi = gvc[0] % 2
                            gvc[0] += 1
                            S.add("act", lambda e, bk=bk, gi=gi: e.activation(out=gv[gi][:], in_=banks[bk][:], func=AF.Gelu_apprx_tanh),
                                  reads=[r_bank[bk]], writes=[r_gv[gi]])
                            k = stc[0] % NST
                            stc[0] += 1
                            st, mv, sc = st_t[k], mv_t[k], sc_t[k]

                            for c in range(4):
                                S.add("dve", lambda e, c=c, gi=gi, st=st: e.bn_stats(out=st[:, c, :], in_=gv[gi][:, c * 128:(c + 1) * 128]),
                                      reads=[r_gv[gi]], writes=[r_st4[k][c]])
                            for c in range(4):
                                S.add("dve", lambda e, c=c, st=st, mv=mv: e.bn_aggr(out=mv[:, c, :], in_=st[:, c:c + 1, :]),
                                      reads=[r_st4[k][c]], writes=[r_mv4[k][c]])
                            S.add("act", lambda e, sc=sc, mv=mv: e.activation(out=sc[:, 0:4], in_=mv[:, :, 1], func=AF.Ln, bias=LN_EPS, scale=1.0),
                                  reads=r_mv4[k], writes=[r_sca[k]])
                            S.add("act", lambda e, sc=sc: e.activation(out=sc[:, 4:8], in_=sc[:, 0:4], func=AF.Exp, scale=-0.5),
                                  reads=[r_sca[k]], writes=[r_scb[k]])
                            S.add("dve", lambda e, gi=gi, mv=mv: e.tensor_tensor(out=gz[gi][:].rearrange("p (g c) -> p g c", g=4), in0=gv[gi][:].rearrange("p (g c) -> p g c", g=4),
                                                                                   in1=mv[:, :, 0:1].to_broadcast([128, 4, 128]), op=ALU.subtract),
                                  reads=[r_gv[gi]] + r_mv4[k], writes=[r_gz[gi]])
                            S.add("pool", lambda e, gi=gi, sc=sc: e.tensor_tensor(out=gz[gi][:].rearrange("p (g c) -> p g c", g=4), in0=gz[gi][:].rearrange("p (g c) -> p g c", g=4),
                                                                                    in1=sc[:, 4:8].unsqueeze(2).to_broadcast([128, 4, 128]), op=ALU.mult),
                                  reads=[r_gz[gi], r_scb[k]], writes=[r_gz[gi]])
                            S.add("dve", lambda e, gi=gi, hb=hb: e.tensor_tensor(out=gz[gi][:], in0=gz[gi][:], in1=arena[0][:, hb * 512:(hb + 1) * 512], op=ALU.mult),
                                  reads=[r_gz[gi], r_arena[0]], writes=[r_gz[gi]])
                            S.add("pool", lambda e, gi=gi, hb=hb, ti=ti: e.tensor_tensor(out=vn[:, ti, hb * 512:(hb + 1) * 512], in0=gz[gi][:], in1=arena[1][:, hb * 512:(hb + 1) * 512], op=ALU.add),
                                  reads=[r_gz[gi], r_arena[1]], writes=[r_vn[ti][hb]])
            S.barrier()
            p2.close()

        if stop_after >= 3:
            p3 = es.enter_context(ExitStack())
            kT = [sbuf(p3, f"kT{i}", [128, 4096], BF16) for i in range(2)]
            vv = [sbuf(p3, f"vv{i}", [128, 32, 132], BF16) for i in range(2)]
            qT = [sbuf(p3, f"qT{i}", [128, 2048], BF16) for i in range(2)]
            NPT = 4
            pt = [sbuf(p3, f"pt{i}", [128, 2, 512], BF16) for i in range(NPT)]
            acs = [sbuf(p3, f"acs{i}", [128, 8, 132], F32) for i in range(2)]
            t1 = [sbuf(p3, f"t1_{i}", [128, 128], F32) for i in range(2)]
            at = [sbuf(p3, f"at{i}", [128, 128], F32) for i in range(2)]
            jk = [sbuf(p3, f"jk{i}", [128, 128], F32) for i in range(2)]
            yb = [sbuf(p3, f"yb{i}", [128, 128], BF16) for i in range(4)]
            fs = [sbuf(p3, f"fs{i}", [128, 8], F32) for i in range(2)]
            cst = [sbuf(p3, f"cst{i}", [128, 512], BF16) for i in range(2)]
            r_kT, r_vv, r_qT = RL(2), RL(2), RL(2)
            r_pt = [RL(2) for _ in range(NPT)]
            r_acs = [RL(8), RL(8)]
            r_t1, r_at, r_jk, r_cst = RL(2), RL(2), RL(2), RL(2)
            r_yb = RL(4)
            r_fs = [RL(6) for _ in range(2)]
            r_acc = RL(8)
            acc_bank = [B0, B0, B0, B1, B1, B1, T1, T1]
            MB2 = make_modbufs(p3, "b", T0)

            def acc_ap(idx, ncol=132):
                o = (idx % 3) * 132
                return banks[acc_bank[idx]][:, o:o + ncol]

            mod_todo = list(range(8, 24)) if not LITE else []
            if mod_todo:
                mod_load(mod_todo[0], MB2)
                mod_load(mod_todo[1], MB2)
            conv_todo = []
            for hq in range(32):
                conv_todo.append(lambda e, hq=hq: e.dma_start(out=w1s[hq, :, :].rearrange("p (k n) -> p k n", k=16),
                                                               in_=w_ff1[:, hq * 256:(hq + 1) * 256].rearrange("(kc p) n -> p kc n", p=128)))
            for cb in range(4):
                for hq8 in range(8):
                    conv_todo.append(lambda e, cb=cb, hq8=hq8: e.dma_start(out=w2s[cb * 8 + hq8, :, :].rearrange("p (j n) -> p j n", j=8),
                                                                         in_=w_ff2[hq8 * 1024:(hq8 + 1) * 1024, cb * 512:(cb + 1) * 512].rearrange("(j p) n -> p j n", p=128)))
            conv_ops = []
            if os.environ.get("DEV_NOCONV", "") == "1":
                conv_todo = []

            def conv_step(nmax):
                for _ in range(nmax):
                    if conv_todo:
                        conv_ops.append(S.add("pool", conv_todo.pop(0), dma=True))

            ptc = [0]
            fcnt = [0]
            cstc = [0]
            ybc = [0]
            pending = []

            def load_head(h):
                hb2 = h % 2
                S.add("sp", lambda e: e.dma_start(out=kT[hb2][:], in_=kscr[h, :, :]), writes=[r_kT[hb2]], dma=True)
                S.add("sp", lambda e: e.dma_start(out=vv[hb2][:].rearrange("p n c -> p (n c)"), in_=vscr[h, :, :]), writes=[r_vv[hb2]], dma=True)
                S.add("sp", lambda e: e.dma_start(out=qT[hb2][:], in_=qscr[h, :, :]), writes=[r_qT[hb2]], dma=True)

            def rec_pv(hb2, pb, pos, tq0, n, diag, ip):
                def f_pv(e):
                    ins = None
                    started = set()
                    for tq in range(tq0, 4):
                        for c in range(2):
                            bkk = acc_bank[c * 4 + tq]
                            st_flag = (n == 0) and (bkk not in started)
                            started.add(bkk)
                            ins = e.matmul(acc_ap(c * 4 + tq), lhsT=pt[pb][:, c, tq * 128:(tq + 1) * 128], rhs=vv[hb2][:, pos, 0:132],
                                           start=st_flag, stop=(diag and ip == tq))
                    return ins
                S.add("pe", f_pv, reads=r_pt[pb] + [r_vv[hb2]], writes=[r_acc[c * 4 + tq] for tq in range(tq0, 4) for c in range(2)])

            def rec_finalize(h, g, ab_):
                for idx in range(8):
                    if False:
                        S.add("act", lambda e, idx=idx: e.copy(out=acs[ab_][:, idx, :], in_=acc_ap(idx)), reads=[r_acc[idx]], writes=[r_acs[ab_][idx]])
                    else:
                        S.add("dve", lambda e, idx=idx: e.tensor_copy(out=acs[ab_][:, idx, :], in_=acc_ap(idx)), reads=[r_acc[idx]], writes=[r_acs[ab_][idx]])
                ybs = []
                for tq in range(4):
                    f = fcnt[0] % 2
                    fcnt[0] += 1
                    y = ybc[0] % 4
                    ybc[0] += 1
                    ybs.append(y)
                    i1, i2 = tq, 4 + tq
                    a1, a2 = acs[ab_][:, i1, :], acs[ab_][:, i2, :]
                    ra1, ra2 = r_acs[ab_][i1], r_acs[ab_][i2]
                    S.add("dve", lambda e, f=f, a1=a1: e.reciprocal(out=fs[f][:, 0:1], in_=a1[:, 128:129]), reads=[ra1], writes=[r_fs[f][0]])
                    S.add("dve", lambda e, f=f, a2=a2: e.reciprocal(out=fs[f][:, 1:2], in_=a2[:, 128:129]), reads=[ra2], writes=[r_fs[f][1]])
                    S.add("dve", lambda e, f=f: e.tensor_tensor(out=fs[f][:, 2:3], in0=fs[f][:, 1:2], in1=neglam[:], op=ALU.mult), reads=[r_fs[f][1], r_neglam], writes=[r_fs[f][2]])
                    S.add("dve", lambda e, f=f, a1=a1: e.tensor_scalar(out=t1[f][:], in0=a1[:, 0:128], scalar1=fs[f][:, 0:1], scalar2=None, op0=ALU.mult),
                          reads=[ra1, r_fs[f][0]], writes=[r_t1[f]])
                    S.add("dve", lambda e, f=f, a2=a2: e.scalar_tensor_tensor(out=at[f][:], in0=a2[:, 0:128], scalar=fs[f][:, 2:3], in1=t1[f][:], op0=ALU.mult, op1=ALU.add),
                          reads=[ra2, r_fs[f][2], r_t1[f]], writes=[r_at[f]])
                    S.add("dve", lambda e, f=f: e.memset(fs[f][:, 3:4], 0.0), writes=[r_fs[f][3]])
                    S.add("act", lambda e, f=f: e.activation(out=jk[f][:], in_=at[f][:], func=AF.Square, accum_out=fs[f][:, 3:4]),
                          reads=[r_at[f], r_fs[f][3]], writes=[r_jk[f], r_fs[f][3]])
                    S.add("act", lambda e, f=f: e.activation(out=fs[f][:, 4:5], in_=fs[f][:, 3:4], func=AF.Ln, bias=LN_EPS, scale=1.0 / 128.0),
                          reads=[r_fs[f][3]], writes=[r_fs[f][4]])
                    S.add("act", lambda e, f=f: e.activation(out=fs[f][:, 5:6], in_=fs[f][:, 4:5], func=AF.Exp, scale=-0.5),
                          reads=[r_fs[f][4]], writes=[r_fs[f][5]])
                    S.add("dve", lambda e, f=f, y=y: e.scalar_tensor_tensor(out=yb[y][:], in0=at[f][:], scalar=fs[f][:, 5:6], in1=sg08[:], op0=ALU.mult, op1=ALU.mult),
                          reads=[r_at[f], r_fs[f][5], r_sg08], writes=[r_yb[y]])

                def tail():
                    for tq in range(4):
                        y = ybs[tq]
                        S.add("pe", lambda e, y=y, tq=tq: e.transpose(out=bview(T0)[:, tq * 128:(tq + 1) * 128], in_=yb[y][:], identity=ident[:]),
                              reads=[r_yb[y], r_ident], writes=[r_bank[T0]])
                    cb_ = cstc[0] % 2
                    cstc[0] += 1
                    S.add("dve", lambda e, cb_=cb_: e.tensor_copy(out=cst[cb_][:], in_=bview(T0)[:, 0:512]), reads=[r_bank[T0]], writes=[r_cst[cb_]])
                    S.add("sp", lambda e, cb_=cb_: e.dma_start(out=catscr[h, :, g * 512:(g + 1) * 512], in_=cst[cb_][:]), reads=[r_cst[cb_]], dma=True)
                    if mod_todo:
                        cbm = mod_todo.pop(0)
                        mod_compute(cbm, MB2)
                        if len(mod_todo) >= 2:
                            mod_load(mod_todo[1], MB2)
                    conv_step(2)
                pending.append(tail)

            load_head(0)
            gcount = 0
            for h in range(8):
                hb2 = h % 2
                if h + 1 < 8:
                    load_head(h + 1)
                for g in range(4):
                    keys = []
                    for Gk in range(2 * g):
                        for ip in range(4):
                            keys.append((Gk * 4 + ip, 0, False, ip))
                    for Gk in (2 * g, 2 * g + 1):
                        for ip in range(4):
                            keys.append((Gk * 4 + ip, ip * 128, Gk == 2 * g + 1, ip))
                    prev = None
                    for n, (pos, c0, diag, ip) in enumerate(keys):
                        pb = ptc[0] % NPT
                        ptc[0] += 1
                        for c in range(2):
                            bk = A[2 * (n % 2) + c]
                            S.add("pe", lambda e, c=c, bk=bk, pos=pos, c0=c0, hb2=hb2, g=g: e.matmul(banks[bk][:, c0:512], lhsT=kT[hb2][c * 64:(c + 1) * 64, pos * 128:(pos + 1) * 128],
                                                                                                  rhs=qT[hb2][c * 64:(c + 1) * 64, g * 512 + c0:(g + 1) * 512], start=True, stop=True),
                                  reads=[r_kT[hb2], r_qT[hb2]], writes=[r_bank[bk]])
                            S.add("act", lambda e, c=c, bk=bk, pb=pb, c0=c0: e.activation(out=pt[pb][:, c, c0:512], in_=banks[bk][:, c0:512], func=AF.Exp, scale=0.125),
                                  reads=[r_bank[bk]], writes=[r_pt[pb][c]])
                        if diag:
                            S.add("pool" if os.environ.get("DEV_P2B", "") == "1" else "dve", lambda e, pb=pb, c0=c0: e.memset(pt[pb][64:128, :, c0:c0 + 64], 0.0), reads=r_pt[pb], writes=r_pt[pb])
                        if prev is not None:
                            rec_pv(*prev)
                        prev = (hb2, pb, pos, c0 // 128, n, diag, ip)
                        if n == 2 and pending:
                            pending.pop(0)()
                    rec_pv(*prev)
                    rec_finalize(h, g, gcount % 2)
                    gcount += 1
            while pending:
                pending.pop(0)()
            while mod_todo:
                cbm = mod_todo.pop(0)
                mod_compute(cbm, MB2)
                if len(mod_todo) >= 2:
                    mod_load(mod_todo[1], MB2)
            conv_step(1000)
            S.barrier()
            p3.close()

        if stop_after >= 4:
            p4 = es.enter_context(ExitStack())
            wo = sbuf(p4, "wo", [128, 16, 2048], BF16)
            cg = [sbuf(p4, f"cg{i}", [128, 16, 512], BF16) for i in range(2)]
            x3t = [sbuf(p4, f"x3_{i}", [128, 2048], F32) for i in range(2)]
            yt = [sbuf(p4, f"y3_{i}", [128, 2048], F32) for i in range(2)]
            mn = [sbuf(p4, "mn0", [128, 4, 2048], BF16)] * 2
            mT = [sbuf(p4, "mT0", [128, 16, 512], BF16)] * 2
            r_wo = RL(4)
            r_cg, r_x3t = RL(2), RL(2)
            r_yt = [RL(4), RL(4)]
            r_mn = [RL(4)] * 2
            r_mT = [RL(16)] * 2
            for cb in range(4):
                S.add("pool", lambda e, cb=cb: e.dma_start(out=wo[:, :, cb * 512:(cb + 1) * 512], in_=w_out[:, cb * 512:(cb + 1) * 512].rearrange("(kc p) n -> p kc n", p=128)),
                      writes=[r_wo[cb]], dma=True)
            S.add("sp", lambda e: e.dma_start(out=arena[0][:], in_=modscr[0:1, :].partition_broadcast(128)), writes=[r_arena[0]], dma=True)
            S.add("sp", lambda e: e.dma_start(out=arena[1][:], in_=ln1g[0:1, :].partition_broadcast(128)), writes=[r_arena[1]], dma=True)
            S.add("sp", lambda e: e.dma_start(out=arena[2][:], in_=ln1b[0:1, :].partition_broadcast(128)), writes=[r_arena[2]], dma=True)
            tcnt = [0]
            for og in range(4):
                cb2 = og % 2
                S.add("sp", lambda e, og=og, cb2=cb2: e.dma_start(out=cg[cb2][:], in_=catscr.rearrange("c p t -> p c t")[:, :, og * 512:(og + 1) * 512]), writes=[r_cg[cb2]], dma=True)
                for i in range(4):
                    ti = og * 4 + i
                    b = tcnt[0] % 2
                    tcnt[0] += 1
                    row0 = ((2 * og + 1) * 4 + i) * 128
                    S.add("sp", lambda e, b=b, row0=row0: e.dma_start(out=x3t[b][:], in_=xs[row0:row0 + 128, :]), writes=[r_x3t[b]], dma=True)
                    for cb in range(4):
                        bk = next_A()

                        def f_mm(e, i=i, cb=cb, bk=bk, cb2=cb2):
                            ins = None
                            for kc in range(16):
                                ins = e.matmul(banks[bk][:], lhsT=cg[cb2][:, kc, i * 128:(i + 1) * 128], rhs=wo[:, kc, cb * 512:(cb + 1) * 512], start=(kc == 0), stop=(kc == 15))
                            return ins
                        S.add("pe", f_mm, reads=[r_cg[cb2], r_wo[cb]], writes=[r_bank[bk]])
                        sl = slice(cb * 512, (cb + 1) * 512)
                        S.add("dve", lambda e, b=b, bk=bk, sl=sl: e.tensor_tensor(out=yt[b][:, sl], in0=banks[bk][:], in1=arena[0][:, sl], op=ALU.mult),
                              reads=[r_bank[bk], r_arena[0]], writes=[r_yt[b][cb]])
                        S.add("dve", lambda e, b=b, sl=sl: e.scalar_tensor_tensor(out=yt[b][:, sl], in0=x3t[b][:, sl], scalar=ALPHA, in1=yt[b][:, sl], op0=ALU.mult, op1=ALU.add),
                              reads=[r_x3t[b], r_yt[b][cb]], writes=[r_yt[b][cb]])
                    rstd, nb, rr = ln_stats(lambda a, c, b=b: yt[b][:, a:c], r_yt[b])
                    S.add("act", lambda e, b=b, rstd=rstd, nb=nb: e.activation(out=yt[b][:], in_=yt[b][:], func=AF.Identity, bias=nb, scale=rstd),
                          reads=r_yt[b] + rr, writes=r_yt[b])
                    S.add("dve", lambda e, b=b: e.tensor_tensor(out=yt[b][:], in0=yt[b][:], in1=arena[1][:], op=ALU.mult), reads=r_yt[b] + [r_arena[1]], writes=r_yt[b])
                    S.add("pool", lambda e, b=b: e.tensor_tensor(out=yt[b][:], in0=yt[b][:], in1=arena[2][:], op=ALU.add), reads=r_yt[b] + [r_arena[2]], writes=r_yt[b])
                    S.add("pool", lambda e, b=b, ti=ti: e.dma_start(out=h1scr[ti * 128:(ti + 1) * 128, :], in_=yt[b][:]), reads=r_yt[b], dma=True)
                    rstd2, nb2, rr2 = ln_stats(lambda a, c, b=b: yt[b][:, a:c], r_yt[b])
                    S.add("act", lambda e, b=b, i=i, cb2=cb2, rstd2=rstd2, nb2=nb2: e.activation(out=mn[cb2][:, i, :], in_=yt[b][:], func=AF.Identity, bias=nb2, scale=rstd2),
                          reads=r_yt[b] + rr2, writes=[r_mn[cb2][i]])
                transpose_group(mn[cb2], r_mn[cb2], mT[cb2], r_mT[cb2], 48, 32)
                S.add("pool", lambda e, og=og, cb2=cb2: e.dma_start(out=mscr[og, :, :], in_=mT[cb2][:].rearrange("p k t -> p (k t)")), reads=r_mT[cb2], dma=True)
            S.barrier()
            p4.close()

        if stop_after >= 5:
            p5 = es.enter_context(ExitStack())
            hT = sbuf(p5, "hT", [128, 64, 512], BF16)
            mg = sbuf(p5, "mg", [128, 16, 512], BF16)
            w1r = [sbuf(p5, f"w1r{i}", [128, 16, 256], BF16) for i in range(3)]
            w2r = [sbuf(p5, f"w2r{i}", [128, 8, 512], BF16) for i in range(3)]
            yf = [sbuf(p5, f"yf{i}", [128, 2048], F32) for i in range(4)]
            h1p = [sbuf(p5, f"h1p{i}", [128, 512], F32) for i in range(2)]
            rl = [sbuf(p5, f"rl{i}", [128, 512], F32) for i in range(2)]
            r_hT = RL(64)
            r_mg = Res()
            r_w1r, r_w2r = RL(3), RL(3)
            r_yf = [RL(4) for _ in range(4)]
            r_h1p, r_rl = RL(2), RL(2)
            S.add("sp", lambda e: e.dma_start(out=arena[0][:], in_=modscr[1:2, :].partition_broadcast(128)), writes=[r_arena[0]], dma=True)
            S.add("sp", lambda e: e.dma_start(out=arena[1][:], in_=ln2g[0:1, :].partition_broadcast(128)), writes=[r_arena[1]], dma=True)
            S.add("sp", lambda e: e.dma_start(out=arena[2][:], in_=ln2b[0:1, :].partition_broadcast(128)), writes=[r_arena[2]], dma=True)
            w1c, w2c, rlc, hpc = [0], [0], [0], [0]
            pend4 = []
            for og in range(4):
                S.add("sp", lambda e, og=og: e.dma_start(out=mg[:].rearrange("p k t -> p (k t)"), in_=mscr[og, :, :]), writes=[r_mg], dma=True)
                for hq in range(32):
                    wi = w1c[0] % 3
                    w1c[0] += 1
                    S.add("sp", lambda e, wi=wi, hq=hq: e.dma_start(out=w1r[wi][:].rearrange("p k n -> p (k n)"), in_=w1s[hq, :, :]),
                          writes=[r_w1r[wi]], dma=True)
                    for j in range(2):
                        hc = hq * 2 + j
                        bk = next_A()

                        def f_mm(e, j=j, bk=bk, wi=wi):
                            ins = None
                            for kc in range(16):
                                ins = e.matmul(banks[bk][:], lhsT=w1r[wi][:, kc, j * 128:(j + 1) * 128], rhs=mg[:, kc, :], start=(kc == 0), stop=(kc == 15))
                            return ins
                        S.add("pe", f_mm, reads=[r_w1r[wi], r_mg], writes=[r_bank[bk]])
                        ri = rlc[0] % 2
                        rlc[0] += 1
                        S.add("act", lambda e, bk=bk, ri=ri: e.activation(out=rl[ri][:], in_=banks[bk][:], func=AF.Relu), reads=[r_bank[bk]], writes=[r_rl[ri]])
                        eng2 = "dve" if hc % 2 == 0 else "pool"
                        S.add(eng2, lambda e, ri=ri, hc=hc: e.tensor_tensor(out=hT[:, hc, :], in0=rl[ri][:], in1=rl[ri][:], op=ALU.mult), reads=[r_rl[ri]], writes=[r_hT[hc]])
                    if hq % 8 == 5 and pend4:
                        pend4.pop(0)()
                for cb in range(4):
                    bset = [B0, B1, T0, T1] if cb % 2 == 0 else A
                    sl = slice(cb * 512, (cb + 1) * 512)
                    for hq8 in range(8):
                        wi = w2c[0] % 3
                        w2c[0] += 1
                        S.add("sp", lambda e, wi=wi, hq8=hq8, cb=cb: e.dma_start(out=w2r[wi][:].rearrange("p j n -> p (j n)"), in_=w2s[cb * 8 + hq8, :, :]),
                              writes=[r_w2r[wi]], dma=True)

                        def f_mm(e, wi=wi, hq8=hq8, bset=bset):
                            ins = None
                            for j in range(8):
                                hc = hq8 * 8 + j
                                for i in range(4):
                                    ins = e.matmul(banks[bset[i]][:], lhsT=hT[:, hc, i * 128:(i + 1) * 128], rhs=w2r[wi][:, j, :], start=(hc == 0), stop=(hc == 63))
                            return ins
                        S.add("pe", f_mm, reads=[r_w2r[wi]] + r_hT[hq8 * 8:(hq8 + 1) * 8], writes=[r_bank[b] for b in bset])
                    for i in range(4):
                        ti = og * 4 + i
                        hp = hpc[0] % 2
                        hpc[0] += 1
                        S.add("pool", lambda e, hp=hp, ti=ti, sl=sl: e.dma_start(out=h1p[hp][:], in_=h1scr[ti * 128:(ti + 1) * 128, sl]), writes=[r_h1p[hp]], dma=True)
                        S.add("dve", lambda e, i=i, sl=sl, bset=bset: e.tensor_tensor(out=yf[i][:, sl], in0=banks[bset[i]][:], in1=arena[0][:, sl], op=ALU.mult),
                              reads=[r_bank[bset[i]], r_arena[0]], writes=[r_yf[i][cb]])
                        S.add("dve", lambda e, i=i, sl=sl, hp=hp: e.scalar_tensor_tensor(out=yf[i][:, sl], in0=h1p[hp][:], scalar=ALPHA, in1=yf[i][:, sl], op0=ALU.mult, op1=ALU.add),
                              reads=[r_h1p[hp], r_yf[i][cb]], writes=[r_yf[i][cb]])
                for i in range(4):
                    ti = og * 4 + i

                    def tail4(i=i, ti=ti):
                        rstd, nb, rr = ln_stats(lambda a, c, i=i: yf[i][:, a:c], r_yf[i])
                        S.add("act", lambda e, i=i, rstd=rstd, nb=nb: e.activation(out=yf[i][:], in_=yf[i][:], func=AF.Identity, bias=nb, scale=rstd),
                              reads=r_yf[i] + rr, writes=r_yf[i])
                        S.add("dve", lambda e, i=i: e.tensor_tensor(out=yf[i][:], in0=yf[i][:], in1=arena[1][:], op=ALU.mult), reads=r_yf[i] + [r_arena[1]], writes=r_yf[i])
                        S.add("pool", lambda e, i=i: e.tensor_tensor(out=yf[i][:], in0=yf[i][:], in1=arena[2][:], op=ALU.add), reads=r_yf[i] + [r_arena[2]], writes=r_yf[i])
                        S.wait_at_end(S.add("pool", lambda e, i=i, ti=ti: e.dma_start(out=out[ti * 128:(ti + 1) * 128, :], in_=yf[i][:]), reads=r_yf[i], dma=True))
                    pend4.append(tail4)
            while pend4:
                pend4.pop(0)()
            S.barrier()
            p5.close()

        S.barrier()
        S.emit()
    return nc


def _core_layout(x, c, core):
    b, j = core // 2, core % 2
    xb = x[b]
    if j == 0:
        loc = np.concatenate([np.zeros((128, 2048), np.float32), xb[:3968]], axis=0)
    else:
        loc = xb
    loc = loc.reshape(32, 128, 2048)
    order = []
    valid = np.ones((128, 32), np.float32)
    for G in range(8):
        for i in range(4):
            L = 8 * (G // 2) + 2 * i + (G % 2)
            order.append(L)
            if j == 0 and L == 0:
                valid[:, G * 4 + i] = 0.0
    xs = np.ascontiguousarray(loc[order].reshape(4096, 2048))
    ccol = np.ascontiguousarray(c[b].reshape(16, 128).T)
    return xs, valid, ccol


_NC_CACHE = {}


def kernel(x, c, w_ada, b_ada, w_in, lambda_q1, lambda_k1, lambda_q2, lambda_k2, subln_g,
           gmlp_ln_g, gmlp_ln_b, gmlp_ws, gmlp_bs, w_out, ln1_g, ln1_b, w_ff1, w_ff2, ln2_g, ln2_b):
    f = lambda a: np.ascontiguousarray(np.asarray(a, dtype=np.float32))
    x = f(x)
    c = f(c)
    shared = {
        "w_ada": f(w_ada)[0], "b_ada": f(b_ada).reshape(1, 12288), "w_in": f(w_in)[0],
        "lambda_q1": f(lambda_q1).reshape(1, 64), "lambda_k1": f(lambda_k1).reshape(1, 64),
        "lambda_q2": f(lambda_q2).reshape(1, 64), "lambda_k2": f(lambda_k2).reshape(1, 64),
        "subln_g": f(subln_g).reshape(1, 128),
        "gmlp_ln_g": f(gmlp_ln_g).reshape(1, 1024), "gmlp_ln_b": f(gmlp_ln_b).reshape(1, 1024),
        "gmlp_ws": f(gmlp_ws)[0], "gmlp_bs": f(gmlp_bs).reshape(1, 1024),
        "w_out": f(w_out)[0], "ln1_g": f(ln1_g).reshape(1, 2048), "ln1_b": f(ln1_b).reshape(1, 2048),
        "w_ff1": f(w_ff1)[0], "w_ff2": f(w_ff2)[0], "ln2_g": f(ln2_g).reshape(1, 2048), "ln2_b": f(ln2_b).reshape(1, 2048),
    }
    in_maps = []
    for core in range(8):
        xs, valid, ccol = _core_layout(x, c, core)
        m = dict(shared)
        m.update({"xs": xs, "valid": valid, "ccol": ccol})
        in_maps.append(m)
    if "nc" not in _NC_CACHE:
        _NC_CACHE["nc"] = build_program()
    nc = _NC_CACHE["nc"]
    res = run_bass_kernel_spmd(nc, in_maps, core_ids=list(range(8)))
    outp = np.empty((4, 4096, 2048), np.float32)
    for core in range(8):
        b, j = core // 2, core % 2
        o = np.asarray(res.results[core]["out"]).reshape(16, 128, 2048)
        ov = outp[b].reshape(32, 128, 2048)
        ov[j::2] = o
    return outp
```
